# Optimizing a Trainium2 kernel written in Bass

```python
import math
import jax, jax.numpy as jnp
from jax import lax
import numpy as np

D_MODEL = 1024
BATCH = 8
SEQ = 4096
DEPTH = 2

GRID_W = 64
CTX_LEN = 256
CONV_DIM = D_MODEL // 2
CONV_K = 31
D_INNER = D_MODEL
SSM_HEAD_DIM = 64
N_SSM_HEADS = D_INNER // SSM_HEAD_DIM
SSM_GROUPS = 4
D_STATE = 128
SSM_CONV_K = 5
CHUNK = 128
NORM_GROUP = D_INNER // SSM_GROUPS
D_XBC = D_INNER + 2 * SSM_GROUPS * D_STATE
D_MIX = CONV_DIM + D_INNER
D_PROJ = 2 * CONV_DIM + D_INNER + D_XBC + N_SSM_HEADS
D_FF = 4 * D_MODEL
DEEPNORM_ALPHA = (2 * DEPTH) ** 0.25
DEEPNORM_BETA = (8 * DEPTH) ** -0.25
LN_EPS = 1e-5
RMS_EPS = 1e-5

kernel_name = "hybrid_conformer_ssd_diffusion_block"

F32 = jnp.float32


def layer_norm(x, g, b):
    xf = x.astype(F32)
    mu = jnp.mean(xf, axis=-1, keepdims=True)
    var = jnp.mean(jnp.square(xf - mu), axis=-1, keepdims=True)
    return ((xf - mu) * lax.rsqrt(var + LN_EPS)).astype(x.dtype) * g + b


def modulate(h, shift, scale):
    return h * (1 + scale) + shift


def dwconv_centred(u, w, b):
    pad = w.shape[0] // 2
    out = lax.conv_general_dilated(u, w[:, None, :].astype(u.dtype), window_strides=(1,),
                                   padding=[(pad, pad)], dimension_numbers=('NWC', 'WIO', 'NWC'),
                                   feature_group_count=u.shape[-1])
    return out + b


def conformer_conv(u, conv_w, conv_b, ln_g, ln_b, on_grid):
    a, g = jnp.split(u, 2, axis=-1)
    v = a * jax.nn.sigmoid(g)
    if on_grid:
        bsz, length, ch = v.shape
        rows = length // GRID_W
        v = dwconv_centred(v.reshape(bsz * rows, GRID_W, ch), conv_w, conv_b).reshape(bsz, length, ch)
    else:
        v = dwconv_centred(v, conv_w, conv_b)
    return jax.nn.silu(layer_norm(v, ln_g, ln_b))


def ssd_chunked(xs, dA, Bm, Cm, h0, with_output):
    b, l, H, P = xs.shape
    G, R, N = SSM_GROUPS, H // SSM_GROUPS, D_STATE
    nc = l // CHUNK
    xs = xs.reshape(b, nc, CHUNK, G, R, P)
    dA = dA.reshape(b, nc, CHUNK, G, R)
    Bm = Bm.reshape(b, nc, CHUNK, G, N)
    Cm = Cm.reshape(b, nc, CHUNK, G, N)
    cs = jnp.cumsum(dA, axis=2)
    decay_to_end = jnp.exp(cs[:, :, -1:] - cs)
    states = jnp.einsum('bclgn,bclgr,bclgrp->bcgrpn', Bm, decay_to_end, xs)
    chunk_decay = jnp.exp(cs[:, :, -1])

    def step(h, inp):
        st, dec = inp
        return h * dec[..., None, None] + st, h

    h_final, h_prev = lax.scan(step, h0, (jnp.moveaxis(states, 1, 0), jnp.moveaxis(chunk_decay, 1, 0)))
    if not with_output:
        return None, h_final
    h_prev = jnp.moveaxis(h_prev, 0, 1)
    cs_t = jnp.moveaxis(cs, 2, -1)
    causal = jnp.tril(jnp.ones((CHUNK, CHUNK), dtype=bool))
    decay_mat = jnp.exp(jnp.where(causal, cs_t[..., :, None] - cs_t[..., None, :], -jnp.inf))
    cb = jnp.einsum('bclgn,bcsgn->bcgls', Cm, Bm)
    y_diag = jnp.einsum('bcgls,bcgrls,bcsgrp->bclgrp', cb, decay_mat, xs)
    y_off = jnp.einsum('bclgn,bcgrpn,bclgr->bclgrp', Cm, h_prev, jnp.exp(cs))
    return (y_diag + y_off).reshape(b, l, H, P), h_final


def ssd_direction(xs, Bm, Cm, dt_raw, dt_bias_d, a_log_d, d_d, h0, with_output):
    dt = jax.nn.softplus(dt_raw + dt_bias_d.astype(F32))
    dA = dt * (-jnp.exp(a_log_d.astype(F32)))
    y, h = ssd_chunked(xs * dt[..., None], dA, Bm, Cm, h0, with_output)
    if with_output:
        y = y + d_d.astype(F32)[:, None] * xs
    return y, h


def ssd_bidirectional(xbc_l, dt_l, xbc_c, dt_c, ssm_conv_w, ssm_conv_b, dt_bias, a_log, d_skip, ctx_out):
    def prep(xbc, dt_raw):
        v = jax.nn.silu(dwconv_centred(xbc, ssm_conv_w, ssm_conv_b)).astype(F32)
        xs, Bm, Cm = jnp.split(v, [D_INNER, D_INNER + SSM_GROUPS * D_STATE], axis=-1)
        b, l = xs.shape[:2]
        return (xs.reshape(b, l, N_SSM_HEADS, SSM_HEAD_DIM), Bm.reshape(b, l, SSM_GROUPS, D_STATE),
                Cm.reshape(b, l, SSM_GROUPS, D_STATE), dt_raw.astype(F32))

    lat = prep(xbc_l, dt_l)
    ctx = prep(xbc_c, dt_c)
    b = xbc_c.shape[0]
    h_zero = jnp.zeros((b, SSM_GROUPS, N_SSM_HEADS // SSM_GROUPS, SSM_HEAD_DIM, D_STATE), F32)
    ys_l, ys_c = [], []
    for d in range(2):
        flip = (lambda a: jnp.flip(a, axis=1)) if d == 1 else (lambda a: a)
        y_cd, h_c = ssd_direction(*[flip(a) for a in ctx], dt_bias[d], a_log[d], d_skip[d], h_zero, ctx_out)
        y_ld, _ = ssd_direction(*[flip(a) for a in lat], dt_bias[d], a_log[d], d_skip[d], h_c, True)
        ys_l.append(flip(y_ld))
        if ctx_out:
            ys_c.append(flip(y_cd))
    bl, ll = xbc_l.shape[:2]
    y_lat = (ys_l[0] + ys_l[1]).reshape(bl, ll, D_INNER)
    y_ctx = (ys_c[0] + ys_c[1]).reshape(b, xbc_c.shape[1], D_INNER) if ctx_out else None
    return y_lat, y_ctx


def gated_rmsnorm(y, z, w):
    g = y * jax.nn.silu(z.astype(F32))
    b, l, _ = g.shape
    g = g.reshape(b, l, SSM_GROUPS, NORM_GROUP)
    g = g * lax.rsqrt(jnp.mean(jnp.square(g), axis=-1, keepdims=True) + RMS_EPS)
    return g.reshape(b, l, D_INNER).astype(z.dtype) * w


def mixer(h_lat, h_ctx, w_in, conv_w, conv_b, conv_ln_g, conv_ln_b, ssm_conv_w, ssm_conv_b,
          dt_bias, a_log, d_skip, ssm_norm_w, w_out, ctx_out):
    cuts = [2 * CONV_DIM, 2 * CONV_DIM + D_INNER, 2 * CONV_DIM + D_INNER + D_XBC]
    u_l, z_l, xbc_l, dt_l = jnp.split(h_lat @ w_in, cuts, axis=-1)
    u_c, z_c, xbc_c, dt_c = jnp.split(h_ctx @ w_in, cuts, axis=-1)
    conv_l = conformer_conv(u_l, conv_w, conv_b, conv_ln_g, conv_ln_b, True)
    y_l, y_c = ssd_bidirectional(xbc_l, dt_l, xbc_c, dt_c, ssm_conv_w, ssm_conv_b, dt_bias, a_log, d_skip, ctx_out)
    out_l = jnp.concatenate([conv_l, gated_rmsnorm(y_l, z_l, ssm_norm_w)], axis=-1) @ w_out
    if not ctx_out:
        return out_l, None
    conv_c = conformer_conv(u_c, conv_w, conv_b, conv_ln_g, conv_ln_b, False)
    out_c = jnp.concatenate([conv_c, gated_rmsnorm(y_c, z_c, ssm_norm_w)], axis=-1) @ w_out
    return out_l, out_c


def sq_relu_mlp(h, w1, w2):
    return jnp.square(jax.nn.relu(h @ w1)) @ w2


def setup_inputs(seed: int = 0) -> dict:
    key = jax.random.key(seed)
    ks = jax.random.split(key, 26)

    def nrm(k, shape, s):
        return s * jax.random.normal(k, shape, F32)

    H = N_SSM_HEADS
    dt0 = jnp.exp(jax.random.uniform(ks[13], (DEPTH, 2, H), F32, math.log(1e-3), math.log(1e-1)))
    return {
        'x': nrm(ks[0], (BATCH, SEQ, D_MODEL), 1.0),
        'c': nrm(ks[1], (BATCH, D_MODEL), 1.0),
        'ctx': nrm(ks[2], (BATCH, CTX_LEN, D_MODEL), 1.0),
        'c_ctx': nrm(ks[3], (D_MODEL,), 1.0),
        'w_mod': nrm(ks[4], (DEPTH, D_MODEL, 6 * D_MODEL), D_MODEL ** -0.5),
        'b_mod': nrm(ks[5], (DEPTH, 6 * D_MODEL), 0.01),
        'w_in': nrm(ks[6], (DEPTH, D_MODEL, D_PROJ), D_MODEL ** -0.5),
        'conv_w': nrm(ks[7], (DEPTH, CONV_K, CONV_DIM), CONV_K ** -0.5),
        'conv_b': nrm(ks[8], (DEPTH, CONV_DIM), 0.01),
        'conv_ln_g': 1.0 + nrm(ks[9], (DEPTH, CONV_DIM), 0.01),
        'conv_ln_b': nrm(ks[10], (DEPTH, CONV_DIM), 0.01),
        'ssm_conv_w': nrm(ks[11], (DEPTH, SSM_CONV_K, D_XBC), SSM_CONV_K ** -0.5),
        'ssm_conv_b': nrm(ks[12], (DEPTH, D_XBC), 0.01),
        'dt_bias': dt0 + jnp.log(-jnp.expm1(-dt0)),
        'a_log': jnp.log(jax.random.uniform(ks[14], (DEPTH, 2, H), F32, 1.0, 16.0)),
        'd_skip': 1.0 + nrm(ks[15], (DEPTH, 2, H), 0.01),
        'ssm_norm_w': 1.0 + nrm(ks[16], (DEPTH, D_INNER), 0.01),
        'w_out': nrm(ks[17], (DEPTH, D_MIX, D_MODEL), DEEPNORM_BETA * D_MIX ** -0.5),
        'ln1_g': 1.0 + nrm(ks[18], (DEPTH, D_MODEL), 0.01),
        'ln1_b': nrm(ks[19], (DEPTH, D_MODEL), 0.01),
        'w1': nrm(ks[20], (DEPTH, D_MODEL, D_FF), D_MODEL ** -0.5),
        'w2': nrm(ks[21], (DEPTH, D_FF, D_MODEL), DEEPNORM_BETA * D_FF ** -0.5),
        'ln2_g': 1.0 + nrm(ks[22], (DEPTH, D_MODEL), 0.01),
        'ln2_b': nrm(ks[23], (DEPTH, D_MODEL), 0.01),
    }


def reference(x, c, ctx, c_ctx, w_mod, b_mod, w_in, conv_w, conv_b, conv_ln_g, conv_ln_b,
              ssm_conv_w, ssm_conv_b, dt_bias, a_log, d_skip, ssm_norm_w, w_out,
              ln1_g, ln1_b, w1, w2, ln2_g, ln2_b):
    x_l, x_c = x, ctx
    silu_c = jax.nn.silu(c)
    silu_cc = jax.nn.silu(c_ctx)
    for i in range(DEPTH):
        last = i == DEPTH - 1
        sh1_l, sc1_l, g1_l, sh2_l, sc2_l, g2_l = jnp.split((silu_c @ w_mod[i] + b_mod[i])[:, None, :], 6, axis=-1)
        sh1_c, sc1_c, g1_c, sh2_c, sc2_c, g2_c = jnp.split((silu_cc @ w_mod[i] + b_mod[i])[None, None, :], 6, axis=-1)
        mix_l, mix_c = mixer(modulate(x_l, sh1_l, sc1_l), modulate(x_c, sh1_c, sc1_c), w_in[i],
                             conv_w[i], conv_b[i], conv_ln_g[i], conv_ln_b[i], ssm_conv_w[i], ssm_conv_b[i],
                             dt_bias[i], a_log[i], d_skip[i], ssm_norm_w[i], w_out[i], not last)
        x_l = layer_norm(DEEPNORM_ALPHA * x_l + g1_l * mix_l, ln1_g[i], ln1_b[i])
        x_l = layer_norm(DEEPNORM_ALPHA * x_l + g2_l * sq_relu_mlp(modulate(x_l, sh2_l, sc2_l), w1[i], w2[i]),
                         ln2_g[i], ln2_b[i])
        if not last:
            x_c = layer_norm(DEEPNORM_ALPHA * x_c + g1_c * mix_c, ln1_g[i], ln1_b[i])
            x_c = layer_norm(DEEPNORM_ALPHA * x_c + g2_c * sq_relu_mlp(modulate(x_c, sh2_c, sc2_c), w1[i], w2[i]),
                             ln2_g[i], ln2_b[i])
    return x_l
```

```python
import contextlib
import numpy as np
import concourse.bass as bass
import concourse.mybir as mybir
from concourse.bass_utils import run_bass_kernel_spmd

F32 = mybir.dt.float32
BF16 = mybir.dt.bfloat16
AF = mybir.ActivationFunctionType
ALU = mybir.AluOpType

D = 1024
DEPTH = 2
GRID_W = 64
CONV_DIM = 512
CONV_K = 31
D_INNER = 1024
HEADS = 16
HP = 64
GROUPS = 4
NST = 128
SK = 5
D_XBC = 2048
D_PROJ = 4112
D_FF = 4096
ALPHA = float((2 * DEPTH) ** 0.25)
LN_EPS = 1e-5
RMS_EPS = 1e-5
NEG = -30000.0

ENGS = ("pe", "act", "dve", "pool", "sp")
import os
MENG = os.environ.get("K_MENG", "pool")
NDIR = int(os.environ.get("K_NDIR", "2"))
NOC2 = int(os.environ.get("K_NOC2", "0"))


class Buf:
    __slots__ = ("name", "w", "r", "sem", "excl")

    def __init__(self, name):
        self.name = name
        self.w = None
        self.r = []
        self.sem = None
        self.excl = False


class Op:
    __slots__ = ("eng", "fn", "deps", "is_dma", "sem", "semval", "sig", "sigval", "idx", "eidx", "waits")

    def __init__(self, eng, fn):
        self.eng = eng
        self.fn = fn
        self.deps = []
        self.is_dma = False
        self.sem = None
        self.semval = 0
        self.sig = False
        self.sigval = 0
        self.idx = 0
        self.eidx = 0
        self.waits = []


class Prog:
    def __init__(self, nc, n_hw_sems=60, n_sw_sems=28, plan=None):
        self.nc = nc
        self.plan = plan
        self.ops = []
        self.eng_ops = {e: [] for e in ENGS}
        n_dma_sems = n_hw_sems + n_sw_sems
        self.n_dma_sems = n_dma_sems
        self.dma_cnt = [0] * n_dma_sems
        self.dma_free = list(range(n_hw_sems))
        self.sw_free = list(range(n_hw_sems, n_dma_sems))
        self.sw_map = {}
        self.all_bufs = []
        self.n = 0
        if plan is not None:
            self.engobj = {"pe": nc.tensor, "act": nc.scalar, "dve": nc.vector, "pool": nc.gpsimd, "sp": nc.sync}
            self.stack = contextlib.ExitStack()
            self.esem = {e: self.stack.enter_context(nc.semaphore("s_" + e)) for e in ENGS}
            self.dsem = [self.stack.enter_context(nc.semaphore("d%d" % i)) for i in range(n_dma_sems)]

    def buf(self, name, dma=False, sw=False):
        b = Buf(name)
        if sw:
            if name not in self.sw_map:
                assert self.sw_free, "out of sw dma semaphores"
                self.sw_map[name] = self.sw_free.pop(0)
            b.sem = self.sw_map[name]
        elif dma:
            assert self.dma_free, "out of dma semaphores"
            b.sem = self.dma_free.pop(0)
        if self.plan is None:
            self.all_bufs.append(b)
        return b

    def release(self, bufs):
        for b in bufs:
            if b.sem is not None:
                if b.sem not in self.sw_map.values():
                    self.dma_free.append(b.sem)
                b.sem = None

    def _emit(self, eng_name, fn, is_dma):
        rec = self.plan.ops[self.n]
        self.n += 1
        assert rec.eng == eng_name and rec.is_dma == is_dma, (rec.eng, eng_name)
        eng = self.engobj[eng_name]
        for kind, d in rec.waits:
            if kind == "dma":
                eng.wait_ge(self.dsem[d.sem], d.semval)
            else:
                eng.wait_ge(self.esem[d.eng], d.sigval)
        if fn is None:
            if rec.sig:
                eng.nop().then_inc(self.esem[eng_name], 1)
            return
        ins = fn(eng)
        if is_dma:
            ins.then_inc(self.dsem[rec.sem], 16)
        elif rec.sig:
            ins.then_inc(self.esem[eng_name], 1)

    def _add(self, op, reads, writes):
        deps = []
        for b in reads:
            if b.w is not None:
                deps.append(b.w)
            if b.excl:
                deps.extend(r for r in b.r if r.eng != op.eng)
        for b in writes:
            if b.w is not None:
                deps.append(b.w)
            deps.extend(b.r)
        for b in reads:
            b.r.append(op)
        for b in writes:
            b.w = op
            b.r = []
        seen = set()
        for d in deps:
            if d is op or id(d) in seen:
                continue
            seen.add(id(d))
            op.deps.append(d)
        op.idx = len(self.ops)
        op.eidx = len(self.eng_ops[op.eng])
        self.ops.append(op)
        self.eng_ops[op.eng].append(op)
        return op

    def op(self, eng, fn, reads=(), writes=()):
        if self.plan is not None:
            return self._emit(eng, fn, False)
        return self._add(Op(eng, None), reads, writes)

    def dma(self, eng, fn, sbuf, reads=(), writes=()):
        if self.plan is not None:
            return self._emit(eng, fn, True)
        o = Op(eng, None)
        o.is_dma = True
        assert sbuf.sem is not None, sbuf.name
        o.sem = sbuf.sem
        self.dma_cnt[o.sem] += 16
        o.semval = self.dma_cnt[o.sem]
        return self._add(o, reads, writes)

    def barrier(self):
        if self.plan is not None:
            for e in ENGS:
                self._emit(e, None, False)
            return
        last = [ops[-1] for ops in self.eng_ops.values() if ops]
        latest = {}
        for o in self.ops:
            if o.is_dma:
                latest[o.sem] = o
        deps = last + list(latest.values())
        for e in ENGS:
            o = Op(e, None)
            o.deps = list(deps)
            o.idx = len(self.ops)
            o.eidx = len(self.eng_ops[e])
            self.ops.append(o)
            self.eng_ops[e].append(o)
        for b in self.all_bufs:
            b.w = None
            b.r = []

    def analyze(self):
        seen_eng = {e: {f: 0 for f in ENGS} for e in ENGS}
        seen_dma = {e: [0] * self.n_dma_sems for e in ENGS}
        for o in self.ops:
            e = o.eng
            for d in o.deps:
                if d.is_dma:
                    if seen_dma[e][d.sem] >= d.semval:
                        continue
                    seen_dma[e][d.sem] = d.semval
                    o.waits.append(("dma", d))
                else:
                    if d.eng == e and e != "pool" and (o.eidx - d.eidx > 2 or e == "pe" or e == "sp"):
                        continue
                    if seen_eng[e][d.eng] >= d.eidx + 1:
                        continue
                    seen_eng[e][d.eng] = d.eidx + 1
                    d.sig = True
                    o.waits.append(("eng", d))
            o.deps = None
        cnt = {e: 0 for e in ENGS}
        for o in self.ops:
            if o.sig:
                cnt[o.eng] += 1
                o.sigval = cnt[o.eng]
        self.sig_counts = cnt

    def finish(self):
        assert self.n == len(self.plan.ops), (self.n, len(self.plan.ops))
        for e in ENGS:
            eng = self.engobj[e]
            for i in range(self.n_dma_sems):
                if self.plan.dma_cnt[i] > 0:
                    eng.wait_ge(self.dsem[i], self.plan.dma_cnt[i])
        self.stack.close()


class Tl:
    __slots__ = ("ap", "b")

    def __init__(self, ap, b):
        self.ap = ap
        self.b = b


PFM_OFF = {}
_o = 0
for _n, _w in (("bmod", 48), ("convb", 4), ("lng", 4), ("lnb", 4), ("sconvb", 16), ("normw", 8),
               ("ln1g", 8), ("ln1b", 8), ("ln2g", 8), ("ln2b", 8), ("convw", 4 * CONV_K), ("sconvw", 16 * SK)):
    PFM_OFF[_n] = (_o, _w)
    _o += _w
NPF = _o
PROW_OFF = {}
_o = 0
for _n, _w in (("bmod_g1", 1024), ("bmod_g2", 1024), ("convb", 512), ("sconvb", 1536), ("ln1g", 1024), ("ln1b", 1024),
               ("ln2g", 1024), ("ln2b", 1024), ("dtb", 32), ("alog", 32), ("dskip", 32)):
    PROW_OFF[_n] = (_o, _w)
    _o += _w
NPR = _o


def _fm(v):
    return np.ascontiguousarray(v.reshape(-1, 128).T)


def pack_params(inp, L):
    pf = np.zeros((128, NPF), np.float32)

    def put(name, arr):
        o, w = PFM_OFF[name]
        assert arr.shape == (128, w), (name, arr.shape)
        pf[:, o:o + w] = arr
    put("bmod", _fm(inp["b_mod"][L]))
    put("convb", _fm(inp["conv_b"][L]))
    put("lng", _fm(inp["conv_ln_g"][L]))
    put("lnb", _fm(inp["conv_ln_b"][L]))
    put("sconvb", _fm(inp["ssm_conv_b"][L]))
    put("normw", _fm(inp["ssm_norm_w"][L]))
    put("ln1g", _fm(inp["ln1_g"][L]))
    put("ln1b", _fm(inp["ln1_b"][L]))
    put("ln2g", _fm(inp["ln2_g"][L]))
    put("ln2b", _fm(inp["ln2_b"][L]))
    cw = inp["conv_w"][L]
    put("convw", np.ascontiguousarray(cw.reshape(CONV_K, 4, 128).transpose(2, 1, 0)).reshape(128, 4 * CONV_K))
    sw = inp["ssm_conv_w"][L]
    put("sconvw", np.ascontiguousarray(sw.reshape(SK, 16, 128).transpose(2, 1, 0)).reshape(128, 16 * SK))
    pr = np.zeros((1, NPR), np.float32)

    def putr(name, arr):
        o, w = PROW_OFF[name]
        pr[0, o:o + w] = arr.reshape(-1)
    putr("bmod_g1", inp["b_mod"][L][2048:3072])
    putr("bmod_g2", inp["b_mod"][L][5120:6144])
    putr("convb", inp["conv_b"][L])
    putr("sconvb", inp["ssm_conv_b"][L][:1536])
    putr("ln1g", inp["ln1_g"][L])
    putr("ln1b", inp["ln1_b"][L])
    putr("ln2g", inp["ln2_g"][L])
    putr("ln2b", inp["ln2_b"][L])
    putr("dtb", inp["dt_bias"][L])
    putr("alog", inp["a_log"][L])
    putr("dskip", inp["d_skip"][L])
    return pf, pr


def build(SEQ, CTX, depth=DEPTH, dbg=False):
    nc = bass.Bass("TRN2", target_bir_lowering=False)
    TALL = CTX + SEQ
    NCC = CTX // 128
    NLC = SEQ // 128
    NCH = NCC + NLC
    TB = 256
    assert CTX % TB == 0 and SEQ % TB == 0

    def dram_in(name, shape, dt=F32):
        return nc.dram_tensor(name, list(shape), dt, kind="ExternalInput").ap()

    def dram_tmp(name, shape, dt=F32):
        if dbg:
            return nc.dram_tensor(name, list(shape), dt, kind="ExternalOutput").ap()
        return nc.dram_tensor(name, list(shape), dt).ap()

    x_in = dram_in("x", [SEQ, D])
    ctx_in = dram_in("ctx", [CTX, D])
    c_fm = dram_in("c_fm", [128, 16])
    W = {}
    for nm, shp in (("w_mod", [depth, D, 6 * D]), ("w_in", [depth, D, D_PROJ]), ("w_out", [depth, 1536, D]),
                    ("w1", [depth, D, D_FF]), ("w2", [depth, D_FF, D])):
        W[nm] = dram_in(nm, shp)
    pfm_in = dram_in("pfm", [depth, 128, NPF])
    prow_in = dram_in("prow", [depth, 1, NPR])
    y_out = nc.dram_tensor("y", [SEQ, D], F32, kind="ExternalOutput").ap()

    sA = dram_tmp("sA", [TALL, D])
    sB = dram_tmp("sB", [TALL, D])
    TP = TALL + 8
    xbcT = dram_tmp("xbcT", [D_XBC, TP], BF16)
    convT_d = dram_tmp("convT", [CONV_DIM, TALL], BF16)
    siluz_d = dram_tmp("siluz", [TALL, D_INNER])
    xs_d = dram_tmp("xs_tok", [TALL, D_INNER], BF16)
    bt_d = dram_tmp("b_tok", [TALL, 512], BF16)
    BT_d = dram_tmp("BT", [512, TALL], BF16)
    CT_d = dram_tmp("CT", [512, TALL], BF16)
    yf_d = dram_tmp("y_f", [TALL, D_INNER])
    g_d = dram_tmp("grow", [2, 2, D])
    dbg_y = dram_tmp("dbg_y", [TALL, D_INNER]) if dbg else None
    dbg_r = dram_tmp("dbg_r", [TALL, D]) if dbg else None

    def seg_of_chunk(ch):
        return 0 if ch < NCC else 1

    def xbc_col(tok):
        return tok + 2 if tok < CTX else tok + 6

    ARENA_W = 53000
    arena = nc.alloc_sbuf_tensor("arena", [128, ARENA_W], F32)
    psum_t = [nc.alloc_psum_tensor("ps%d" % i, [128, 512], F32) for i in range(8)]
    plan = Prog(nc)
    program(nc, plan, locals())
    plan.analyze()
    em = Prog(nc, plan=plan)
    program(nc, em, locals())
    em.finish()
    return nc, plan


def program(nc, P, env):
    SEQ, CTX, depth, TALL, NCC, NLC, NCH, TB, TP, ARENA_W = (env[k] for k in (
        "SEQ", "CTX", "depth", "TALL", "NCC", "NLC", "NCH", "TB", "TP", "ARENA_W"))
    x_in, ctx_in, c_fm, W, pfm_in, prow_in, y_out = (env[k] for k in (
        "x_in", "ctx_in", "c_fm", "W", "pfm_in", "prow_in", "y_out"))
    sA, sB, xbcT, convT_d, siluz_d, xs_d, bt_d, BT_d, CT_d, yf_d, g_d = (env[k] for k in (
        "sA", "sB", "xbcT", "convT_d", "siluz_d", "xs_d", "bt_d", "BT_d", "CT_d", "yf_d", "g_d"))
    arena, psum_t, seg_of_chunk, xbc_col = env["arena"], env["psum_t"], env["seg_of_chunk"], env["xbc_col"]
    dbg, dbg_y, dbg_r = env["dbg"], env["dbg_y"], env["dbg_r"]
    aoff = [0]

    def alloc(shape, dt=F32, name="t", dma=False, parts=128):
        n = int(np.prod(shape))
        nw = n if dt == F32 else (n + 1) // 2
        nw = (nw + 7) // 8 * 8
        assert aoff[0] + nw <= ARENA_W, ("arena overflow", name, aoff[0], nw)
        v = arena[0:parts, aoff[0]:aoff[0] + nw]
        aoff[0] += nw
        if dt != F32:
            v = v.bitcast(dt)
        v = v[:, 0:n]
        if len(shape) == 2:
            v = v.rearrange("p (a b) -> p a b", a=shape[0])
        elif len(shape) == 3:
            v = v.rearrange("p (a b c) -> p a b c", a=shape[0], b=shape[1])
        return Tl(v, P.buf(name, dma=dma))

    psum = [Tl(psum_t[i][:, :], P.buf("ps%d" % i)) for i in range(8)]
    for t in psum:
        t.b.excl = True

    def psb(i):
        return psum[i].ap.bitcast(BF16)

    ident = alloc([128], F32, "ident")
    identb = alloc([128], BF16, "identb")
    triF = alloc([128], F32, "triF")
    triB = alloc([128], F32, "triB")
    ntriF = alloc([128], F32, "ntriF")
    ntriB = alloc([128], F32, "ntriB")
    maskF = alloc([4, 128], BF16, "maskF")
    maskB = alloc([4, 128], BF16, "maskB")
    onesb = alloc([128], BF16, "onesb")
    epsln = alloc([1], F32, "epsln")
    epsrms = alloc([1], F32, "epsrms")
    one_t = alloc([1], F32, "one_t")
    dt_sb = alloc([NCH, 2, HEADS], F32, "dt_sb")
    C0 = [ident, identb, triF, triB, ntriF, ntriB, maskF, maskB, onesb, epsln, epsrms, one_t]

    def pool_fill(t, ap, val):
        P.op("pool", lambda e, ap=ap, val=val: e.memset(ap, val), writes=[t.b])

    def pool_sel(t, ap, pattern, cmp, fill, base, cm):
        P.op("pool", lambda e: e.affine_select(out=ap, in_=ap, pattern=pattern, compare_op=cmp, fill=fill,
                                               base=base, channel_multiplier=cm), reads=[t.b], writes=[t.b])

    pool_fill(ident, ident.ap, 0.0)
    pool_sel(ident, ident.ap, [[-1, 128]], ALU.not_equal, 1.0, 0, 1)
    P.op("pool", lambda e: e.tensor_copy(out=identb.ap, in_=ident.ap), reads=[ident.b], writes=[identb.b])
    pool_fill(triF, triF.ap, 1.0)
    pool_sel(triF, triF.ap, [[1, 128]], ALU.is_ge, 0.0, 0, -1)
    pool_fill(triB, triB.ap, 1.0)
    pool_sel(triB, triB.ap, [[-1, 128]], ALU.is_ge, 0.0, 0, 1)
    P.op("pool", lambda e: e.tensor_scalar(out=ntriF.ap, in0=triF.ap, scalar1=-1.0, scalar2=None, op0=ALU.mult),
         reads=[triF.b], writes=[ntriF.b])
    P.op("pool", lambda e: e.tensor_scalar(out=ntriB.ap, in0=triB.ap, scalar1=-1.0, scalar2=None, op0=ALU.mult),
         reads=[triB.b], writes=[ntriB.b])
    pool_fill(maskF, maskF.ap, 0.0)
    pool_sel(maskF, maskF.ap, [[0, 4], [1, 128]], ALU.is_ge, NEG, 0, -1)
    pool_fill(maskB, maskB.ap, 0.0)
    pool_sel(maskB, maskB.ap, [[0, 4], [-1, 128]], ALU.is_ge, NEG, 0, 1)
    pool_fill(onesb, onesb.ap, 1.0)
    pool_fill(epsln, epsln.ap, LN_EPS)
    pool_fill(epsrms, epsrms.ap, RMS_EPS)
    pool_fill(one_t, one_t.ap, 1.0)

    zt = alloc([16, 8], BF16, "zt", dma=True)
    pool_fill(zt, zt.ap, 0.0)
    xb3 = xbcT.rearrange("(c p) t -> p c t", p=128)
    for c0 in (0, CTX + 2, CTX + 4, TP - 2):
        P.dma("sp", lambda e, c0=c0: e.dma_start(out=xb3[:, :, c0:c0 + 2], in_=zt.ap[:, :, 0:2]), zt.b, reads=[zt.b])
    persist_mark = aoff[0]

    def rstd_from(var_ap, out_t, eps_t, reads, n=1, scale=1.0):
        P.op("act", lambda e: e.activation(out=out_t.ap, in_=var_ap, func=AF.Ln, bias=eps_t.ap[:, 0:1], scale=scale),
             reads=reads + [eps_t.b], writes=[out_t.b])
        P.op("act", lambda e: e.activation(out=out_t.ap, in_=out_t.ap, func=AF.Exp, scale=-0.5), reads=[out_t.b],
             writes=[out_t.b])

    for L in range(depth):
        last = (L == depth - 1)
        src = None if L == 0 else sB

        def stream_rows(tok0, n, src=src):
            if src is None:
                return ctx_in[tok0:tok0 + n, :] if tok0 < CTX else x_in[tok0 - CTX:tok0 - CTX + n, :]
            return src[tok0:tok0 + n, :]

        P.barrier()
        aoff[0] = persist_mark
        pf = alloc([NPF], F32, "pf", dma=True)
        P.dma("sp", lambda e, L=L: e.dma_start(out=pf.ap, in_=pfm_in[L]), pf.b, writes=[pf.b])
        AB = alloc([2, 8, 4], F32, "AB")
        hrow = alloc([3, 2, HEADS], F32, "hrow", dma=True)
        negA = alloc([2, HEADS], F32, "negA")
        layer_mark = aoff[0]

        def pfv(name):
            o, w = PFM_OFF[name]
            return pf.ap[:, o:o + w]
        prow2 = alloc([NPR], F32, "prow2", dma=True, parts=2)
        P.dma("sp", lambda e, L=L: e.dma_start(out=prow2.ap, in_=prow_in[L].partition_broadcast(2)), prow2.b,
              writes=[prow2.b])
        csil = alloc([8, 2], F32, "csil", dma=True)
        P.dma("sp", lambda e: e.dma_start(out=csil.ap, in_=c_fm.rearrange("p (k s) -> p k s", s=2)), csil.b,
              writes=[csil.b])
        P.op("act", lambda e: e.activation(out=csil.ap, in_=csil.ap, func=AF.Silu), reads=[csil.b], writes=[csil.b])
        modfm = alloc([48, 2], F32, "modfm")
        grow = alloc([2, D], F32, "grow", dma=True, parts=2)
        wm = [alloc([8, 1024], F32, "wm%d" % i, dma=True) for i in range(2)]
        wmod3 = W["w_mod"][L].rearrange("(k p) n -> p k n", p=128)
        for cb in range(6):
            t = wm[cb % 2]
            P.dma("sp", lambda e, t=t, cb=cb: e.dma_start(out=t.ap, in_=wmod3[:, :, cb * 1024:(cb + 1) * 1024]),
                  t.b, writes=[t.b])
            if cb in (2, 5):
                which = 0 if cb == 2 else 1
                for half in range(2):
                    pb = psum[half]
                    for k in range(8):
                        P.op("pe", lambda e, t=t, k=k, half=half, pb=pb: e.matmul(
                            pb.ap[0:2, :], lhsT=csil.ap[:, k, :], rhs=t.ap[:, k, half * 512:(half + 1) * 512],
                            start=(k == 0), stop=(k == 7)), reads=[t.b, csil.b], writes=[pb.b])
                    o, _w = PROW_OFF["bmod_g1" if which == 0 else "bmod_g2"]
                    P.op("dve", lambda e, pb=pb, half=half, which=which, o=o: e.tensor_tensor(
                        out=grow.ap[:, which, half * 512:(half + 1) * 512], in0=pb.ap[0:2, :],
                        in1=prow2.ap[:, o + half * 512:o + (half + 1) * 512], op=ALU.add),
                        reads=[pb.b, prow2.b], writes=[grow.b])
            else:
                pb = psum[2 + (cb % 2)]
                for j in range(8):
                    for k in range(8):
                        P.op("pe", lambda e, t=t, k=k, j=j, pb=pb: e.matmul(
                            pb.ap[:, j * 2:j * 2 + 2], lhsT=t.ap[:, k, j * 128:(j + 1) * 128], rhs=csil.ap[:, k, :],
                            start=(k == 0), stop=(k == 7)), reads=[t.b, csil.b], writes=[pb.b])
                bo = PFM_OFF["bmod"][0] + cb * 8
                P.op("dve", lambda e, pb=pb, cb=cb, bo=bo: e.tensor_tensor(
                    out=modfm.ap[:, cb * 8:(cb + 1) * 8, :],
                    in0=pb.ap[:, 0:16].rearrange("p (j s) -> p j s", s=2),
                    in1=pf.ap[:, bo:bo + 8].unsqueeze(2).broadcast_to([128, 8, 2]), op=ALU.add),
                    reads=[pb.b, pf.b], writes=[modfm.b])
        P.dma("sp", lambda e: e.dma_start(out=g_d.rearrange("s w d -> s (w d)"),
                                          in_=grow.ap.rearrange("p w d -> p (w d)")), grow.b, reads=[grow.b])
        gp = alloc([8], F32, "gp")
        bp = alloc([8], F32, "bp")
        if L == 0:
            P.op("dve", lambda e: e.memset(gp.ap, 1.0), writes=[gp.b])
            P.op("dve", lambda e: e.memset(bp.ap, 0.0), writes=[bp.b])
        else:
            pfp = alloc([NPF], F32, "pfp", dma=True)
            P.dma("sp", lambda e, L=L: e.dma_start(out=pfp.ap, in_=pfm_in[L - 1]), pfp.b, writes=[pfp.b])
            o1, o2 = PFM_OFF["ln2g"][0], PFM_OFF["ln2b"][0]
            P.op("dve", lambda e: e.tensor_copy(out=gp.ap, in_=pfp.ap[:, o1:o1 + 8]), reads=[pfp.b], writes=[gp.b])
            P.op("dve", lambda e: e.tensor_copy(out=bp.ap, in_=pfp.ap[:, o2:o2 + 8]), reads=[pfp.b], writes=[bp.b])
        tmp1 = alloc([8, 2], F32, "tmp1")
        for stg in range(2):
            gsrc = gp.ap if stg == 0 else pfv("ln1g")
            bsrc = bp.ap if stg == 0 else pfv("ln1b")
            gb = [gp.b, bp.b] if stg == 0 else [pf.b]
            shc, scc = (0, 1) if stg == 0 else (3, 4)
            P.op("dve", lambda e, scc=scc: e.tensor_scalar(out=tmp1.ap, in0=modfm.ap[:, scc * 8:(scc + 1) * 8, :],
                                                           scalar1=1.0, scalar2=None, op0=ALU.add),
                 reads=[modfm.b], writes=[tmp1.b])
            P.op("dve", lambda e, stg=stg, gsrc=gsrc: e.tensor_tensor(
                out=AB.ap[:, stg, :, 0:2], in0=tmp1.ap, in1=gsrc.unsqueeze(2).broadcast_to([128, 8, 2]), op=ALU.mult),
                reads=[tmp1.b] + gb, writes=[AB.b])
            P.op("dve", lambda e, bsrc=bsrc: e.tensor_tensor(
                out=tmp1.ap, in0=tmp1.ap, in1=bsrc.unsqueeze(2).broadcast_to([128, 8, 2]), op=ALU.mult),
                reads=[tmp1.b] + gb, writes=[tmp1.b])
            P.op("dve", lambda e, stg=stg, shc=shc: e.tensor_tensor(
                out=AB.ap[:, stg, :, 2:4], in0=tmp1.ap, in1=modfm.ap[:, shc * 8:(shc + 1) * 8, :], op=ALU.add),
                reads=[tmp1.b, modfm.b], writes=[AB.b])
        o = PROW_OFF["dtb"][0]
        P.dma("sp", lambda e, L=L, o=o: e.dma_start(
            out=hrow.ap.rearrange("p a b c -> p (a b c)"), in_=prow_in[L][:, o:o + 96].partition_broadcast(128)),
            hrow.b, writes=[hrow.b])
        P.op("act", lambda e: e.activation(out=negA.ap, in_=hrow.ap[:, 1], func=AF.Exp), reads=[hrow.b], writes=[negA.b])
        P.op("dve", lambda e: e.tensor_scalar(out=negA.ap, in0=negA.ap, scalar1=-1.0, scalar2=None, op0=ALU.mult),
             reads=[negA.b], writes=[negA.b])

        P.barrier()
        aoff[0] = layer_mark
        w_in = alloc([8, D_PROJ], BF16, "w_in")
        win3 = W["w_in"][L].rearrange("(k p) n -> p k n", p=128)
        win_pc = [P.buf("w_in_pc%d" % i, sw=True) for i in range(9)]
        for wpc in (1, 0, 4, 5, 6, 7, 2, 3, 8):
            c0, c1 = wpc * 512, min((wpc + 1) * 512, D_PROJ)
            P.dma("pool", lambda e, c0=c0, c1=c1: e.dma_start(out=w_in.ap[:, :, c0:c1], in_=win3[:, :, c0:c1]),
                  win_pc[wpc], writes=[win_pc[wpc]])
        d31 = alloc([4, CONV_K, 128], BF16, "d31")
        o = PFM_OFF["convw"][0]
        for cc in range(4):
            P.op("dve", lambda e, cc=cc, o=o: e.tensor_tensor(
                out=d31.ap[:, cc], in0=ident.ap.unsqueeze(1).broadcast_to([128, CONV_K, 128]),
                in1=pf.ap[:, o + cc * CONV_K:o + (cc + 1) * CONV_K].unsqueeze(2).broadcast_to([128, CONV_K, 128]),
                op=ALU.mult), reads=[ident.b, pf.b], writes=[d31.b])
        convb_r = alloc([512], BF16, "convb_r", parts=1)
        cb_st = alloc([512], F32, "cb_st", dma=True, parts=1)
        o = PROW_OFF["convb"][0]
        P.dma("sp", lambda e, L=L, o=o: e.dma_start(out=cb_st.ap, in_=prow_in[L][:, o:o + 512]), cb_st.b, writes=[cb_st.b])
        P.op("dve", lambda e: e.tensor_copy(out=convb_r.ap, in_=cb_st.ap), reads=[cb_st.b], writes=[convb_r.b])
        dtb_bc = hrow.ap[:, 0]
        xin = [alloc([2, D], F32, "xin%d" % i, dma=True) for i in range(2)]
        hT = [alloc([8, TB], BF16, "hT%d" % i) for i in range(2)]
        vpad = [alloc([4, 2, 188], BF16, "vpad%d" % i) for i in range(2)]
        for t in vpad:
            pool_fill(t, t.ap, 0.0)
        sig = [alloc([TB], F32, "sig%d" % i) for i in range(2)]
        xbc_ev = [alloc([16, TB], BF16, "xbc_ev%d" % i, dma=True) for i in range(2)]
        yhat = [alloc([512], BF16, "yhat%d" % i) for i in range(2)]
        cT_ev = [alloc([4, TB], BF16, "cT_ev%d" % i, dma=True) for i in range(2)]
        sz_ev = [alloc([2, D_INNER], F32, "sz_ev%d" % i, dma=True) for i in range(2)]
        st6 = alloc([6], F32, "st6")
        mv = alloc([2], F32, "mv")
        rs = alloc([1], F32, "rs")
        dtt = alloc([2, HEADS], F32, "dtt")
        cvT3 = convT_d.rearrange("(c p) t -> p c t", p=128)
        pc = [0]

        def nextps():
            pc[0] += 1
            return psum[pc[0] % 8]

        nblk = TALL // TB
        for bi in range(nblk):
            tok0 = bi * TB
            seg = 0 if tok0 < CTX else 1
            full = not (last and seg == 0)
            xi, h, vp, xe, ce, se = xin[bi % 2], hT[bi % 2], vpad[bi % 2], xbc_ev[bi % 2], cT_ev[bi % 2], sz_ev[bi % 2]
            for nb in ([0, 1] if bi == 0 else [bi + 1]):
                if nb < nblk:
                    xn = xin[nb % 2]
                    P.dma("sp", lambda e, xn=xn, nb=nb: e.dma_start(
                        out=xn.ap, in_=stream_rows(nb * TB, TB).rearrange("(j p) d -> p j d", p=128)), xn.b, writes=[xn.b])
            for k in range(8):
                pb = nextps()
                for j in range(2):
                    P.op("pe", lambda e, pb=pb, xi=xi, j=j, k=k: e.transpose(
                        out=pb.ap[:, j * 128:(j + 1) * 128], in_=xi.ap[:, j, k * 128:(k + 1) * 128], identity=ident.ap),
                        reads=[xi.b, ident.b], writes=[pb.b])
                P.op("act", lambda e, pb=pb, h=h, k=k, seg=seg: e.activation(
                    out=h.ap[:, k, :], in_=pb.ap[:, 0:TB], func=AF.Identity,
                    scale=AB.ap[:, 0, k, seg:seg + 1], bias=AB.ap[:, 0, k, 2 + seg:3 + seg]),
                    reads=[pb.b, AB.b], writes=[h.b])

            def ws_mm(col0, pb, h=h):
                for k in range(8):
                    P.op("pe", lambda e, k=k, pb=pb, col0=col0: e.matmul(
                        pb.ap[:, 0:TB], lhsT=w_in.ap[:, k, col0:col0 + 128], rhs=h.ap[:, k, :],
                        start=(k == 0), stop=(k == 7)), reads=[win_pc[col0 // 512], h.b], writes=[pb.b])
            if full:
                for cc in range(4):
                    sg = sig[cc % 2]
                    pg = nextps()
                    ws_mm(512 + cc * 128, pg)
                    P.op("act", lambda e, pg=pg, sg=sg: e.activation(out=sg.ap, in_=pg.ap[:, 0:TB], func=AF.Sigmoid),
                         reads=[pg.b], writes=[sg.b])
                    pa = nextps()
                    ws_mm(cc * 128, pa)
                    P.op("dve", lambda e, pa=pa, sg=sg, vp=vp, cc=cc: e.tensor_tensor(
                        out=vp.ap[:, cc, :, 30:158].rearrange("p j (w r) -> p j r w", r=2),
                        in0=pa.ap[:, 0:TB].rearrange("p (j r w) -> p j r w", j=2, r=2),
                        in1=sg.ap.rearrange("p (j r w) -> p j r w", j=2, r=2), op=ALU.mult),
                        reads=[pa.b, sg.b], writes=[vp.b])
            for cc in range(16):
                pb = nextps()
                ws_mm(2048 + cc * 128, pb)
                if cc % 2 == 0:
                    P.op("act", lambda e, pb=pb, xe=xe, cc=cc: e.activation(out=xe.ap[:, cc, :], in_=pb.ap[:, 0:TB],
                                                                          func=AF.Identity), reads=[pb.b], writes=[xe.b])
                else:
                    P.op("dve", lambda e, pb=pb, xe=xe, cc=cc: e.tensor_copy(out=xe.ap[:, cc, :], in_=pb.ap[:, 0:TB]),
                         reads=[pb.b], writes=[xe.b])
            col0 = xbc_col(tok0)
            P.dma("act", lambda e, xe=xe, col0=col0: e.dma_start(out=xb3[:, :, col0:col0 + TB], in_=xe.ap), xe.b,
                  reads=[xe.b])
            for j in range(2):
                ch = (tok0 // 128) + j
                if full:
                    pcv = nextps()
                    for cc in range(4):
                        P.op("pe", lambda e, pcv=pcv, cc=cc: e.matmul(
                            pcv.ap[:, cc * 128:(cc + 1) * 128], lhsT=onesb.ap[0:1, :],
                            rhs=convb_r.ap[0:1, cc * 128:(cc + 1) * 128], start=True, stop=False, skip_group_check=True),
                            reads=[onesb.b, convb_r.b], writes=[pcv.b])
                        for tp in range(CONV_K):
                            P.op("pe", lambda e, pcv=pcv, cc=cc, tp=tp, vp=vp, j=j: e.matmul(
                                pcv.ap[:, cc * 128:(cc + 1) * 128], lhsT=vp.ap[:, cc, j, 2 * tp:2 * tp + 128],
                                rhs=d31.ap[:, cc, tp, :], start=False, stop=(tp == CONV_K - 1), skip_group_check=True),
                                reads=[vp.b, d31.b], writes=[pcv.b])
                    yh = yhat[j]
                    P.op("dve", lambda e, pcv=pcv: e.bn_stats(out=st6.ap, in_=pcv.ap), reads=[pcv.b], writes=[st6.b])
                    P.op("dve", lambda e: e.bn_aggr(out=mv.ap, in_=st6.ap), reads=[st6.b], writes=[mv.b])
                    rstd_from(mv.ap[:, 1:2], rs, epsln, [mv.b])
                    P.op("dve", lambda e, pcv=pcv, yh=yh: e.tensor_scalar(
                        out=yh.ap, in0=pcv.ap, scalar1=mv.ap[:, 0:1], scalar2=rs.ap[:, 0:1], op0=ALU.subtract, op1=ALU.mult),
                        reads=[pcv.b, mv.b, rs.b], writes=[yh.b])
                    pt = nextps()
                    pti = psum.index(pt)
                    for cc in range(4):
                        P.op("pe", lambda e, pti=pti, yh=yh, cc=cc: e.transpose(
                            out=psb(pti)[:, cc * 128:(cc + 1) * 128], in_=yh.ap[:, cc * 128:(cc + 1) * 128],
                            identity=identb.ap), reads=[yh.b, identb.b], writes=[pt.b])
                    og, ob = PFM_OFF["lng"][0], PFM_OFF["lnb"][0]
                    for cc in range(4):
                        P.op("act", lambda e, pti=pti, ce=ce, cc=cc, j=j, og=og, ob=ob: e.activation(
                            out=ce.ap[:, cc, j * 128:(j + 1) * 128].rearrange("p (r w) -> p w r", r=2),
                            in_=psb(pti)[:, cc * 128:(cc + 1) * 128].rearrange("p (w r) -> p w r", r=2), func=AF.Silu,
                            scale=pf.ap[:, og + cc:og + cc + 1], bias=pf.ap[:, ob + cc:ob + cc + 1]),
                            reads=[pt.b, pf.b], writes=[ce.b])
                    for half in range(2):
                        pz = nextps()
                        for k in range(8):
                            P.op("pe", lambda e, pz=pz, k=k, j=j, half=half, h=h: e.matmul(
                                pz.ap, lhsT=h.ap[:, k, j * 128:(j + 1) * 128],
                                rhs=w_in.ap[:, k, 1024 + half * 512:1024 + (half + 1) * 512],
                                start=(k == 0), stop=(k == 7)), reads=[h.b, win_pc[2 + half]], writes=[pz.b])
                        P.op("act", lambda e, pz=pz, se=se, j=j, half=half: e.activation(
                            out=se.ap[:, j, half * 512:(half + 1) * 512], in_=pz.ap, func=AF.Silu),
                            reads=[pz.b], writes=[se.b])
                pd = nextps()
                for k in range(8):
                    P.op("pe", lambda e, pd=pd, k=k, j=j, h=h: e.matmul(
                        pd.ap[:, 0:16], lhsT=h.ap[:, k, j * 128:(j + 1) * 128], rhs=w_in.ap[:, k, 4096:4112],
                        start=(k == 0), stop=(k == 7)), reads=[h.b, win_pc[8]], writes=[pd.b])
                P.op("dve", lambda e, pd=pd: e.tensor_tensor(
                    out=dtt.ap, in0=pd.ap[:, 0:16].unsqueeze(1).broadcast_to([128, 2, HEADS]), in1=dtb_bc, op=ALU.add),
                    reads=[pd.b, hrow.b], writes=[dtt.b])
                P.op("act", lambda e: e.activation(out=dtt.ap, in_=dtt.ap, func=AF.Exp), reads=[dtt.b], writes=[dtt.b])
                P.op("act", lambda e, ch=ch: e.activation(out=dt_sb.ap[:, ch], in_=dtt.ap, func=AF.Ln, bias=one_t.ap[:, 0:1]),
                     reads=[dtt.b, one_t.b], writes=[dt_sb.b])
            if full:
                P.dma("act", lambda e, ce=ce, tok0=tok0: e.dma_start(out=cvT3[:, :, tok0:tok0 + TB], in_=ce.ap), ce.b,
                      reads=[ce.b])
                P.dma("act", lambda e, se=se, tok0=tok0: e.dma_start(
                    out=siluz_d[tok0:tok0 + TB, :].rearrange("(j p) d -> p j d", p=128), in_=se.ap), se.b, reads=[se.b])

        P.barrier()
        P.release([t.b for t in xin + xbc_ev + cT_ev + sz_ev] + win_pc + [cb_st.b, prow2.b, csil.b, grow.b] + [t.b for t in wm] + ([pfp.b] if L > 0 else []))
        aoff[0] = layer_mark
        d5 = alloc([16, SK, 128], BF16, "d5")
        o = PFM_OFF["sconvw"][0]
        for c4 in range(4):
            P.op("dve", lambda e, c4=c4, o=o: e.tensor_tensor(
                out=d5.ap[:, c4 * 4:(c4 + 1) * 4].rearrange("p c k j -> p (c k) j"),
                in0=ident.ap.unsqueeze(1).broadcast_to([128, 4 * SK, 128]),
                in1=pf.ap[:, o + c4 * 4 * SK:o + (c4 + 1) * 4 * SK].unsqueeze(2).broadcast_to([128, 4 * SK, 128]),
                op=ALU.mult), reads=[ident.b, pf.b], writes=[d5.b])
        scb_r = alloc([1536], BF16, "scb_r", parts=1)
        sb_st = alloc([1536], F32, "sb_st", dma=True, parts=1)
        o = PROW_OFF["sconvb"][0]
        P.dma("sp", lambda e, L=L, o=o: e.dma_start(out=sb_st.ap, in_=prow_in[L][:, o:o + 1536]), sb_st.b, writes=[sb_st.b])
        P.op("dve", lambda e: e.tensor_copy(out=scb_r.ap, in_=sb_st.ap), reads=[sb_st.b], writes=[scb_r.b])
        xc = [alloc([16, 132], BF16, "xc%d" % i, dma=True) for i in range(3)]
        xs_ev = [alloc([D_INNER], BF16, "xs_ev%d" % i, dma=True) for i in range(2)]
        bt_ev = [alloc([512], BF16, "bt_ev%d" % i, dma=True) for i in range(2)]
        BT_ev = [alloc([4, 128], BF16, "BT_ev%d" % i, dma=True) for i in range(2)]
        CT_ev = [alloc([4, 128], BF16, "CT_ev%d" % i, dma=True) for i in range(2)]
        BT3 = BT_d.rearrange("(g p) t -> p g t", p=128)
        CT3 = CT_d.rearrange("(g p) t -> p g t", p=128)
        osb = PFM_OFF["sconvb"][0]
        for ch in range(NCH):
            tok0 = ch * 128
            x_, xs_, bt_, BT_, CT_ = xc[ch % 3], xs_ev[ch % 2], bt_ev[ch % 2], BT_ev[ch % 2], CT_ev[ch % 2]
            for nch in ([0, 1, 2] if ch == 0 else [ch + 2]):
                if nch < NCH:
                    xn = xc[nch % 3]
                    coln = xbc_col(nch * 128) - 2
                    P.dma("sp", lambda e, xn=xn, coln=coln: e.dma_start(out=xn.ap, in_=xb3[:, :, coln:coln + 132]), xn.b,
                          writes=[xn.b])
            for which, dst in ((0, BT_), (1, CT_)):
                pb = nextps()
                for g in range(4):
                    cc = 8 + which * 4 + g
                    for tp in range(SK):
                        P.op("pe", lambda e, pb=pb, g=g, cc=cc, tp=tp, x_=x_: e.matmul(
                            pb.ap[:, g * 128:(g + 1) * 128], lhsT=d5.ap[:, cc, tp, :], rhs=x_.ap[:, cc, tp:tp + 128],
                            start=(tp == 0), stop=(tp == SK - 1)), reads=[d5.b, x_.b], writes=[pb.b])
                for g in range(4):
                    cc = 8 + which * 4 + g
                    P.op("act", lambda e, pb=pb, g=g, cc=cc, dst=dst: e.activation(
                        out=dst.ap[:, g, :], in_=pb.ap[:, g * 128:(g + 1) * 128], func=AF.Silu,
                        bias=pf.ap[:, osb + cc:osb + cc + 1]), reads=[pb.b, pf.b], writes=[dst.b])
            for grp in range(3):
                pb = nextps()
                for q in range(4):
                    cc = grp * 4 + q
                    P.op("pe", lambda e, pb=pb, q=q, cc=cc: e.matmul(
                        pb.ap[:, q * 128:(q + 1) * 128], lhsT=onesb.ap[0:1, :], rhs=scb_r.ap[0:1, cc * 128:(cc + 1) * 128],
                        start=True, stop=False, skip_group_check=True), reads=[onesb.b, scb_r.b], writes=[pb.b])
                    for tp in range(SK):
                        P.op("pe", lambda e, pb=pb, q=q, cc=cc, tp=tp, x_=x_: e.matmul(
                            pb.ap[:, q * 128:(q + 1) * 128], lhsT=x_.ap[:, cc, tp:tp + 128], rhs=d5.ap[:, cc, tp, :],
                            start=False, stop=(tp == SK - 1), skip_group_check=True), reads=[d5.b, x_.b], writes=[pb.b])
                dst = xs_.ap[:, grp * 512:(grp + 1) * 512] if grp < 2 else bt_.ap
                dstb = xs_.b if grp < 2 else bt_.b
                P.op("act", lambda e, pb=pb, dst=dst: e.activation(out=dst, in_=pb.ap, func=AF.Silu),
                     reads=[pb.b], writes=[dstb])
            P.dma("act", lambda e, xs_=xs_, tok0=tok0: e.dma_start(out=xs_d[tok0:tok0 + 128, :], in_=xs_.ap), xs_.b,
                  reads=[xs_.b])
            P.dma("act", lambda e, bt_=bt_, tok0=tok0: e.dma_start(out=bt_d[tok0:tok0 + 128, :], in_=bt_.ap), bt_.b,
                  reads=[bt_.b])
            P.dma("act", lambda e, BT_=BT_, tok0=tok0: e.dma_start(out=BT3[:, :, tok0:tok0 + 128], in_=BT_.ap), BT_.b,
                  reads=[BT_.b])
            P.dma("act", lambda e, CT_=CT_, tok0=tok0: e.dma_start(out=CT3[:, :, tok0:tok0 + 128], in_=CT_.ap), CT_.b,
                  reads=[CT_.b])

        P.barrier()
        P.release([t.b for t in xc + xs_ev + bt_ev + BT_ev + CT_ev] + [sb_st.b])
        aoff[0] = layer_mark
        w_out = alloc([12, D], BF16, "w_out")
        w_out.b = P.buf("w_out_sw", sw=True)
        wo3 = W["w_out"][L].rearrange("(k p) n -> p k n", p=128)
        for k in range(12):
            P.dma("pool", lambda e, k=k: e.dma_start(out=w_out.ap[:, k, :], in_=wo3[:, k, :]), w_out.b, writes=[w_out.b])
        dsk = alloc([2 * HEADS, 128], BF16, "dsk")
        P.op("dve", lambda e: e.tensor_tensor(
            out=dsk.ap, in0=ident.ap.unsqueeze(1).broadcast_to([128, 2 * HEADS, 128]),
            in1=hrow.ap[:, 2].rearrange("p a b -> p (a b)").unsqueeze(2).broadcast_to([128, 2 * HEADS, 128]),
            op=ALU.mult), reads=[ident.b, hrow.b], writes=[dsk.b])
        NS = 3
        xs_l = [alloc([D_INNER], BF16, "xs_l%d" % i, dma=True) for i in range(NS)]
        bt_l = [alloc([512], BF16, "bt_l%d" % i, dma=True) for i in range(NS)]
        BT_l = [alloc([4, 128], BF16, "BT_l%d" % i, dma=True) for i in range(NS)]
        CT_l = [alloc([4, 128], BF16, "CT_l%d" % i, dma=True) for i in range(NS)]
        hst = alloc([D_INNER], F32, "hst")
        hbf_t = [alloc([D_INNER], BF16, "hbf%d" % i) for i in range(2)]
        htmp = alloc([D_INNER], F32, "htmp")
        dA_t = [alloc([HEADS], F32, "dA%d" % i) for i in range(NS)]
        cs_t = [alloc([HEADS], F32, "cs%d" % i) for i in range(NS)]
        cshi_t = [alloc([HEADS], BF16, "cshi%d" % i) for i in range(NS)]
        cslo_t = [alloc([HEADS], BF16, "cslo%d" % i) for i in range(NS)]
        wv_t = [alloc([HEADS], F32, "wv%d" % i) for i in range(NS)]
        cdec_t = [alloc([HEADS], F32, "cdec%d" % i) for i in range(NS)]
        E1_t = [alloc([4, 128], BF16, "E1_%d" % i) for i in range(3)]
        E2_t = [alloc([4, 128], BF16, "E2_%d" % i) for i in range(3)]
        Mo_t = [alloc([HEADS, 128], BF16, "Mo%d" % i) for i in range(NS)]
        Md_t = [alloc([HEADS, 128], BF16, "Md%d" % i) for i in range(NS)]
        cbm_t = [alloc([4, 128], BF16, "cbm%d" % i) for i in range(NS)]
        xdt_t = [alloc([D_INNER], BF16, "xdt%d" % i) for i in range(NS)]
        xw_t = [alloc([D_INNER], BF16, "xw%d" % i) for i in range(NS)]
        yfs = [alloc([D_INNER], F32, "yfs%d" % i, dma=True) for i in range(2)]
        szl = [alloc([D_INNER], F32, "szl%d" % i, dma=True) for i in range(2)]
        cvl = [alloc([4, 128], BF16, "cvl%d" % i, dma=True) for i in range(2)]
        xres = [alloc([D], F32, "xres%d" % i, dma=True) for i in range(2)]
        gbuf_t = [alloc([D_INNER], F32, "gbuf%d" % i) for i in range(2)]
        gsq = alloc([D_INNER], F32, "gsq")
        gnb = alloc([D_INNER], BF16, "gnb")
        gT = alloc([8, 128], BF16, "gT")
        ssq = alloc([4], F32, "ssq")
        rr = alloc([D], F32, "rr")
        xo = [alloc([D], F32, "xo%d" % i, dma=True) for i in range(2)]
        st12 = alloc([12], F32, "st12")
        agb = alloc([D], F32, "agb", dma=True)
        abb = alloc([D], F32, "abb", dma=True)
        g1b = alloc([D], F32, "g1b", dma=True)
        onw = PFM_OFF["normw"][0]

        def load_bc(seg, stage):
            if stage == 0:
                if L == 0:
                    P.op("pool", lambda e: e.memset(agb.ap, ALPHA), writes=[agb.b])
                    P.op("pool", lambda e: e.memset(abb.ap, 0.0), writes=[abb.b])
                    srcs = None
                else:
                    srcs = (prow_in[L - 1], PROW_OFF["ln2g"][0], PROW_OFF["ln2b"][0])
            else:
                srcs = (prow_in[L], PROW_OFF["ln1g"][0], PROW_OFF["ln1b"][0])
            if srcs is not None:
                pr, og_, ob_ = srcs
                P.dma("sp", lambda e: e.dma_start(out=agb.ap, in_=pr[:, og_:og_ + D].partition_broadcast(128)), agb.b,
                      writes=[agb.b])
                P.dma("sp", lambda e: e.dma_start(out=abb.ap, in_=pr[:, ob_:ob_ + D].partition_broadcast(128)), abb.b,
                      writes=[abb.b])
                P.op("pool", lambda e: e.tensor_scalar(out=agb.ap, in0=agb.ap, scalar1=ALPHA, scalar2=None, op0=ALU.mult),
                     reads=[agb.b], writes=[agb.b])
                P.op("pool", lambda e: e.tensor_scalar(out=abb.ap, in0=abb.ap, scalar1=ALPHA, scalar2=None, op0=ALU.mult),
                     reads=[abb.b], writes=[abb.b])
            P.dma("sp", lambda e: e.dma_start(out=g1b.ap, in_=g_d[seg, stage:stage + 1, :].partition_broadcast(128)), g1b.b,
                  writes=[g1b.b])

        def residual_pre(xr):
            P.op("pool", lambda e, xr=xr: e.tensor_tensor(out=xr.ap, in0=xr.ap, in1=agb.ap, op=ALU.mult),
                 reads=[xr.b, agb.b], writes=[xr.b])
            P.op("pool", lambda e, xr=xr: e.tensor_tensor(out=xr.ap, in0=xr.ap, in1=abb.ap, op=ALU.add),
                 reads=[xr.b, abb.b], writes=[xr.b])

        def residual_ln(pbs, xr, out_t, final=None, pre=True):
            if pre:
                residual_pre(xr)
            for half in range(2):
                sl = slice(half * 512, (half + 1) * 512)
                P.op("dve", lambda e, half=half, sl=sl: e.tensor_tensor(out=rr.ap[:, sl], in0=pbs[half].ap, in1=g1b.ap[:, sl],
                                                                        op=ALU.mult), reads=[pbs[half].b, g1b.b], writes=[rr.b])
            P.op("dve", lambda e, xr=xr: e.tensor_tensor(out=rr.ap, in0=rr.ap, in1=xr.ap, op=ALU.add),
                 reads=[rr.b, xr.b], writes=[rr.b])
            for half in range(2):
                P.op("dve", lambda e, half=half: e.bn_stats(out=st12.ap[:, half * 6:(half + 1) * 6],
                                                            in_=rr.ap[:, half * 512:(half + 1) * 512]),
                     reads=[rr.b], writes=[st12.b])
            P.op("dve", lambda e: e.bn_aggr(out=mv.ap, in_=st12.ap), reads=[st12.b], writes=[mv.b])
            rstd_from(mv.ap[:, 1:2], rs, epsln, [mv.b])
            P.op("dve", lambda e, out_t=out_t: e.tensor_scalar(
                out=out_t.ap, in0=rr.ap, scalar1=mv.ap[:, 0:1], scalar2=rs.ap[:, 0:1], op0=ALU.subtract, op1=ALU.mult),
                reads=[rr.b, mv.b, rs.b], writes=[out_t.b])
            if final is not None:
                fg, fb = final
                P.op("pool", lambda e, out_t=out_t: e.tensor_tensor(out=out_t.ap, in0=out_t.ap, in1=fg.ap, op=ALU.mult),
                     reads=[out_t.b, fg.b], writes=[out_t.b])
                P.op("pool", lambda e, out_t=out_t: e.tensor_tensor(out=out_t.ap, in0=out_t.ap, in1=fb.ap, op=ALU.add),
                     reads=[out_t.b, fb.b], writes=[out_t.b])

        st6 = alloc([6], F32, "st6b")
        mv = alloc([2], F32, "mvb")
        rs = alloc([1], F32, "rsb")
        pcs_b, pcb_b, pR, py, pst = psum[0], psum[1], [psum[2], psum[3], psum[4]], [psum[5], psum[6]], [psum[7], psum[1]]
        yf_bufs = [P.buf("yfd%d" % i) for i in range(NCH)]
        pt_b, pti = psum[7], 7
        for d in range(NDIR):
            tri, msk = (triF, maskF) if d == 0 else (triB, maskB)
            if d == 0:
                order = list(range(NCH))
            else:
                order = list(range(NCC - 1, -1, -1)) + list(range(NCH - 1, NCC - 1, -1))
            P.op("dve", lambda e: e.memset(hst.ap, 0.0), writes=[hst.b])
            for hb_ in hbf_t:
                P.op("pool", lambda e, hb_=hb_: e.memset(hb_.ap, 0.0), writes=[hb_.b])
            state = {"seg": -1, "rq": 0}

            def wanty(ch):
                return not (last and seg_of_chunk(ch) == 0)

            def stage_A_load(it, ch):
                tok0 = ch * 128
                b2 = it % NS
                xs_, bt_, BT_, CT_ = xs_l[b2], bt_l[b2], BT_l[b2], CT_l[b2]
                P.dma("sp", lambda e: e.dma_start(out=xs_.ap, in_=xs_d[tok0:tok0 + 128, :]), xs_.b, writes=[xs_.b])
                P.dma("sp", lambda e: e.dma_start(out=bt_.ap, in_=bt_d[tok0:tok0 + 128, :]), bt_.b, writes=[bt_.b])
                if wanty(ch):
                    P.dma("sp", lambda e: e.dma_start(out=BT_.ap, in_=BT3[:, :, tok0:tok0 + 128]), BT_.b, writes=[BT_.b])
                    P.dma("sp", lambda e: e.dma_start(out=CT_.ap, in_=CT3[:, :, tok0:tok0 + 128]), CT_.b, writes=[CT_.b])

            def stage_A(it, ch, d=d, tri=tri, msk=msk):
                tok0 = ch * 128
                b2 = it % NS
                want_y = wanty(ch)
                xs_, bt_, BT_, CT_ = xs_l[b2], bt_l[b2], BT_l[b2], CT_l[b2]
                dA, cs, wv, cdec, xdt, xw, cbm, Mo, Md = (dA_t[b2], cs_t[b2], wv_t[b2], cdec_t[b2], xdt_t[b2], xw_t[b2],
                                                         cbm_t[b2], Mo_t[b2], Md_t[b2])
                cshi, cslo = cshi_t[b2], cslo_t[b2]
                dtv = dt_sb.ap[:, ch, d, :]
                P.op("dve", lambda e: e.tensor_tensor(out=dA.ap, in0=dtv, in1=negA.ap[:, d, :], op=ALU.mult),
                     reads=[dt_sb.b, negA.b], writes=[dA.b])
                P.op("pe", lambda e: e.matmul(pcs_b.ap[:, 0:16], lhsT=tri.ap, rhs=dA.ap, start=True, stop=True),
                     reads=[tri.b, dA.b], writes=[pcs_b.b])
                P.op("pe", lambda e: e.matmul(pcs_b.ap[:, 16:32], lhsT=triF.ap[:, 127:128].broadcast_to([128, 128]),
                                              rhs=dA.ap, start=True, stop=True), reads=[triF.b, dA.b], writes=[pcs_b.b])
                P.op("dve", lambda e: e.tensor_scalar(out=cs.ap, in0=pcs_b.ap[:, 0:16], scalar1=-1.0, scalar2=None, op0=ALU.mult),
                     reads=[pcs_b.b], writes=[cs.b])
                P.op("dve", lambda e: e.tensor_tensor(out=wv.ap, in0=pcs_b.ap[:, 16:32], in1=cs.ap, op=ALU.add),
                     reads=[pcs_b.b, cs.b], writes=[wv.b])
                P.op("dve", lambda e: e.tensor_copy(out=cshi.ap, in_=cs.ap), reads=[cs.b], writes=[cshi.b])
                P.op("dve", lambda e: e.tensor_tensor(out=cslo.ap, in0=cs.ap, in1=cshi.ap, op=ALU.subtract),
                     reads=[cs.b, cshi.b], writes=[cslo.b])
                P.op("act", lambda e: e.activation(out=cdec.ap, in_=pcs_b.ap[:, 16:32], func=AF.Exp),
                     reads=[pcs_b.b], writes=[cdec.b])
                P.op("act", lambda e: e.activation(out=wv.ap, in_=wv.ap, func=AF.Exp), reads=[wv.b], writes=[wv.b])
                P.op("dve", lambda e: e.tensor_tensor(out=wv.ap, in0=wv.ap, in1=dtv, op=ALU.mult),
                     reads=[wv.b, dt_sb.b], writes=[wv.b])
                yield
                xs3 = xs_.ap.rearrange("p (h q) -> p h q", q=HP)
                P.op("pool", lambda e: e.tensor_tensor(
                    out=xw.ap.rearrange("p (h q) -> p h q", q=HP), in0=xs3,
                    in1=wv.ap.unsqueeze(2).broadcast_to([128, HEADS, HP]), op=ALU.mult),
                    reads=[xs_.b, wv.b], writes=[xw.b])
                if not want_y:
                    return
                P.op("dve", lambda e: e.tensor_tensor(
                    out=xdt.ap.rearrange("p (h q) -> p h q", q=HP), in0=xs3,
                    in1=dtv.unsqueeze(2).broadcast_to([128, HEADS, HP]), op=ALU.mult),
                    reads=[xs_.b, dt_sb.b], writes=[xdt.b])
                for g in range(4):
                    P.op("pe", lambda e, g=g: e.matmul(
                        pcb_b.ap[:, g * 128:(g + 1) * 128], lhsT=BT_.ap[:, g, :], rhs=CT_.ap[:, g, :], start=True, stop=True),
                        reads=[BT_.b, CT_.b], writes=[pcb_b.b])
                P.op("dve", lambda e: e.tensor_copy(out=cbm.ap.rearrange("p g l -> p (g l)"), in_=pcb_b.ap),
                     reads=[pcb_b.b], writes=[cbm.b])
                qinfo = []

                def part2(q, pb, E2):
                    P.op("pe", lambda e: e.matmul(pb.ap, lhsT=identb.ap, rhs=msk.ap.rearrange("p g l -> p (g l)"),
                                                  start=False, stop=False, skip_group_check=True),
                         reads=[identb.b, msk.b], writes=[pb.b])
                    P.op("pe", lambda e: e.matmul(pb.ap, lhsT=identb.ap,
                                                  rhs=cshi.ap[:, q * 4:(q + 1) * 4].unsqueeze(2).broadcast_to([128, 4, 128]),
                                                  start=False, stop=False, skip_group_check=True),
                         reads=[identb.b, cshi.b], writes=[pb.b])
                    P.op("pe", lambda e: e.matmul(pb.ap, lhsT=identb.ap,
                                                  rhs=cslo.ap[:, q * 4:(q + 1) * 4].unsqueeze(2).broadcast_to([128, 4, 128]),
                                                  start=False, stop=True, skip_group_check=True),
                         reads=[identb.b, cslo.b], writes=[pb.b])
                    P.op("act", lambda e: e.activation(out=E2.ap.rearrange("p r l -> p (r l)"), in_=pb.ap, func=AF.Exp),
                         reads=[pb.b], writes=[E2.b])
                    P.op("dve", lambda e: e.tensor_tensor(
                        out=Md.ap[:, q * 4:(q + 1) * 4, :], in0=E2.ap,
                        in1=cbm.ap[:, q:q + 1, :].broadcast_to([128, 4, 128]), op=ALU.mult),
                        reads=[E2.b, cbm.b], writes=[Md.b])

                for q in range(4):
                    state["rq"] += 1
                    pb = pR[state["rq"] % 3]
                    E1, E2 = E1_t[state["rq"] % 3], E2_t[state["rq"] % 3]
                    for r in range(4):
                        h = q * 4 + r
                        P.op("pe", lambda e, pb=pb, h=h, r=r: e.matmul(
                            pb.ap[:, r * 128:(r + 1) * 128], lhsT=dA.ap[:, h:h + 1].broadcast_to([128, 128]), rhs=tri.ap,
                            start=(r == 0), stop=False, skip_group_check=True), reads=[dA.b, tri.b], writes=[pb.b])
                    P.op("act", lambda e, pb=pb, E1=E1: e.activation(out=E1.ap.rearrange("p r l -> p (r l)"), in_=pb.ap,
                                                                    func=AF.Exp), reads=[pb.b], writes=[E1.b])
                    P.op("pool", lambda e, E1=E1, q=q: e.tensor_tensor(
                        out=Mo.ap[:, q * 4:(q + 1) * 4, :], in0=E1.ap,
                        in1=CT_.ap[:, q:q + 1, :].broadcast_to([128, 4, 128]), op=ALU.mult),
                        reads=[E1.b, CT_.b], writes=[Mo.b])
                    if qinfo:
                        part2(*qinfo.pop())
                    qinfo.append((q, pb, E2))
                    yield
                part2(*qinfo.pop())

            def stage_B(it, ch, d=d):
                tok0 = ch * 128
                b2 = it % NS
                want_y = wanty(ch)
                xs_, bt_ = xs_l[b2], bt_l[b2]
                cdec, xdt, xw, Mo, Md = cdec_t[b2], xdt_t[b2], xw_t[b2], Mo_t[b2], Md_t[b2]
                htm = htmp
                hb_prev, hb_new = hbf_t[(it + 1) % 2], hbf_t[it % 2]
                for g in range(4):
                    pb = pst[g // 2]
                    P.op("pe", lambda e, pb=pb, g=g: e.matmul(
                        pb.ap[:, (g % 2) * 256:(g % 2 + 1) * 256], lhsT=bt_.ap[:, g * 128:(g + 1) * 128],
                        rhs=xw.ap[:, g * 256:(g + 1) * 256], start=True, stop=True), reads=[bt_.b, xw.b], writes=[pb.b])
                P.op("pool", lambda e: e.tensor_tensor(
                    out=htm.ap.rearrange("p (h q) -> p h q", q=HP), in0=hst.ap.rearrange("p (h q) -> p h q", q=HP),
                    in1=cdec.ap.unsqueeze(2).broadcast_to([128, HEADS, HP]), op=ALU.mult),
                    reads=[hst.b, cdec.b], writes=[htm.b])
                if want_y:
                    for h in range(HEADS):
                        pyb = py[h // 8]
                        osl = slice((h % 8) * HP, (h % 8 + 1) * HP)
                        P.op("pe", lambda e, pyb=pyb, h=h, osl=osl: e.matmul(
                            pyb.ap[:, osl], lhsT=Md.ap[:, h, :], rhs=xdt.ap[:, h * HP:(h + 1) * HP],
                            start=(h % 8 == 0), stop=False, skip_group_check=True), reads=[Md.b, xdt.b], writes=[pyb.b])
                        P.op("pe", lambda e, pyb=pyb, h=h, osl=osl: e.matmul(
                            pyb.ap[:, osl], lhsT=dsk.ap[:, d * HEADS + h, :], rhs=xs_.ap[:, h * HP:(h + 1) * HP],
                            start=False, stop=False, skip_group_check=True), reads=[dsk.b, xs_.b], writes=[pyb.b])
                    for h in range(HEADS):
                        pyb = py[h // 8]
                        osl = slice((h % 8) * HP, (h % 8 + 1) * HP)
                        P.op("pe", lambda e, pyb=pyb, h=h, osl=osl: e.matmul(
                            pyb.ap[:, osl], lhsT=Mo.ap[:, h, :], rhs=hb_prev.ap[:, h * HP:(h + 1) * HP],
                            start=False, stop=True, skip_group_check=True), reads=[Mo.b, hb_prev.b], writes=[pyb.b])
                yield
                for half in range(2):
                    sl = slice(half * 512, (half + 1) * 512)
                    P.op("dve", lambda e, half=half, sl=sl: e.tensor_tensor(
                        out=hst.ap[:, sl], in0=htm.ap[:, sl], in1=pst[half].ap, op=ALU.add),
                        reads=[htm.b, pst[half].b], writes=[hst.b])
                yield
                P.op("act", lambda e: e.activation(out=hb_new.ap, in_=hst.ap, func=AF.Identity), reads=[hst.b],
                     writes=[hb_new.b])
                yield
                if not want_y:
                    return
                c2 = it % 2
                yf = yfs[c2]
                if d == 0:
                    P.op("act", lambda e: e.activation(out=yf.ap[:, 0:512], in_=py[0].ap, func=AF.Identity),
                         reads=[py[0].b], writes=[yf.b])
                    P.op("dve", lambda e: e.tensor_copy(out=yf.ap[:, 512:1024], in_=py[1].ap), reads=[py[1].b], writes=[yf.b])
                    P.dma("act", lambda e: e.dma_start(out=yf_d[tok0:tok0 + 128, :], in_=yf.ap), yf.b, reads=[yf.b],
                          writes=[yf_bufs[ch]])
                    return
                seg = seg_of_chunk(ch)
                sz, cv, xr, gbuf = szl[c2], cvl[c2], xres[c2], gbuf_t[c2]
                P.dma("sp", lambda e: e.dma_start(out=yf.ap, in_=yf_d[tok0:tok0 + 128, :]), yf.b, writes=[yf.b],
                      reads=[yf_bufs[ch]])
                P.dma("sp", lambda e: e.dma_start(out=sz.ap, in_=siluz_d[tok0:tok0 + 128, :]), sz.b, writes=[sz.b])
                P.dma("sp", lambda e: e.dma_start(out=cv.ap, in_=cvT3[:, :, tok0:tok0 + 128]), cv.b, writes=[cv.b])
                P.dma("sp", lambda e: e.dma_start(out=xr.ap, in_=stream_rows(tok0, 128)), xr.b, writes=[xr.b])
                for half in range(2):
                    sl = slice(half * 512, (half + 1) * 512)
                    P.op("dve", lambda e, sl=sl, half=half: e.tensor_tensor(
                        out=gbuf.ap[:, sl], in0=py[half].ap, in1=yf.ap[:, sl], op=ALU.add),
                        reads=[py[half].b, yf.b], writes=[gbuf.b])
                if dbg:
                    P.dma("sp", lambda e: e.dma_start(out=dbg_y[tok0:tok0 + 128, :], in_=gbuf.ap), sz.b, reads=[gbuf.b])
                P.op("pool", lambda e: e.tensor_tensor(out=gbuf.ap, in0=gbuf.ap, in1=sz.ap, op=ALU.mult),
                     reads=[gbuf.b, sz.b], writes=[gbuf.b])

            def stage_C2(it, ch):
                tok0 = ch * 128
                b2 = it % 2
                seg = seg_of_chunk(ch)
                if seg != state["seg"]:
                    load_bc(seg, 0)
                    state["seg"] = seg
                cv, xr, gbuf, xo_ = cvl[b2], xres[b2], gbuf_t[b2], xo[b2]
                residual_pre(xr)
                for g in range(4):
                    P.op("act", lambda e, g=g: e.activation(out=gsq.ap[:, g * 256:(g + 1) * 256], in_=gbuf.ap[:, g * 256:(g + 1) * 256],
                                                            func=AF.Square, accum_out=ssq.ap[:, g:g + 1]),
                         reads=[gbuf.b], writes=[gsq.b, ssq.b])
                rstd_from(ssq.ap, ssq, epsrms, [ssq.b], scale=1.0 / 256.0)
                yield
                P.op("dve", lambda e: e.tensor_tensor(
                    out=gnb.ap.rearrange("p (g q) -> p g q", q=256), in0=gbuf.ap.rearrange("p (g q) -> p g q", q=256),
                    in1=ssq.ap.unsqueeze(2).broadcast_to([128, 4, 256]), op=ALU.mult), reads=[gbuf.b, ssq.b], writes=[gnb.b])
                yield
                po = [py[0], py[1]]
                for k in range(8):
                    P.op("pe", lambda e, k=k: e.transpose(out=psb(pti)[:, k * 128:(k + 1) * 128],
                                                          in_=gnb.ap[:, k * 128:(k + 1) * 128], identity=identb.ap),
                         reads=[gnb.b, identb.b], writes=[pt_b.b])
                yield
                for k in range(8):
                    P.op("act", lambda e, k=k: e.activation(out=gT.ap[:, k, :], in_=psb(pti)[:, k * 128:(k + 1) * 128],
                                                            func=AF.Identity, scale=pf.ap[:, onw + k:onw + k + 1]),
                         reads=[pt_b.b, pf.b], writes=[gT.b])
                yield
                for k in range(12):
                    for half in range(2):
                        lhs = cv.ap[:, k, :] if k < 4 else gT.ap[:, k - 4, :]
                        lb = cv.b if k < 4 else gT.b
                        P.op("pe", lambda e, half=half, k=k, lhs=lhs: e.matmul(
                            po[half].ap, lhsT=lhs, rhs=w_out.ap[:, k, half * 512:(half + 1) * 512],
                            start=(k == 0), stop=(k == 11), skip_group_check=True), reads=[lb, w_out.b], writes=[po[half].b])
                yield
                residual_ln(po, xr, xo_, pre=False)
                P.dma("act", lambda e: e.dma_start(out=sA[tok0:tok0 + 128, :], in_=xo_.ap), xo_.b, reads=[xo_.b])
                if dbg:
                    P.dma("sp", lambda e: e.dma_start(out=dbg_r[tok0:tok0 + 128, :], in_=rr.ap), xo_.b, reads=[rr.b])

            n = len(order)

            def run(g):
                for _ in g:
                    pass

            def step(g):
                if g is not None:
                    next(g, None)

            for i0 in range(min(3, n)):
                stage_A_load(i0, order[i0])
            run(stage_A(0, order[0]))
            if n > 1:
                run(stage_A(1, order[1]))
            for it in range(n):
                gens = [stage_B(it, order[it])]
                if it + 2 < n:
                    gens.append(stage_A(it + 2, order[it + 2]))
                if d == 1 and it >= 1 and wanty(order[it - 1]):
                    gens.append(stage_C2(it - 1, order[it - 1]))
                while gens:
                    for g in list(gens):
                        try:
                            next(g)
                        except StopIteration:
                            gens.remove(g)
                if it + 3 < n:
                    stage_A_load(it + 3, order[it + 3])
            if d == 1 and wanty(order[n - 1]):
                run(stage_C2(n - 1, order[n - 1]))

        P.barrier()
        P.release([t.b for t in xs_l + bt_l + BT_l + CT_l + yfs + szl + cvl + xres + xo] + [w_out.b, agb.b, abb.b, g1b.b])
        aoff[0] = layer_mark
        w1 = alloc([8, D_FF], BF16, "w1")
        w13 = W["w1"][L].rearrange("(k p) n -> p k n", p=128)
        w1_pc = [P.buf("w1_pc%d" % i, sw=True) for i in range(8)]
        for wpc in range(8):
            P.dma("pool", lambda e, wpc=wpc: e.dma_start(out=w1.ap[:, :, wpc * 512:(wpc + 1) * 512],
                                                         in_=w13[:, :, wpc * 512:(wpc + 1) * 512]), w1_pc[wpc], writes=[w1_pc[wpc]])
        w2 = alloc([32, D], BF16, "w2")
        w23 = W["w2"][L].rearrange("(k p) n -> p k n", p=128)
        w2_pc = [P.buf("w2_pc%d" % i, sw=True) for i in range(8)]
        for wpc in range(8):
            for hf in range(2):
                P.dma("pool", lambda e, wpc=wpc, hf=hf: e.dma_start(out=w2.ap[:, wpc * 4:(wpc + 1) * 4, hf * 512:(hf + 1) * 512],
                                                                   in_=w23[:, wpc * 4:(wpc + 1) * 4, hf * 512:(hf + 1) * 512]),
                      w2_pc[wpc], writes=[w2_pc[wpc]])
        agb = alloc([D], F32, "agb3", dma=True)
        abb = alloc([D], F32, "abb3", dma=True)
        g1b = alloc([D], F32, "g1b3", dma=True)
        xin3 = [alloc([2, D], F32, "xin3_%d" % i, dma=True) for i in range(2)]
        h2T = [alloc([8, TB], BF16, "h2T0")] * 2
        hid = alloc([32, TB], BF16, "hid")
        rtmp = [alloc([TB], F32, "rtmp%d" % i) for i in range(2)]
        rr = alloc([D], F32, "rr3")
        xo3 = [alloc([D], F32, "xo3_0", dma=True)] * 2
        st12 = alloc([12], F32, "st12_3")
        mv = alloc([2], F32, "mv3")
        rs = alloc([1], F32, "rs3")
        fin = None
        if last:
            fg = alloc([D], F32, "fg", dma=True)
            fb = alloc([D], F32, "fb", dma=True)
            o1_, o2_ = PROW_OFF["ln2g"][0], PROW_OFF["ln2b"][0]
            P.dma("sp", lambda e: e.dma_start(out=fg.ap, in_=prow_in[L][:, o1_:o1_ + D].partition_broadcast(128)), fg.b,
                  writes=[fg.b])
            P.dma("sp", lambda e: e.dma_start(out=fb.ap, in_=prow_in[L][:, o2_:o2_ + D].partition_broadcast(128)), fb.b,
                  writes=[fb.b])
            fin = (fg, fb)
        cur_seg = -1
        blk0 = (CTX // TB) if last else 0
        for bi in range(blk0, nblk):
            tok0 = bi * TB
            seg = 0 if tok0 < CTX else 1
            if seg != cur_seg:
                load_bc(seg, 1)
                cur_seg = seg
            xi, h = xin3[bi % 2], h2T[bi % 2]
            for nb in ([blk0, blk0 + 1] if bi == blk0 else [bi + 1]):
                if nb < nblk:
                    xn = xin3[nb % 2]
                    P.dma("sp", lambda e, xn=xn, nb=nb: e.dma_start(
                        out=xn.ap, in_=sA[nb * TB:(nb + 1) * TB, :].rearrange("(j p) d -> p j d", p=128)), xn.b, writes=[xn.b])
            for k in range(8):
                pb = nextps()
                for j in range(2):
                    P.op("pe", lambda e, pb=pb, xi=xi, j=j, k=k: e.transpose(
                        out=pb.ap[:, j * 128:(j + 1) * 128], in_=xi.ap[:, j, k * 128:(k + 1) * 128], identity=ident.ap),
                        reads=[xi.b, ident.b], writes=[pb.b])
                P.op("act", lambda e, pb=pb, h=h, k=k, seg=seg: e.activation(
                    out=h.ap[:, k, :], in_=pb.ap[:, 0:TB], func=AF.Identity,
                    scale=AB.ap[:, 1, k, seg:seg + 1], bias=AB.ap[:, 1, k, 2 + seg:3 + seg]),
                    reads=[pb.b, AB.b], writes=[h.b])
            for f in range(32):
                pb = nextps()
                for k in range(8):
                    P.op("pe", lambda e, pb=pb, k=k, f=f, h=h: e.matmul(
                        pb.ap[:, 0:TB], lhsT=w1.ap[:, k, f * 128:(f + 1) * 128], rhs=h.ap[:, k, :],
                        start=(k == 0), stop=(k == 7)), reads=[w1_pc[f // 4], h.b], writes=[pb.b])
                rt = rtmp[f % 2]
                if f % 2 == 0:
                    P.op("act", lambda e, pb=pb, rt=rt: e.activation(out=rt.ap, in_=pb.ap[:, 0:TB], func=AF.Relu),
                         reads=[pb.b], writes=[rt.b])
                else:
                    P.op("dve", lambda e, pb=pb, rt=rt: e.tensor_scalar(out=rt.ap, in0=pb.ap[:, 0:TB], scalar1=0.0, scalar2=None,
                                                                        op0=ALU.max), reads=[pb.b], writes=[rt.b])
                P.op("pool", lambda e, f=f, rt=rt: e.tensor_tensor(out=hid.ap[:, f, :], in0=rt.ap, in1=rt.ap, op=ALU.mult),
                     reads=[rt.b], writes=[hid.b])
            for j in range(2):
                po = [nextps(), nextps()]
                for half in range(2):
                    for f in range(32):
                        P.op("pe", lambda e, half=half, f=f, j=j, po=po: e.matmul(
                            po[half].ap, lhsT=hid.ap[:, f, j * 128:(j + 1) * 128], rhs=w2.ap[:, f, half * 512:(half + 1) * 512],
                            start=(f == 0), stop=(f == 31)), reads=[hid.b, w2_pc[f // 4]], writes=[po[half].b])
                xo_ = xo3[j]
                xr = Tl(xi.ap[:, j, :], xi.b)
                residual_ln(po, xr, xo_, final=fin)
                t0_ = tok0 + j * 128
                if last:
                    P.dma("sp", lambda e, xo_=xo_, t0_=t0_: e.dma_start(out=y_out[t0_ - CTX:t0_ - CTX + 128, :], in_=xo_.ap),
                          xo_.b, reads=[xo_.b])
                else:
                    P.dma("sp", lambda e, xo_=xo_, t0_=t0_: e.dma_start(out=sB[t0_:t0_ + 128, :], in_=xo_.ap), xo_.b,
                          reads=[xo_.b])
        P.barrier()
        rel = w1_pc + w2_pc + [agb.b, abb.b, g1b.b, pf.b, hrow.b] + [t.b for t in xin3 + xo3[:1]]
        if last:
            rel += [fin[0].b, fin[1].b]
        P.release(rel)


_CACHE = {}


def kernel(**inp):
    inp = {k: np.asarray(v) for k, v in inp.items()}
    B, SEQ, _ = inp["x"].shape
    CTX = inp["ctx"].shape[1]
    depth = inp["w_in"].shape[0]
    key = (SEQ, CTX, depth)
    if key not in _CACHE:
        _CACHE[key] = build(SEQ, CTX, depth)
    nc, _P = _CACHE[key]
    packs = [pack_params(inp, L) for L in range(depth)]
    pfm = np.stack([p[0] for p in packs]).astype(np.float32)
    prow = np.stack([p[1] for p in packs]).astype(np.float32)
    shared = {k: np.ascontiguousarray(inp[k], dtype=np.float32) for k in ("w_mod", "w_in", "w_out", "w1", "w2")}
    in_maps = []
    for b in range(B):
        cf = np.stack([_fm(inp["c_ctx"]), _fm(inp["c"][b])], axis=-1).reshape(128, 16).astype(np.float32)
        m = {"x": np.ascontiguousarray(inp["x"][b], dtype=np.float32),
             "ctx": np.ascontiguousarray(inp["ctx"][b], dtype=np.float32),
             "c_fm": np.ascontiguousarray(cf), "pfm": pfm, "prow": prow}
        m.update(shared)
        in_maps.append(m)
    res = run_bass_kernel_spmd(nc, in_maps, core_ids=list(range(B)))
    return np.stack([np.asarray(r["y"], dtype=np.float32) for r in res.results], axis=0)
```

```python
import contextlib
import numpy as np
import concourse.bass as bass
import concourse.mybir as mybir
from concourse.bass_utils import run_bass_kernel_spmd

F32 = mybir.dt.float32
BF16 = mybir.dt.bfloat16
AF = mybir.ActivationFunctionType
ALU = mybir.AluOpType

D = 1024
DEPTH = 2
GRID_W = 64
CONV_DIM = 512
CONV_K = 31
D_INNER = 1024
HEADS = 16
HP = 64
GROUPS = 4
NST = 128
SK = 5
D_XBC = 2048
D_PROJ = 4112
D_FF = 4096
ALPHA = float((2 * DEPTH) ** 0.25)
LN_EPS = 1e-5
RMS_EPS = 1e-5
NEG = -30000.0

ENGS = ("pe", "act", "dve", "pool", "sp")
import os
MENG = os.environ.get("K_MENG", "pool")
NDIR = int(os.environ.get("K_NDIR", "2"))
NOC2 = int(os.environ.get("K_NOC2", "0"))


class Buf:
    __slots__ = ("name", "w", "r", "sem", "excl")

    def __init__(self, name):
        self.name = name
        self.w = None
        self.r = []
        self.sem = None
        self.excl = False


class Op:
    __slots__ = ("eng", "fn", "deps", "is_dma", "sem", "semval", "sig", "sigval", "idx", "eidx", "waits")

    def __init__(self, eng, fn):
        self.eng = eng
        self.fn = fn
        self.deps = []
        self.is_dma = False
        self.sem = None
        self.semval = 0
        self.sig = False
        self.sigval = 0
        self.idx = 0
        self.eidx = 0
        self.waits = []


class Prog:
    def __init__(self, nc, n_hw_sems=60, n_sw_sems=28, plan=None):
        self.nc = nc
        self.plan = plan
        self.ops = []
        self.eng_ops = {e: [] for e in ENGS}
        n_dma_sems = n_hw_sems + n_sw_sems
        self.n_dma_sems = n_dma_sems
        self.dma_cnt = [0] * n_dma_sems
        self.dma_free = list(range(n_hw_sems))
        self.sw_free = list(range(n_hw_sems, n_dma_sems))
        self.sw_map = {}
        self.all_bufs = []
        self.n = 0
        if plan is not None:
            self.engobj = {"pe": nc.tensor, "act": nc.scalar, "dve": nc.vector, "pool": nc.gpsimd, "sp": nc.sync}
            self.stack = contextlib.ExitStack()
            self.esem = {e: self.stack.enter_context(nc.semaphore("s_" + e)) for e in ENGS}
            self.dsem = [self.stack.enter_context(nc.semaphore("d%d" % i)) for i in range(n_dma_sems)]

    def buf(self, name, dma=False, sw=False):
        b = Buf(name)
        if sw:
            if name not in self.sw_map:
                assert self.sw_free, "out of sw dma semaphores"
                self.sw_map[name] = self.sw_free.pop(0)
            b.sem = self.sw_map[name]
        elif dma:
            assert self.dma_free, "out of dma semaphores"
            b.sem = self.dma_free.pop(0)
        if self.plan is None:
            self.all_bufs.append(b)
        return b

    def release(self, bufs):
        for b in bufs:
            if b.sem is not None:
                if b.sem not in self.sw_map.values():
                    self.dma_free.append(b.sem)
                b.sem = None

    def _emit(self, eng_name, fn, is_dma):
        rec = self.plan.ops[self.n]
        self.n += 1
        assert rec.eng == eng_name and rec.is_dma == is_dma, (rec.eng, eng_name)
        eng = self.engobj[eng_name]
        for kind, d in rec.waits:
            if kind == "dma":
                eng.wait_ge(self.dsem[d.sem], d.semval)
            else:
                eng.wait_ge(self.esem[d.eng], d.sigval)
        if fn is None:
            if rec.sig:
                eng.nop().then_inc(self.esem[eng_name], 1)
            return
        ins = fn(eng)
        if is_dma:
            ins.then_inc(self.dsem[rec.sem], 16)
        elif rec.sig:
            ins.then_inc(self.esem[eng_name], 1)

    def _add(self, op, reads, writes):
        deps = []
        for b in reads:
            if b.w is not None:
                deps.append(b.w)
            if b.excl:
                deps.extend(r for r in b.r if r.eng != op.eng)
        for b in writes:
            if b.w is not None:
                deps.append(b.w)
            deps.extend(b.r)
        for b in reads:
            b.r.append(op)
        for b in writes:
            b.w = op
            b.r = []
        seen = set()
        for d in deps:
            if d is op or id(d) in seen:
                continue
            seen.add(id(d))
            op.deps.append(d)
        op.idx = len(self.ops)
        op.eidx = len(self.eng_ops[op.eng])
        self.ops.append(op)
        self.eng_ops[op.eng].append(op)
        return op

    def op(self, eng, fn, reads=(), writes=()):
        if self.plan is not None:
            return self._emit(eng, fn, False)
        return self._add(Op(eng, None), reads, writes)

    def dma(self, eng, fn, sbuf, reads=(), writes=()):
        if self.plan is not None:
            return self._emit(eng, fn, True)
        o = Op(eng, None)
        o.is_dma = True
        assert sbuf.sem is not None, sbuf.name
        o.sem = sbuf.sem
        self.dma_cnt[o.sem] += 16
        o.semval = self.dma_cnt[o.sem]
        return self._add(o, reads, writes)

    def barrier(self):
        if self.plan is not None:
            for e in ENGS:
                self._emit(e, None, False)
            return
        last = [ops[-1] for ops in self.eng_ops.values() if ops]
        latest = {}
        for o in self.ops:
            if o.is_dma:
                latest[o.sem] = o
        deps = last + list(latest.values())
        for e in ENGS:
            o = Op(e, None)
            o.deps = list(deps)
            o.idx = len(self.ops)
            o.eidx = len(self.eng_ops[e])
            self.ops.append(o)
            self.eng_ops[e].append(o)
        for b in self.all_bufs:
            b.w = None
            b.r = []

    def analyze(self):
        seen_eng = {e: {f: 0 for f in ENGS} for e in ENGS}
        seen_dma = {e: [0] * self.n_dma_sems for e in ENGS}
        for o in self.ops:
            e = o.eng
            for d in o.deps:
                if d.is_dma:
                    if seen_dma[e][d.sem] >= d.semval:
                        continue
                    seen_dma[e][d.sem] = d.semval
                    o.waits.append(("dma", d))
                else:
                    if d.eng == e and e != "pool" and (o.eidx - d.eidx > 2 or e == "pe" or e == "sp"):
                        continue
                    if seen_eng[e][d.eng] >= d.eidx + 1:
                        continue
                    seen_eng[e][d.eng] = d.eidx + 1
                    d.sig = True
                    o.waits.append(("eng", d))
            o.deps = None
        cnt = {e: 0 for e in ENGS}
        for o in self.ops:
            if o.sig:
                cnt[o.eng] += 1
                o.sigval = cnt[o.eng]
        self.sig_counts = cnt

    def finish(self):
        assert self.n == len(self.plan.ops), (self.n, len(self.plan.ops))
        for e in ENGS:
            eng = self.engobj[e]
            for i in range(self.n_dma_sems):
                if self.plan.dma_cnt[i] > 0:
                    eng.wait_ge(self.dsem[i], self.plan.dma_cnt[i])
        self.stack.close()


class Tl:
    __slots__ = ("ap", "b")

    def __init__(self, ap, b):
        self.ap = ap
        self.b = b


PFM_OFF = {}
_o = 0
for _n, _w in (("bmod", 48), ("convb", 4), ("lng", 4), ("lnb", 4), ("sconvb", 16), ("normw", 8),
               ("ln1g", 8), ("ln1b", 8), ("ln2g", 8), ("ln2b", 8), ("convw", 4 * CONV_K), ("sconvw", 16 * SK)):
    PFM_OFF[_n] = (_o, _w)
    _o += _w
NPF = _o
PROW_OFF = {}
_o = 0
for _n, _w in (("bmod_g1", 1024), ("bmod_g2", 1024), ("convb", 512), ("sconvb", 1536), ("ln1g", 1024), ("ln1b", 1024),
               ("ln2g", 1024), ("ln2b", 1024), ("dtb", 32), ("alog", 32), ("dskip", 32)):
    PROW_OFF[_n] = (_o, _w)
    _o += _w
NPR = _o


def _fm(v):
    return np.ascontiguousarray(v.reshape(-1, 128).T)


def pack_params(inp, L):
    pf = np.zeros((128, NPF), np.float32)

    def put(name, arr):
        o, w = PFM_OFF[name]
        assert arr.shape == (128, w), (name, arr.shape)
        pf[:, o:o + w] = arr
    put("bmod", _fm(inp["b_mod"][L]))
    put("convb", _fm(inp["conv_b"][L]))
    put("lng", _fm(inp["conv_ln_g"][L]))
    put("lnb", _fm(inp["conv_ln_b"][L]))
    put("sconvb", _fm(inp["ssm_conv_b"][L]))
    put("normw", _fm(inp["ssm_norm_w"][L]))
    put("ln1g", _fm(inp["ln1_g"][L]))
    put("ln1b", _fm(inp["ln1_b"][L]))
    put("ln2g", _fm(inp["ln2_g"][L]))
    put("ln2b", _fm(inp["ln2_b"][L]))
    cw = inp["conv_w"][L]
    put("convw", np.ascontiguousarray(cw.reshape(CONV_K, 4, 128).transpose(2, 1, 0)).reshape(128, 4 * CONV_K))
    sw = inp["ssm_conv_w"][L]
    put("sconvw", np.ascontiguousarray(sw.reshape(SK, 16, 128).transpose(2, 1, 0)).reshape(128, 16 * SK))
    pr = np.zeros((1, NPR), np.float32)

    def putr(name, arr):
        o, w = PROW_OFF[name]
        pr[0, o:o + w] = arr.reshape(-1)
    putr("bmod_g1", inp["b_mod"][L][2048:3072])
    putr("bmod_g2", inp["b_mod"][L][5120:6144])
    putr("convb", inp["conv_b"][L])
    putr("sconvb", inp["ssm_conv_b"][L][:1536])
    putr("ln1g", inp["ln1_g"][L])
    putr("ln1b", inp["ln1_b"][L])
    putr("ln2g", inp["ln2_g"][L])
    putr("ln2b", inp["ln2_b"][L])
    putr("dtb", inp["dt_bias"][L])
    putr("alog", inp["a_log"][L])
    putr("dskip", inp["d_skip"][L])
    return pf, pr


def build(SEQ, CTX, depth=DEPTH, dbg=False):
    nc = bass.Bass("TRN2", target_bir_lowering=False)
    TALL = CTX + SEQ
    NCC = CTX // 128
    NLC = SEQ // 128
    NCH = NCC + NLC
    TB = 256
    assert CTX % TB == 0 and SEQ % TB == 0

    def dram_in(name, shape, dt=F32):
        return nc.dram_tensor(name, list(shape), dt, kind="ExternalInput").ap()

    def dram_tmp(name, shape, dt=F32):
        if dbg:
            return nc.dram_tensor(name, list(shape), dt, kind="ExternalOutput").ap()
        return nc.dram_tensor(name, list(shape), dt).ap()

    x_in = dram_in("x", [SEQ, D])
    ctx_in = dram_in("ctx", [CTX, D])
    c_fm = dram_in("c_fm", [128, 16])
    W = {}
    for nm, shp in (("w_mod", [depth, D, 6 * D]), ("w_in", [depth, D, D_PROJ]), ("w_out", [depth, 1536, D]),
                    ("w1", [depth, D, D_FF]), ("w2", [depth, D_FF, D])):
        W[nm] = dram_in(nm, shp)
    pfm_in = dram_in("pfm", [depth, 128, NPF])
    prow_in = dram_in("prow", [depth, 1, NPR])
    y_out = nc.dram_tensor("y", [SEQ, D], F32, kind="ExternalOutput").ap()

    sA = dram_tmp("sA", [TALL, D])
    sB = dram_tmp("sB", [TALL, D])
    TP = TALL + 8
    xbcT = dram_tmp("xbcT", [D_XBC, TP], BF16)
    convT_d = dram_tmp("convT", [CONV_DIM, TALL], BF16)
    siluz_d = dram_tmp("siluz", [TALL, D_INNER])
    xs_d = dram_tmp("xs_tok", [TALL, D_INNER], BF16)
    bt_d = dram_tmp("b_tok", [TALL, 512], BF16)
    BT_d = dram_tmp("BT", [512, TALL], BF16)
    CT_d = dram_tmp("CT", [512, TALL], BF16)
    yf_d = dram_tmp("y_f", [TALL, D_INNER])
    g_d = dram_tmp("grow", [2, 2, D])
    dbg_y = dram_tmp("dbg_y", [TALL, D_INNER]) if dbg else None
    dbg_r = dram_tmp("dbg_r", [TALL, D]) if dbg else None

    def seg_of_chunk(ch):
        return 0 if ch < NCC else 1

    def xbc_col(tok):
        return tok + 2 if tok < CTX else tok + 6

    ARENA_W = 53000
    arena = nc.alloc_sbuf_tensor("arena", [128, ARENA_W], F32)
    psum_t = [nc.alloc_psum_tensor("ps%d" % i, [128, 512], F32) for i in range(8)]
    plan = Prog(nc)
    program(nc, plan, locals())
    plan.analyze()
    em = Prog(nc, plan=plan)
    program(nc, em, locals())
    em.finish()
    return nc, plan


def program(nc, P, env):
    SEQ, CTX, depth, TALL, NCC, NLC, NCH, TB, TP, ARENA_W = (env[k] for k in (
        "SEQ", "CTX", "depth", "TALL", "NCC", "NLC", "NCH", "TB", "TP", "ARENA_W"))
    x_in, ctx_in, c_fm, W, pfm_in, prow_in, y_out = (env[k] for k in (
        "x_in", "ctx_in", "c_fm", "W", "pfm_in", "prow_in", "y_out"))
    sA, sB, xbcT, convT_d, siluz_d, xs_d, bt_d, BT_d, CT_d, yf_d, g_d = (env[k] for k in (
        "sA", "sB", "xbcT", "convT_d", "siluz_d", "xs_d", "bt_d", "BT_d", "CT_d", "yf_d", "g_d"))
    arena, psum_t, seg_of_chunk, xbc_col = env["arena"], env["psum_t"], env["seg_of_chunk"], env["xbc_col"]
    dbg, dbg_y, dbg_r = env["dbg"], env["dbg_y"], env["dbg_r"]
    aoff = [0]

    def alloc(shape, dt=F32, name="t", dma=False, parts=128):
        n = int(np.prod(shape))
        nw = n if dt == F32 else (n + 1) // 2
        nw = (nw + 7) // 8 * 8
        assert aoff[0] + nw <= ARENA_W, ("arena overflow", name, aoff[0], nw)
        v = arena[0:parts, aoff[0]:aoff[0] + nw]
        aoff[0] += nw
        if dt != F32:
            v = v.bitcast(dt)
        v = v[:, 0:n]
        if len(shape) == 2:
            v = v.rearrange("p (a b) -> p a b", a=shape[0])
        elif len(shape) == 3:
            v = v.rearrange("p (a b c) -> p a b c", a=shape[0], b=shape[1])
        return Tl(v, P.buf(name, dma=dma))

    psum = [Tl(psum_t[i][:, :], P.buf("ps%d" % i)) for i in range(8)]
    for t in psum:
        t.b.excl = True

    def psb(i):
        return psum[i].ap.bitcast(BF16)

    ident = alloc([128], F32, "ident")
    identb = alloc([128], BF16, "identb")
    triF = alloc([128], F32, "triF")
    triB = alloc([128], F32, "triB")
    ntriF = alloc([128], F32, "ntriF")
    ntriB = alloc([128], F32, "ntriB")
    maskF = alloc([4, 128], BF16, "maskF")
    maskB = alloc([4, 128], BF16, "maskB")
    onesb = alloc([128], BF16, "onesb")
    epsln = alloc([1], F32, "epsln")
    epsrms = alloc([1], F32, "epsrms")
    one_t = alloc([1], F32, "one_t")
    dt_sb = alloc([NCH, 2, HEADS], F32, "dt_sb")
    C0 = [ident, identb, triF, triB, ntriF, ntriB, maskF, maskB, onesb, epsln, epsrms, one_t]

    def pool_fill(t, ap, val):
        P.op("pool", lambda e, ap=ap, val=val: e.memset(ap, val), writes=[t.b])

    def pool_sel(t, ap, pattern, cmp, fill, base, cm):
        P.op("pool", lambda e: e.affine_select(out=ap, in_=ap, pattern=pattern, compare_op=cmp, fill=fill,
                                               base=base, channel_multiplier=cm), reads=[t.b], writes=[t.b])

    pool_fill(ident, ident.ap, 0.0)
    pool_sel(ident, ident.ap, [[-1, 128]], ALU.not_equal, 1.0, 0, 1)
    P.op("pool", lambda e: e.tensor_copy(out=identb.ap, in_=ident.ap), reads=[ident.b], writes=[identb.b])
    pool_fill(triF, triF.ap, 1.0)
    pool_sel(triF, triF.ap, [[1, 128]], ALU.is_ge, 0.0, 0, -1)
    pool_fill(triB, triB.ap, 1.0)
    pool_sel(triB, triB.ap, [[-1, 128]], ALU.is_ge, 0.0, 0, 1)
    P.op("pool", lambda e: e.tensor_scalar(out=ntriF.ap, in0=triF.ap, scalar1=-1.0, scalar2=None, op0=ALU.mult),
         reads=[triF.b], writes=[ntriF.b])
    P.op("pool", lambda e: e.tensor_scalar(out=ntriB.ap, in0=triB.ap, scalar1=-1.0, scalar2=None, op0=ALU.mult),
         reads=[triB.b], writes=[ntriB.b])
    pool_fill(maskF, maskF.ap, 0.0)
    pool_sel(maskF, maskF.ap, [[0, 4], [1, 128]], ALU.is_ge, NEG, 0, -1)
    pool_fill(maskB, maskB.ap, 0.0)
    pool_sel(maskB, maskB.ap, [[0, 4], [-1, 128]], ALU.is_ge, NEG, 0, 1)
    pool_fill(onesb, onesb.ap, 1.0)
    pool_fill(epsln, epsln.ap, LN_EPS)
    pool_fill(epsrms, epsrms.ap, RMS_EPS)
    pool_fill(one_t, one_t.ap, 1.0)

    zt = alloc([16, 8], BF16, "zt", dma=True)
    pool_fill(zt, zt.ap, 0.0)
    xb3 = xbcT.rearrange("(c p) t -> p c t", p=128)
    for c0 in (0, CTX + 2, CTX + 4, TP - 2):
        P.dma("sp", lambda e, c0=c0: e.dma_start(out=xb3[:, :, c0:c0 + 2], in_=zt.ap[:, :, 0:2]), zt.b, reads=[zt.b])
    persist_mark = aoff[0]

    def rstd_from(var_ap, out_t, eps_t, reads, n=1, scale=1.0):
        P.op("act", lambda e: e.activation(out=out_t.ap, in_=var_ap, func=AF.Ln, bias=eps_t.ap[:, 0:1], scale=scale),
             reads=reads + [eps_t.b], writes=[out_t.b])
        P.op("act", lambda e: e.activation(out=out_t.ap, in_=out_t.ap, func=AF.Exp, scale=-0.5), reads=[out_t.b],
             writes=[out_t.b])

    for L in range(depth):
        last = (L == depth - 1)
        src = None if L == 0 else sB

        def stream_rows(tok0, n, src=src):
            if src is None:
                return ctx_in[tok0:tok0 + n, :] if tok0 < CTX else x_in[tok0 - CTX:tok0 - CTX + n, :]
            return src[tok0:tok0 + n, :]

        P.barrier()
        aoff[0] = persist_mark
        pf = alloc([NPF], F32, "pf", dma=True)
        P.dma("sp", lambda e, L=L: e.dma_start(out=pf.ap, in_=pfm_in[L]), pf.b, writes=[pf.b])
        AB = alloc([2, 8, 4], F32, "AB")
        hrow = alloc([3, 2, HEADS], F32, "hrow", dma=True)
        negA = alloc([2, HEADS], F32, "negA")
        layer_mark = aoff[0]

        def pfv(name):
            o, w = PFM_OFF[name]
            return pf.ap[:, o:o + w]
        prow2 = alloc([NPR], F32, "prow2", dma=True, parts=2)
        P.dma("sp", lambda e, L=L: e.dma_start(out=prow2.ap, in_=prow_in[L].partition_broadcast(2)), prow2.b,
              writes=[prow2.b])
        csil = alloc([8, 2], F32, "csil", dma=True)
        P.dma("sp", lambda e: e.dma_start(out=csil.ap, in_=c_fm.rearrange("p (k s) -> p k s", s=2)), csil.b,
              writes=[csil.b])
        P.op("act", lambda e: e.activation(out=csil.ap, in_=csil.ap, func=AF.Silu), reads=[csil.b], writes=[csil.b])
        modfm = alloc([48, 2], F32, "modfm")
        grow = alloc([2, D], F32, "grow", dma=True, parts=2)
        wm = [alloc([8, 1024], F32, "wm%d" % i, dma=True) for i in range(2)]
        wmod3 = W["w_mod"][L].rearrange("(k p) n -> p k n", p=128)
        for cb in range(6):
            t = wm[cb % 2]
            P.dma("sp", lambda e, t=t, cb=cb: e.dma_start(out=t.ap, in_=wmod3[:, :, cb * 1024:(cb + 1) * 1024]),
                  t.b, writes=[t.b])
            if cb in (2, 5):
                which = 0 if cb == 2 else 1
                for half in range(2):
                    pb = psum[half]
                    for k in range(8):
                        P.op("pe", lambda e, t=t, k=k, half=half, pb=pb: e.matmul(
                            pb.ap[0:2, :], lhsT=csil.ap[:, k, :], rhs=t.ap[:, k, half * 512:(half + 1) * 512],
                            start=(k == 0), stop=(k == 7)), reads=[t.b, csil.b], writes=[pb.b])
                    o, _w = PROW_OFF["bmod_g1" if which == 0 else "bmod_g2"]
                    P.op("dve", lambda e, pb=pb, half=half, which=which, o=o: e.tensor_tensor(
                        out=grow.ap[:, which, half * 512:(half + 1) * 512], in0=pb.ap[0:2, :],
                        in1=prow2.ap[:, o + half * 512:o + (half + 1) * 512], op=ALU.add),
                        reads=[pb.b, prow2.b], writes=[grow.b])
            else:
                pb = psum[2 + (cb % 2)]
                for j in range(8):
                    for k in range(8):
                        P.op("pe", lambda e, t=t, k=k, j=j, pb=pb: e.matmul(
                            pb.ap[:, j * 2:j * 2 + 2], lhsT=t.ap[:, k, j * 128:(j + 1) * 128], rhs=csil.ap[:, k, :],
                            start=(k == 0), stop=(k == 7)), reads=[t.b, csil.b], writes=[pb.b])
                bo = PFM_OFF["bmod"][0] + cb * 8
                P.op("dve", lambda e, pb=pb, cb=cb, bo=bo: e.tensor_tensor(
                    out=modfm.ap[:, cb * 8:(cb + 1) * 8, :],
                    in0=pb.ap[:, 0:16].rearrange("p (j s) -> p j s", s=2),
                    in1=pf.ap[:, bo:bo + 8].unsqueeze(2).broadcast_to([128, 8, 2]), op=ALU.add),
                    reads=[pb.b, pf.b], writes=[modfm.b])
        P.dma("sp", lambda e: e.dma_start(out=g_d.rearrange("s w d -> s (w d)"),
                                          in_=grow.ap.rearrange("p w d -> p (w d)")), grow.b, reads=[grow.b])
        gp = alloc([8], F32, "gp")
        bp = alloc([8], F32, "bp")
        if L == 0:
            P.op("dve", lambda e: e.memset(gp.ap, 1.0), writes=[gp.b])
            P.op("dve", lambda e: e.memset(bp.ap, 0.0), writes=[bp.b])
        else:
            pfp = alloc([NPF], F32, "pfp", dma=True)
            P.dma("sp", lambda e, L=L: e.dma_start(out=pfp.ap, in_=pfm_in[L - 1]), pfp.b, writes=[pfp.b])
            o1, o2 = PFM_OFF["ln2g"][0], PFM_OFF["ln2b"][0]
            P.op("dve", lambda e: e.tensor_copy(out=gp.ap, in_=pfp.ap[:, o1:o1 + 8]), reads=[pfp.b], writes=[gp.b])
            P.op("dve", lambda e: e.tensor_copy(out=bp.ap, in_=pfp.ap[:, o2:o2 + 8]), reads=[pfp.b], writes=[bp.b])
        tmp1 = alloc([8, 2], F32, "tmp1")
        for stg in range(2):
            gsrc = gp.ap if stg == 0 else pfv("ln1g")
            bsrc = bp.ap if stg == 0 else pfv("ln1b")
            gb = [gp.b, bp.b] if stg == 0 else [pf.b]
            shc, scc = (0, 1) if stg == 0 else (3, 4)
            P.op("dve", lambda e, scc=scc: e.tensor_scalar(out=tmp1.ap, in0=modfm.ap[:, scc * 8:(scc + 1) * 8, :],
                                                           scalar1=1.0, scalar2=None, op0=ALU.add),
                 reads=[modfm.b], writes=[tmp1.b])
            P.op("dve", lambda e, stg=stg, gsrc=gsrc: e.tensor_tensor(
                out=AB.ap[:, stg, :, 0:2], in0=tmp1.ap, in1=gsrc.unsqueeze(2).broadcast_to([128, 8, 2]), op=ALU.mult),
                reads=[tmp1.b] + gb, writes=[AB.b])
            P.op("dve", lambda e, bsrc=bsrc: e.tensor_tensor(
                out=tmp1.ap, in0=tmp1.ap, in1=bsrc.unsqueeze(2).broadcast_to([128, 8, 2]), op=ALU.mult),
                reads=[tmp1.b] + gb, writes=[tmp1.b])
            P.op("dve", lambda e, stg=stg, shc=shc: e.tensor_tensor(
                out=AB.ap[:, stg, :, 2:4], in0=tmp1.ap, in1=modfm.ap[:, shc * 8:(shc + 1) * 8, :], op=ALU.add),
                reads=[tmp1.b, modfm.b], writes=[AB.b])
        o = PROW_OFF["dtb"][0]
        P.dma("sp", lambda e, L=L, o=o: e.dma_start(
            out=hrow.ap.rearrange("p a b c -> p (a b c)"), in_=prow_in[L][:, o:o + 96].partition_broadcast(128)),
            hrow.b, writes=[hrow.b])
        P.op("act", lambda e: e.activation(out=negA.ap, in_=hrow.ap[:, 1], func=AF.Exp), reads=[hrow.b], writes=[negA.b])
        P.op("dve", lambda e: e.tensor_scalar(out=negA.ap, in0=negA.ap, scalar1=-1.0, scalar2=None, op0=ALU.mult),
             reads=[negA.b], writes=[negA.b])

        P.barrier()
        aoff[0] = layer_mark
        w_in = alloc([8, D_PROJ], BF16, "w_in")
        win3 = W["w_in"][L].rearrange("(k p) n -> p k n", p=128)
        win_pc = [P.buf("w_in_pc%d" % i, sw=True) for i in range(9)]
        for wpc in (1, 0, 4, 5, 6, 7, 2, 3, 8):
            c0, c1 = wpc * 512, min((wpc + 1) * 512, D_PROJ)
            P.dma("pool", lambda e, c0=c0, c1=c1: e.dma_start(out=w_in.ap[:, :, c0:c1], in_=win3[:, :, c0:c1]),
                  win_pc[wpc], writes=[win_pc[wpc]])
        d31 = alloc([4, CONV_K, 128], BF16, "d31")
        o = PFM_OFF["convw"][0]
        for cc in range(4):
            P.op("dve", lambda e, cc=cc, o=o: e.tensor_tensor(
                out=d31.ap[:, cc], in0=ident.ap.unsqueeze(1).broadcast_to([128, CONV_K, 128]),
                in1=pf.ap[:, o + cc * CONV_K:o + (cc + 1) * CONV_K].unsqueeze(2).broadcast_to([128, CONV_K, 128]),
                op=ALU.mult), reads=[ident.b, pf.b], writes=[d31.b])
        convb_r = alloc([512], BF16, "convb_r", parts=1)
        cb_st = alloc([512], F32, "cb_st", dma=True, parts=1)
        o = PROW_OFF["convb"][0]
        P.dma("sp", lambda e, L=L, o=o: e.dma_start(out=cb_st.ap, in_=prow_in[L][:, o:o + 512]), cb_st.b, writes=[cb_st.b])
        P.op("dve", lambda e: e.tensor_copy(out=convb_r.ap, in_=cb_st.ap), reads=[cb_st.b], writes=[convb_r.b])
        dtb_bc = hrow.ap[:, 0]
        xin = [alloc([2, D], F32, "xin%d" % i, dma=True) for i in range(2)]
        hT = [alloc([8, TB], BF16, "hT%d" % i) for i in range(2)]
        vpad = [alloc([4, 2, 188], BF16, "vpad%d" % i) for i in range(2)]
        for t in vpad:
            pool_fill(t, t.ap, 0.0)
        sig = [alloc([TB], F32, "sig%d" % i) for i in range(2)]
        xbc_ev = [alloc([16, TB], BF16, "xbc_ev%d" % i, dma=True) for i in range(2)]
        yhat = [alloc([512], BF16, "yhat%d" % i) for i in range(2)]
        cT_ev = [alloc([4, TB], BF16, "cT_ev%d" % i, dma=True) for i in range(2)]
        sz_ev = [alloc([2, D_INNER], F32, "sz_ev%d" % i, dma=True) for i in range(2)]
        st6 = alloc([6], F32, "st6")
        mv = alloc([2], F32, "mv")
        rs = alloc([1], F32, "rs")
        dtt = alloc([2, HEADS], F32, "dtt")
        cvT3 = convT_d.rearrange("(c p) t -> p c t", p=128)
        pc = [0]

        def nextps():
            pc[0] += 1
            return psum[pc[0] % 8]

        nblk = TALL // TB
        for bi in range(nblk):
            tok0 = bi * TB
            seg = 0 if tok0 < CTX else 1
            full = not (last and seg == 0)
            xi, h, vp, xe, ce, se = xin[bi % 2], hT[bi % 2], vpad[bi % 2], xbc_ev[bi % 2], cT_ev[bi % 2], sz_ev[bi % 2]
            for nb in ([0, 1] if bi == 0 else [bi + 1]):
                if nb < nblk:
                    xn = xin[nb % 2]
                    P.dma("sp", lambda e, xn=xn, nb=nb: e.dma_start(
                        out=xn.ap, in_=stream_rows(nb * TB, TB).rearrange("(j p) d -> p j d", p=128)), xn.b, writes=[xn.b])
            for k in range(8):
                pb = nextps()
                for j in range(2):
                    P.op("pe", lambda e, pb=pb, xi=xi, j=j, k=k: e.transpose(
                        out=pb.ap[:, j * 128:(j + 1) * 128], in_=xi.ap[:, j, k * 128:(k + 1) * 128], identity=ident.ap),
                        reads=[xi.b, ident.b], writes=[pb.b])
                P.op("act", lambda e, pb=pb, h=h, k=k, seg=seg: e.activation(
                    out=h.ap[:, k, :], in_=pb.ap[:, 0:TB], func=AF.Identity,
                    scale=AB.ap[:, 0, k, seg:seg + 1], bias=AB.ap[:, 0, k, 2 + seg:3 + seg]),
                    reads=[pb.b, AB.b], writes=[h.b])

            def ws_mm(col0, pb, h=h):
                for k in range(8):
                    P.op("pe", lambda e, k=k, pb=pb, col0=col0: e.matmul(
                        pb.ap[:, 0:TB], lhsT=w_in.ap[:, k, col0:col0 + 128], rhs=h.ap[:, k, :],
                        start=(k == 0), stop=(k == 7)), reads=[win_pc[col0 // 512], h.b], writes=[pb.b])
            if full:
                for cc in range(4):
                    sg = sig[cc % 2]
                    pg = nextps()
                    ws_mm(512 + cc * 128, pg)
                    P.op("act", lambda e, pg=pg, sg=sg: e.activation(out=sg.ap, in_=pg.ap[:, 0:TB], func=AF.Sigmoid),
                         reads=[pg.b], writes=[sg.b])
                    pa = nextps()
                    ws_mm(cc * 128, pa)
                    P.op("dve", lambda e, pa=pa, sg=sg, vp=vp, cc=cc: e.tensor_tensor(
                        out=vp.ap[:, cc, :, 30:158].rearrange("p j (w r) -> p j r w", r=2),
                        in0=pa.ap[:, 0:TB].rearrange("p (j r w) -> p j r w", j=2, r=2),
                        in1=sg.ap.rearrange("p (j r w) -> p j r w", j=2, r=2), op=ALU.mult),
                        reads=[pa.b, sg.b], writes=[vp.b])
            for cc in range(16):
                pb = nextps()
                ws_mm(2048 + cc * 128, pb)
                if cc % 2 == 0:
                    P.op("act", lambda e, pb=pb, xe=xe, cc=cc: e.activation(out=xe.ap[:, cc, :], in_=pb.ap[:, 0:TB],
                                                                          func=AF.Identity), reads=[pb.b], writes=[xe.b])
                else:
                    P.op("dve", lambda e, pb=pb, xe=xe, cc=cc: e.tensor_copy(out=xe.ap[:, cc, :], in_=pb.ap[:, 0:TB]),
                         reads=[pb.b], writes=[xe.b])
            col0 = xbc_col(tok0)
            P.dma("act", lambda e, xe=xe, col0=col0: e.dma_start(out=xb3[:, :, col0:col0 + TB], in_=xe.ap), xe.b,
                  reads=[xe.b])
            for j in range(2):
                ch = (tok0 // 128) + j
                if full:
                    pcv = nextps()
                    for cc in range(4):
                        P.op("pe", lambda e, pcv=pcv, cc=cc: e.matmul(
                            pcv.ap[:, cc * 128:(cc + 1) * 128], lhsT=onesb.ap[0:1, :],
                            rhs=convb_r.ap[0:1, cc * 128:(cc + 1) * 128], start=True, stop=False, skip_group_check=True),
                            reads=[onesb.b, convb_r.b], writes=[pcv.b])
                        for tp in range(CONV_K):
                            P.op("pe", lambda e, pcv=pcv, cc=cc, tp=tp, vp=vp, j=j: e.matmul(
                                pcv.ap[:, cc * 128:(cc + 1) * 128], lhsT=vp.ap[:, cc, j, 2 * tp:2 * tp + 128],
                                rhs=d31.ap[:, cc, tp, :], start=False, stop=(tp == CONV_K - 1), skip_group_check=True),
                                reads=[vp.b, d31.b], writes=[pcv.b])
                    yh = yhat[j]
                    P.op("dve", lambda e, pcv=pcv: e.bn_stats(out=st6.ap, in_=pcv.ap), reads=[pcv.b], writes=[st6.b])
                    P.op("dve", lambda e: e.bn_aggr(out=mv.ap, in_=st6.ap), reads=[st6.b], writes=[mv.b])
                    rstd_from(mv.ap[:, 1:2], rs, epsln, [mv.b])
                    P.op("dve", lambda e, pcv=pcv, yh=yh: e.tensor_scalar(
                        out=yh.ap, in0=pcv.ap, scalar1=mv.ap[:, 0:1], scalar2=rs.ap[:, 0:1], op0=ALU.subtract, op1=ALU.mult),
                        reads=[pcv.b, mv.b, rs.b], writes=[yh.b])
                    pt = nextps()
                    pti = psum.index(pt)
                    for cc in range(4):
                        P.op("pe", lambda e, pti=pti, yh=yh, cc=cc: e.transpose(
                            out=psb(pti)[:, cc * 128:(cc + 1) * 128], in_=yh.ap[:, cc * 128:(cc + 1) * 128],
                            identity=identb.ap), reads=[yh.b, identb.b], writes=[pt.b])
                    og, ob = PFM_OFF["lng"][0], PFM_OFF["lnb"][0]
                    for cc in range(4):
                        P.op("act", lambda e, pti=pti, ce=ce, cc=cc, j=j, og=og, ob=ob: e.activation(
                            out=ce.ap[:, cc, j * 128:(j + 1) * 128].rearrange("p (r w) -> p w r", r=2),
                            in_=psb(pti)[:, cc * 128:(cc + 1) * 128].rearrange("p (w r) -> p w r", r=2), func=AF.Silu,
                            scale=pf.ap[:, og + cc:og + cc + 1], bias=pf.ap[:, ob + cc:ob + cc + 1]),
                            reads=[pt.b, pf.b], writes=[ce.b])
                    for half in range(2):
                        pz = nextps()
                        for k in range(8):
                            P.op("pe", lambda e, pz=pz, k=k, j=j, half=half, h=h: e.matmul(
                                pz.ap, lhsT=h.ap[:, k, j * 128:(j + 1) * 128],
                                rhs=w_in.ap[:, k, 1024 + half * 512:1024 + (half + 1) * 512],
                                start=(k == 0), stop=(k == 7)), reads=[h.b, win_pc[2 + half]], writes=[pz.b])
                        P.op("act", lambda e, pz=pz, se=se, j=j, half=half: e.activation(
                            out=se.ap[:, j, half * 512:(half + 1) * 512], in_=pz.ap, func=AF.Silu),
                            reads=[pz.b], writes=[se.b])
                pd = nextps()
                for k in range(8):
                    P.op("pe", lambda e, pd=pd, k=k, j=j, h=h: e.matmul(
                        pd.ap[:, 0:16], lhsT=h.ap[:, k, j * 128:(j + 1) * 128], rhs=w_in.ap[:, k, 4096:4112],
                        start=(k == 0), stop=(k == 7)), reads=[h.b, win_pc[8]], writes=[pd.b])
                P.op("dve", lambda e, pd=pd: e.tensor_tensor(
                    out=dtt.ap, in0=pd.ap[:, 0:16].unsqueeze(1).broadcast_to([128, 2, HEADS]), in1=dtb_bc, op=ALU.add),
                    reads=[pd.b, hrow.b], writes=[dtt.b])
                P.op("act", lambda e: e.activation(out=dtt.ap, in_=dtt.ap, func=AF.Exp), reads=[dtt.b], writes=[dtt.b])
                P.op("act", lambda e, ch=ch: e.activation(out=dt_sb.ap[:, ch], in_=dtt.ap, func=AF.Ln, bias=one_t.ap[:, 0:1]),
                     reads=[dtt.b, one_t.b], writes=[dt_sb.b])
            if full:
                P.dma("act", lambda e, ce=ce, tok0=tok0: e.dma_start(out=cvT3[:, :, tok0:tok0 + TB], in_=ce.ap), ce.b,
                      reads=[ce.b])
                P.dma("act", lambda e, se=se, tok0=tok0: e.dma_start(
                    out=siluz_d[tok0:tok0 + TB, :].rearrange("(j p) d -> p j d", p=128), in_=se.ap), se.b, reads=[se.b])

        P.barrier()
        P.release([t.b for t in xin + xbc_ev + cT_ev + sz_ev] + win_pc + [cb_st.b, prow2.b, csil.b, grow.b] + [t.b for t in wm] + ([pfp.b] if L > 0 else []))
        aoff[0] = layer_mark
        d5 = alloc([16, SK, 128], BF16, "d5")
        o = PFM_OFF["sconvw"][0]
        for c4 in range(4):
            P.op("dve", lambda e, c4=c4, o=o: e.tensor_tensor(
                out=d5.ap[:, c4 * 4:(c4 + 1) * 4].rearrange("p c k j -> p (c k) j"),
                in0=ident.ap.unsqueeze(1).broadcast_to([128, 4 * SK, 128]),
                in1=pf.ap[:, o + c4 * 4 * SK:o + (c4 + 1) * 4 * SK].unsqueeze(2).broadcast_to([128, 4 * SK, 128]),
                op=ALU.mult), reads=[ident.b, pf.b], writes=[d5.b])
        scb_r = alloc([1536], BF16, "scb_r", parts=1)
        sb_st = alloc([1536], F32, "sb_st", dma=True, parts=1)
        o = PROW_OFF["sconvb"][0]
        P.dma("sp", lambda e, L=L, o=o: e.dma_start(out=sb_st.ap, in_=prow_in[L][:, o:o + 1536]), sb_st.b, writes=[sb_st.b])
        P.op("dve", lambda e: e.tensor_copy(out=scb_r.ap, in_=sb_st.ap), reads=[sb_st.b], writes=[scb_r.b])
        xc = [alloc([16, 132], BF16, "xc%d" % i, dma=True) for i in range(3)]
        xs_ev = [alloc([D_INNER], BF16, "xs_ev%d" % i, dma=True) for i in range(2)]
        bt_ev = [alloc([512], BF16, "bt_ev%d" % i, dma=True) for i in range(2)]
        BT_ev = [alloc([4, 128], BF16, "BT_ev%d" % i, dma=True) for i in range(2)]
        CT_ev = [alloc([4, 128], BF16, "CT_ev%d" % i, dma=True) for i in range(2)]
        BT3 = BT_d.rearrange("(g p) t -> p g t", p=128)
        CT3 = CT_d.rearrange("(g p) t -> p g t", p=128)
        osb = PFM_OFF["sconvb"][0]
        for ch in range(NCH):
            tok0 = ch * 128
            x_, xs_, bt_, BT_, CT_ = xc[ch % 3], xs_ev[ch % 2], bt_ev[ch % 2], BT_ev[ch % 2], CT_ev[ch % 2]
            for nch in ([0, 1, 2] if ch == 0 else [ch + 2]):
                if nch < NCH:
                    xn = xc[nch % 3]
                    coln = xbc_col(nch * 128) - 2
                    P.dma("sp", lambda e, xn=xn, coln=coln: e.dma_start(out=xn.ap, in_=xb3[:, :, coln:coln + 132]), xn.b,
                          writes=[xn.b])
            for which, dst in ((0, BT_), (1, CT_)):
                pb = nextps()
                for g in range(4):
                    cc = 8 + which * 4 + g
                    for tp in range(SK):
                        P.op("pe", lambda e, pb=pb, g=g, cc=cc, tp=tp, x_=x_: e.matmul(
                            pb.ap[:, g * 128:(g + 1) * 128], lhsT=d5.ap[:, cc, tp, :], rhs=x_.ap[:, cc, tp:tp + 128],
                            start=(tp == 0), stop=(tp == SK - 1)), reads=[d5.b, x_.b], writes=[pb.b])
                for g in range(4):
                    cc = 8 + which * 4 + g
                    P.op("act", lambda e, pb=pb, g=g, cc=cc, dst=dst: e.activation(
                        out=dst.ap[:, g, :], in_=pb.ap[:, g * 128:(g + 1) * 128], func=AF.Silu,
                        bias=pf.ap[:, osb + cc:osb + cc + 1]), reads=[pb.b, pf.b], writes=[dst.b])
            for grp in range(3):
                pb = nextps()
                for q in range(4):
                    cc = grp * 4 + q
                    P.op("pe", lambda e, pb=pb, q=q, cc=cc: e.matmul(
                        pb.ap[:, q * 128:(q + 1) * 128], lhsT=onesb.ap[0:1, :], rhs=scb_r.ap[0:1, cc * 128:(cc + 1) * 128],
                        start=True, stop=False, skip_group_check=True), reads=[onesb.b, scb_r.b], writes=[pb.b])
                    for tp in range(SK):
                        P.op("pe", lambda e, pb=pb, q=q, cc=cc, tp=tp, x_=x_: e.matmul(
                            pb.ap[:, q * 128:(q + 1) * 128], lhsT=x_.ap[:, cc, tp:tp + 128], rhs=d5.ap[:, cc, tp, :],
                            start=False, stop=(tp == SK - 1), skip_group_check=True), reads=[d5.b, x_.b], writes=[pb.b])
                dst = xs_.ap[:, grp * 512:(grp + 1) * 512] if grp < 2 else bt_.ap
                dstb = xs_.b if grp < 2 else bt_.b
                P.op("act", lambda e, pb=pb, dst=dst: e.activation(out=dst, in_=pb.ap, func=AF.Silu),
                     reads=[pb.b], writes=[dstb])
            P.dma("act", lambda e, xs_=xs_, tok0=tok0: e.dma_start(out=xs_d[tok0:tok0 + 128, :], in_=xs_.ap), xs_.b,
                  reads=[xs_.b])
            P.dma("act", lambda e, bt_=bt_, tok0=tok0: e.dma_start(out=bt_d[tok0:tok0 + 128, :], in_=bt_.ap), bt_.b,
                  reads=[bt_.b])
            P.dma("act", lambda e, BT_=BT_, tok0=tok0: e.dma_start(out=BT3[:, :, tok0:tok0 + 128], in_=BT_.ap), BT_.b,
                  reads=[BT_.b])
            P.dma("act", lambda e, CT_=CT_, tok0=tok0: e.dma_start(out=CT3[:, :, tok0:tok0 + 128], in_=CT_.ap), CT_.b,
                  reads=[CT_.b])

        P.barrier()
        P.release([t.b for t in xc + xs_ev + bt_ev + BT_ev + CT_ev] + [sb_st.b])
        aoff[0] = layer_mark
        w_out = alloc([12, D], BF16, "w_out")
        w_out.b = P.buf("w_out_sw", sw=True)
        wo3 = W["w_out"][L].rearrange("(k p) n -> p k n", p=128)
        for k in range(12):
            P.dma("pool", lambda e, k=k: e.dma_start(out=w_out.ap[:, k, :], in_=wo3[:, k, :]), w_out.b, writes=[w_out.b])
        dsk = alloc([2 * HEADS, 128], BF16, "dsk")
        P.op("dve", lambda e: e.tensor_tensor(
            out=dsk.ap, in0=ident.ap.unsqueeze(1).broadcast_to([128, 2 * HEADS, 128]),
            in1=hrow.ap[:, 2].rearrange("p a b -> p (a b)").unsqueeze(2).broadcast_to([128, 2 * HEADS, 128]),
            op=ALU.mult), reads=[ident.b, hrow.b], writes=[dsk.b])
        NS = 3
        xs_l = [alloc([D_INNER], BF16, "xs_l%d" % i, dma=True) for i in range(NS)]
        bt_l = [alloc([512], BF16, "bt_l%d" % i, dma=True) for i in range(NS)]
        BT_l = [alloc([4, 128], BF16, "BT_l%d" % i, dma=True) for i in range(NS)]
        CT_l = [alloc([4, 128], BF16, "CT_l%d" % i, dma=True) for i in range(NS)]
        hst = alloc([D_INNER], F32, "hst")
        hbf_t = [alloc([D_INNER], BF16, "hbf%d" % i) for i in range(2)]
        htmp = alloc([D_INNER], F32, "htmp")
        dA_t = [alloc([HEADS], F32, "dA%d" % i) for i in range(NS)]
        cs_t = [alloc([HEADS], F32, "cs%d" % i) for i in range(NS)]
        cshi_t = [alloc([HEADS], BF16, "cshi%d" % i) for i in range(NS)]
        cslo_t = [alloc([HEADS], BF16, "cslo%d" % i) for i in range(NS)]
        wv_t = [alloc([HEADS], F32, "wv%d" % i) for i in range(NS)]
        cdec_t = [alloc([HEADS], F32, "cdec%d" % i) for i in range(NS)]
        E1_t = [alloc([4, 128], BF16, "E1_%d" % i) for i in range(3)]
        E2_t = [alloc([4, 128], BF16, "E2_%d" % i) for i in range(3)]
        Mo_t = [alloc([HEADS, 128], BF16, "Mo%d" % i) for i in range(NS)]
        Md_t = [alloc([HEADS, 128], BF16, "Md%d" % i) for i in range(NS)]
        cbm_t = [alloc([4, 128], BF16, "cbm%d" % i) for i in range(NS)]
        xdt_t = [alloc([D_INNER], BF16, "xdt%d" % i) for i in range(NS)]
        xw_t = [alloc([D_INNER], BF16, "xw%d" % i) for i in range(NS)]
        yfs = [alloc([D_INNER], F32, "yfs%d" % i, dma=True) for i in range(2)]
        szl = [alloc([D_INNER], F32, "szl%d" % i, dma=True) for i in range(2)]
        cvl = [alloc([4, 128], BF16, "cvl%d" % i, dma=True) for i in range(2)]
        xres = [alloc([D], F32, "xres%d" % i, dma=True) for i in range(2)]
        gbuf_t = [alloc([D_INNER], F32, "gbuf%d" % i) for i in range(2)]
        gsq = alloc([D_INNER], F32, "gsq")
        gnb = alloc([D_INNER], BF16, "gnb")
        gT = alloc([8, 128], BF16, "gT")
        ssq = alloc([4], F32, "ssq")
        rr = alloc([D], F32, "rr")
        xo = [alloc([D], F32, "xo%d" % i, dma=True) for i in range(2)]
        st12 = alloc([12], F32, "st12")
        agb = alloc([D], F32, "agb", dma=True)
        abb = alloc([D], F32, "abb", dma=True)
        g1b = alloc([D], F32, "g1b", dma=True)
        onw = PFM_OFF["normw"][0]

        def load_bc(seg, stage):
            if stage == 0:
                if L == 0:
                    P.op("pool", lambda e: e.memset(agb.ap, ALPHA), writes=[agb.b])
                    P.op("pool", lambda e: e.memset(abb.ap, 0.0), writes=[abb.b])
                    srcs = None
                else:
                    srcs = (prow_in[L - 1], PROW_OFF["ln2g"][0], PROW_OFF["ln2b"][0])
            else:
                srcs = (prow_in[L], PROW_OFF["ln1g"][0], PROW_OFF["ln1b"][0])
            if srcs is not None:
                pr, og_, ob_ = srcs
                P.dma("sp", lambda e: e.dma_start(out=agb.ap, in_=pr[:, og_:og_ + D].partition_broadcast(128)), agb.b,
                      writes=[agb.b])
                P.dma("sp", lambda e: e.dma_start(out=abb.ap, in_=pr[:, ob_:ob_ + D].partition_broadcast(128)), abb.b,
                      writes=[abb.b])
                P.op("pool", lambda e: e.tensor_scalar(out=agb.ap, in0=agb.ap, scalar1=ALPHA, scalar2=None, op0=ALU.mult),
                     reads=[agb.b], writes=[agb.b])
                P.op("pool", lambda e: e.tensor_scalar(out=abb.ap, in0=abb.ap, scalar1=ALPHA, scalar2=None, op0=ALU.mult),
                     reads=[abb.b], writes=[abb.b])
            P.dma("sp", lambda e: e.dma_start(out=g1b.ap, in_=g_d[seg, stage:stage + 1, :].partition_broadcast(128)), g1b.b,
                  writes=[g1b.b])

        def residual_pre(xr):
            P.op("pool", lambda e, xr=xr: e.tensor_tensor(out=xr.ap, in0=xr.ap, in1=agb.ap, op=ALU.mult),
                 reads=[xr.b, agb.b], writes=[xr.b])
            P.op("pool", lambda e, xr=xr: e.tensor_tensor(out=xr.ap, in0=xr.ap, in1=abb.ap, op=ALU.add),
                 reads=[xr.b, abb.b], writes=[xr.b])

        def residual_ln(pbs, xr, out_t, final=None, pre=True):
            for _ in residual_ln_gen(pbs, xr, out_t, final=final, pre=pre):
                pass

        def residual_ln_gen(pbs, xr, out_t, final=None, pre=True):
            if pre:
                residual_pre(xr)
            for half in range(2):
                sl = slice(half * 512, (half + 1) * 512)
                P.op("dve", lambda e, half=half, sl=sl: e.tensor_tensor(out=rr.ap[:, sl], in0=pbs[half].ap, in1=g1b.ap[:, sl],
                                                                        op=ALU.mult), reads=[pbs[half].b, g1b.b], writes=[rr.b])
            P.op("dve", lambda e, xr=xr: e.tensor_tensor(out=rr.ap, in0=rr.ap, in1=xr.ap, op=ALU.add),
                 reads=[rr.b, xr.b], writes=[rr.b])
            yield
            for half in range(2):
                P.op("dve", lambda e, half=half: e.bn_stats(out=st12.ap[:, half * 6:(half + 1) * 6],
                                                            in_=rr.ap[:, half * 512:(half + 1) * 512]),
                     reads=[rr.b], writes=[st12.b])
            P.op("dve", lambda e: e.bn_aggr(out=mv.ap, in_=st12.ap), reads=[st12.b], writes=[mv.b])
            rstd_from(mv.ap[:, 1:2], rs, epsln, [mv.b])
            yield
            P.op("dve", lambda e, out_t=out_t: e.tensor_scalar(
                out=out_t.ap, in0=rr.ap, scalar1=mv.ap[:, 0:1], scalar2=rs.ap[:, 0:1], op0=ALU.subtract, op1=ALU.mult),
                reads=[rr.b, mv.b, rs.b], writes=[out_t.b])
            if final is not None:
                fg, fb = final
                P.op("pool", lambda e, out_t=out_t: e.tensor_tensor(out=out_t.ap, in0=out_t.ap, in1=fg.ap, op=ALU.mult),
                     reads=[out_t.b, fg.b], writes=[out_t.b])
                P.op("pool", lambda e, out_t=out_t: e.tensor_tensor(out=out_t.ap, in0=out_t.ap, in1=fb.ap, op=ALU.add),
                     reads=[out_t.b, fb.b], writes=[out_t.b])

        st6 = alloc([6], F32, "st6b")
        mv = alloc([2], F32, "mvb")
        rs = alloc([1], F32, "rsb")
        pcs_b, pcb_b, pR, py, pst = psum[0], psum[1], [psum[2], psum[3], psum[4]], [psum[5], psum[6]], [psum[7], psum[1]]
        yf_bufs = [P.buf("yfd%d" % i) for i in range(NCH)]
        pt_b, pti = psum[7], 7
        for d in range(NDIR):
            tri, msk = (triF, maskF) if d == 0 else (triB, maskB)
            if d == 0:
                order = list(range(NCH))
            else:
                order = list(range(NCC - 1, -1, -1)) + list(range(NCH - 1, NCC - 1, -1))
            P.op("dve", lambda e: e.memset(hst.ap, 0.0), writes=[hst.b])
            for hb_ in hbf_t:
                P.op("pool", lambda e, hb_=hb_: e.memset(hb_.ap, 0.0), writes=[hb_.b])
            state = {"seg": -1, "rq": 0}

            def wanty(ch):
                return not (last and seg_of_chunk(ch) == 0)

            def stage_A_load(it, ch):
                tok0 = ch * 128
                b2 = it % NS
                xs_, bt_, BT_, CT_ = xs_l[b2], bt_l[b2], BT_l[b2], CT_l[b2]
                P.dma("sp", lambda e: e.dma_start(out=xs_.ap, in_=xs_d[tok0:tok0 + 128, :]), xs_.b, writes=[xs_.b])
                P.dma("sp", lambda e: e.dma_start(out=bt_.ap, in_=bt_d[tok0:tok0 + 128, :]), bt_.b, writes=[bt_.b])
                if wanty(ch):
                    P.dma("sp", lambda e: e.dma_start(out=BT_.ap, in_=BT3[:, :, tok0:tok0 + 128]), BT_.b, writes=[BT_.b])
                    P.dma("sp", lambda e: e.dma_start(out=CT_.ap, in_=CT3[:, :, tok0:tok0 + 128]), CT_.b, writes=[CT_.b])

            def stage_A(it, ch, d=d, tri=tri, msk=msk):
                tok0 = ch * 128
                b2 = it % NS
                want_y = wanty(ch)
                xs_, bt_, BT_, CT_ = xs_l[b2], bt_l[b2], BT_l[b2], CT_l[b2]
                dA, cs, wv, cdec, xdt, xw, cbm, Mo, Md = (dA_t[b2], cs_t[b2], wv_t[b2], cdec_t[b2], xdt_t[b2], xw_t[b2],
                                                         cbm_t[b2], Mo_t[b2], Md_t[b2])
                cshi, cslo = cshi_t[b2], cslo_t[b2]
                dtv = dt_sb.ap[:, ch, d, :]
                P.op("dve", lambda e: e.tensor_tensor(out=dA.ap, in0=dtv, in1=negA.ap[:, d, :], op=ALU.mult),
                     reads=[dt_sb.b, negA.b], writes=[dA.b])
                P.op("pe", lambda e: e.matmul(pcs_b.ap[:, 0:16], lhsT=tri.ap, rhs=dA.ap, start=True, stop=True),
                     reads=[tri.b, dA.b], writes=[pcs_b.b])
                P.op("pe", lambda e: e.matmul(pcs_b.ap[:, 16:32], lhsT=triF.ap[:, 127:128].broadcast_to([128, 128]),
                                              rhs=dA.ap, start=True, stop=True), reads=[triF.b, dA.b], writes=[pcs_b.b])
                P.op("dve", lambda e: e.tensor_scalar(out=cs.ap, in0=pcs_b.ap[:, 0:16], scalar1=-1.0, scalar2=None, op0=ALU.mult),
                     reads=[pcs_b.b], writes=[cs.b])
                P.op("dve", lambda e: e.tensor_tensor(out=wv.ap, in0=pcs_b.ap[:, 16:32], in1=cs.ap, op=ALU.add),
                     reads=[pcs_b.b, cs.b], writes=[wv.b])
                P.op("dve", lambda e: e.tensor_copy(out=cshi.ap, in_=cs.ap), reads=[cs.b], writes=[cshi.b])
                P.op("dve", lambda e: e.tensor_tensor(out=cslo.ap, in0=cs.ap, in1=cshi.ap, op=ALU.subtract),
                     reads=[cs.b, cshi.b], writes=[cslo.b])
                P.op("act", lambda e: e.activation(out=cdec.ap, in_=pcs_b.ap[:, 16:32], func=AF.Exp),
                     reads=[pcs_b.b], writes=[cdec.b])
                P.op("act", lambda e: e.activation(out=wv.ap, in_=wv.ap, func=AF.Exp), reads=[wv.b], writes=[wv.b])
                P.op("dve", lambda e: e.tensor_tensor(out=wv.ap, in0=wv.ap, in1=dtv, op=ALU.mult),
                     reads=[wv.b, dt_sb.b], writes=[wv.b])
                yield
                xs3 = xs_.ap.rearrange("p (h q) -> p h q", q=HP)
                P.op("pool", lambda e: e.tensor_tensor(
                    out=xw.ap.rearrange("p (h q) -> p h q", q=HP), in0=xs3,
                    in1=wv.ap.unsqueeze(2).broadcast_to([128, HEADS, HP]), op=ALU.mult),
                    reads=[xs_.b, wv.b], writes=[xw.b])
                if not want_y:
                    return
                P.op("dve", lambda e: e.tensor_tensor(
                    out=xdt.ap.rearrange("p (h q) -> p h q", q=HP), in0=xs3,
                    in1=dtv.unsqueeze(2).broadcast_to([128, HEADS, HP]), op=ALU.mult),
                    reads=[xs_.b, dt_sb.b], writes=[xdt.b])
                for g in range(4):
                    P.op("pe", lambda e, g=g: e.matmul(
                        pcb_b.ap[:, g * 128:(g + 1) * 128], lhsT=BT_.ap[:, g, :], rhs=CT_.ap[:, g, :], start=True, stop=True),
                        reads=[BT_.b, CT_.b], writes=[pcb_b.b])
                P.op("dve", lambda e: e.tensor_copy(out=cbm.ap.rearrange("p g l -> p (g l)"), in_=pcb_b.ap),
                     reads=[pcb_b.b], writes=[cbm.b])
                qinfo = []

                def part2(q, pb, E2):
                    P.op("pe", lambda e: e.matmul(pb.ap, lhsT=identb.ap, rhs=msk.ap.rearrange("p g l -> p (g l)"),
                                                  start=False, stop=False, skip_group_check=True),
                         reads=[identb.b, msk.b], writes=[pb.b])
                    P.op("pe", lambda e: e.matmul(pb.ap, lhsT=identb.ap,
                                                  rhs=cshi.ap[:, q * 4:(q + 1) * 4].unsqueeze(2).broadcast_to([128, 4, 128]),
                                                  start=False, stop=False, skip_group_check=True),
                         reads=[identb.b, cshi.b], writes=[pb.b])
                    P.op("pe", lambda e: e.matmul(pb.ap, lhsT=identb.ap,
                                                  rhs=cslo.ap[:, q * 4:(q + 1) * 4].unsqueeze(2).broadcast_to([128, 4, 128]),
                                                  start=False, stop=True, skip_group_check=True),
                         reads=[identb.b, cslo.b], writes=[pb.b])
                    P.op("act", lambda e: e.activation(out=E2.ap.rearrange("p r l -> p (r l)"), in_=pb.ap, func=AF.Exp),
                         reads=[pb.b], writes=[E2.b])
                    P.op("dve", lambda e: e.tensor_tensor(
                        out=Md.ap[:, q * 4:(q + 1) * 4, :], in0=E2.ap,
                        in1=cbm.ap[:, q:q + 1, :].broadcast_to([128, 4, 128]), op=ALU.mult),
                        reads=[E2.b, cbm.b], writes=[Md.b])

                for q in range(4):
                    state["rq"] += 1
                    pb = pR[state["rq"] % 3]
                    E1, E2 = E1_t[state["rq"] % 3], E2_t[state["rq"] % 3]
                    for r in range(4):
                        h = q * 4 + r
                        P.op("pe", lambda e, pb=pb, h=h, r=r: e.matmul(
                            pb.ap[:, r * 128:(r + 1) * 128], lhsT=dA.ap[:, h:h + 1].broadcast_to([128, 128]), rhs=tri.ap,
                            start=(r == 0), stop=False, skip_group_check=True), reads=[dA.b, tri.b], writes=[pb.b])
                    P.op("act", lambda e, pb=pb, E1=E1: e.activation(out=E1.ap.rearrange("p r l -> p (r l)"), in_=pb.ap,
                                                                    func=AF.Exp), reads=[pb.b], writes=[E1.b])
                    P.op("pool", lambda e, E1=E1, q=q: e.tensor_tensor(
                        out=Mo.ap[:, q * 4:(q + 1) * 4, :], in0=E1.ap,
                        in1=CT_.ap[:, q:q + 1, :].broadcast_to([128, 4, 128]), op=ALU.mult),
                        reads=[E1.b, CT_.b], writes=[Mo.b])
                    yield
                    if qinfo:
                        part2(*qinfo.pop())
                        yield
                    qinfo.append((q, pb, E2))
                part2(*qinfo.pop())

            def stage_B(it, ch, d=d):
                tok0 = ch * 128
                b2 = it % NS
                want_y = wanty(ch)
                xs_, bt_ = xs_l[b2], bt_l[b2]
                cdec, xdt, xw, Mo, Md = cdec_t[b2], xdt_t[b2], xw_t[b2], Mo_t[b2], Md_t[b2]
                htm = htmp
                hb_prev, hb_new = hbf_t[(it + 1) % 2], hbf_t[it % 2]
                for g in range(4):
                    pb = pst[g // 2]
                    P.op("pe", lambda e, pb=pb, g=g: e.matmul(
                        pb.ap[:, (g % 2) * 256:(g % 2 + 1) * 256], lhsT=bt_.ap[:, g * 128:(g + 1) * 128],
                        rhs=xw.ap[:, g * 256:(g + 1) * 256], start=True, stop=True), reads=[bt_.b, xw.b], writes=[pb.b])
                P.op("pool", lambda e: e.tensor_tensor(
                    out=htm.ap.rearrange("p (h q) -> p h q", q=HP), in0=hst.ap.rearrange("p (h q) -> p h q", q=HP),
                    in1=cdec.ap.unsqueeze(2).broadcast_to([128, HEADS, HP]), op=ALU.mult),
                    reads=[hst.b, cdec.b], writes=[htm.b])
                if want_y:
                    for h in range(HEADS):
                        pyb = py[h // 8]
                        osl = slice((h % 8) * HP, (h % 8 + 1) * HP)
                        P.op("pe", lambda e, pyb=pyb, h=h, osl=osl: e.matmul(
                            pyb.ap[:, osl], lhsT=Md.ap[:, h, :], rhs=xdt.ap[:, h * HP:(h + 1) * HP],
                            start=(h % 8 == 0), stop=False, skip_group_check=True), reads=[Md.b, xdt.b], writes=[pyb.b])
                        P.op("pe", lambda e, pyb=pyb, h=h, osl=osl: e.matmul(
                            pyb.ap[:, osl], lhsT=dsk.ap[:, d * HEADS + h, :], rhs=xs_.ap[:, h * HP:(h + 1) * HP],
                            start=False, stop=False, skip_group_check=True), reads=[dsk.b, xs_.b], writes=[pyb.b])
                    for h in range(HEADS):
                        pyb = py[h // 8]
                        osl = slice((h % 8) * HP, (h % 8 + 1) * HP)
                        P.op("pe", lambda e, pyb=pyb, h=h, osl=osl: e.matmul(
                            pyb.ap[:, osl], lhsT=Mo.ap[:, h, :], rhs=hb_prev.ap[:, h * HP:(h + 1) * HP],
                            start=False, stop=True, skip_group_check=True), reads=[Mo.b, hb_prev.b], writes=[pyb.b])
                yield
                for half in range(2):
                    sl = slice(half * 512, (half + 1) * 512)
                    P.op("dve", lambda e, half=half, sl=sl: e.tensor_tensor(
                        out=hst.ap[:, sl], in0=htm.ap[:, sl], in1=pst[half].ap, op=ALU.add),
                        reads=[htm.b, pst[half].b], writes=[hst.b])
                yield
                P.op("act", lambda e: e.activation(out=hb_new.ap, in_=hst.ap, func=AF.Identity), reads=[hst.b],
                     writes=[hb_new.b])
                yield
                if not want_y:
                    return
                c2 = it % 2
                yf = yfs[c2]
                if d == 0:
                    P.op("act", lambda e: e.activation(out=yf.ap[:, 0:512], in_=py[0].ap, func=AF.Identity),
                         reads=[py[0].b], writes=[yf.b])
                    P.op("dve", lambda e: e.tensor_copy(out=yf.ap[:, 512:1024], in_=py[1].ap), reads=[py[1].b], writes=[yf.b])
                    P.dma("act", lambda e: e.dma_start(out=yf_d[tok0:tok0 + 128, :], in_=yf.ap), yf.b, reads=[yf.b],
                          writes=[yf_bufs[ch]])
                    return
                seg = seg_of_chunk(ch)
                sz, cv, xr, gbuf = szl[c2], cvl[c2], xres[c2], gbuf_t[c2]
                P.dma("sp", lambda e: e.dma_start(out=yf.ap, in_=yf_d[tok0:tok0 + 128, :]), yf.b, writes=[yf.b],
                      reads=[yf_bufs[ch]])
                P.dma("sp", lambda e: e.dma_start(out=sz.ap, in_=siluz_d[tok0:tok0 + 128, :]), sz.b, writes=[sz.b])
                P.dma("sp", lambda e: e.dma_start(out=cv.ap, in_=cvT3[:, :, tok0:tok0 + 128]), cv.b, writes=[cv.b])
                P.dma("sp", lambda e: e.dma_start(out=xr.ap, in_=stream_rows(tok0, 128)), xr.b, writes=[xr.b])
                for half in range(2):
                    sl = slice(half * 512, (half + 1) * 512)
                    P.op("dve", lambda e, sl=sl, half=half: e.tensor_tensor(
                        out=gbuf.ap[:, sl], in0=py[half].ap, in1=yf.ap[:, sl], op=ALU.add),
                        reads=[py[half].b, yf.b], writes=[gbuf.b])
                if dbg:
                    P.dma("sp", lambda e: e.dma_start(out=dbg_y[tok0:tok0 + 128, :], in_=gbuf.ap), sz.b, reads=[gbuf.b])
                P.op("pool", lambda e: e.tensor_tensor(out=gbuf.ap, in0=gbuf.ap, in1=sz.ap, op=ALU.mult),
                     reads=[gbuf.b, sz.b], writes=[gbuf.b])

            def stage_C2(it, ch):
                tok0 = ch * 128
                b2 = it % 2
                seg = seg_of_chunk(ch)
                if seg != state["seg"]:
                    load_bc(seg, 0)
                    state["seg"] = seg
                cv, xr, gbuf, xo_ = cvl[b2], xres[b2], gbuf_t[b2], xo[b2]
                residual_pre(xr)
                for g in range(4):
                    P.op("act", lambda e, g=g: e.activation(out=gsq.ap[:, g * 256:(g + 1) * 256], in_=gbuf.ap[:, g * 256:(g + 1) * 256],
                                                            func=AF.Square, accum_out=ssq.ap[:, g:g + 1]),
                         reads=[gbuf.b], writes=[gsq.b, ssq.b])
                rstd_from(ssq.ap, ssq, epsrms, [ssq.b], scale=1.0 / 256.0)
                yield
                P.op("dve", lambda e: e.tensor_tensor(
                    out=gnb.ap.rearrange("p (g q) -> p g q", q=256), in0=gbuf.ap.rearrange("p (g q) -> p g q", q=256),
                    in1=ssq.ap.unsqueeze(2).broadcast_to([128, 4, 256]), op=ALU.mult), reads=[gbuf.b, ssq.b], writes=[gnb.b])
                yield
                po = [py[0], py[1]]
                for k in range(8):
                    P.op("pe", lambda e, k=k: e.transpose(out=psb(pti)[:, k * 128:(k + 1) * 128],
                                                          in_=gnb.ap[:, k * 128:(k + 1) * 128], identity=identb.ap),
                         reads=[gnb.b, identb.b], writes=[pt_b.b])
                yield
                for k in range(8):
                    P.op("act", lambda e, k=k: e.activation(out=gT.ap[:, k, :], in_=psb(pti)[:, k * 128:(k + 1) * 128],
                                                            func=AF.Identity, scale=pf.ap[:, onw + k:onw + k + 1]),
                         reads=[pt_b.b, pf.b], writes=[gT.b])
                yield
                for k in range(12):
                    for half in range(2):
                        lhs = cv.ap[:, k, :] if k < 4 else gT.ap[:, k - 4, :]
                        lb = cv.b if k < 4 else gT.b
                        P.op("pe", lambda e, half=half, k=k, lhs=lhs: e.matmul(
                            po[half].ap, lhsT=lhs, rhs=w_out.ap[:, k, half * 512:(half + 1) * 512],
                            start=(k == 0), stop=(k == 11), skip_group_check=True), reads=[lb, w_out.b], writes=[po[half].b])
                yield
                yield from residual_ln_gen(po, xr, xo_, pre=False)
                P.dma("act", lambda e: e.dma_start(out=sA[tok0:tok0 + 128, :], in_=xo_.ap), xo_.b, reads=[xo_.b])
                if dbg:
                    P.dma("sp", lambda e: e.dma_start(out=dbg_r[tok0:tok0 + 128, :], in_=rr.ap), xo_.b, reads=[rr.b])

            n = len(order)

            def run(g):
                for _ in g:
                    pass

            def step(g):
                if g is not None:
                    next(g, None)

            for i0 in range(min(3, n)):
                stage_A_load(i0, order[i0])
            run(stage_A(0, order[0]))
            if n > 1:
                run(stage_A(1, order[1]))
            for it in range(n):
                gens = [stage_B(it, order[it])]
                if d == 1 and it >= 1 and wanty(order[it - 1]):
                    gens.append(stage_C2(it - 1, order[it - 1]))
                if it + 2 < n:
                    gens.append(stage_A(it + 2, order[it + 2]))
                while gens:
                    for g in list(gens):
                        try:
                            next(g)
                        except StopIteration:
                            gens.remove(g)
                if it + 3 < n:
                    stage_A_load(it + 3, order[it + 3])
            if d == 1 and wanty(order[n - 1]):
                run(stage_C2(n - 1, order[n - 1]))

        P.barrier()
        P.release([t.b for t in xs_l + bt_l + BT_l + CT_l + yfs + szl + cvl + xres + xo] + [w_out.b, agb.b, abb.b, g1b.b])
        aoff[0] = layer_mark
        w1 = alloc([8, D_FF], BF16, "w1")
        w13 = W["w1"][L].rearrange("(k p) n -> p k n", p=128)
        w1_pc = [P.buf("w1_pc%d" % i, sw=True) for i in range(8)]
        for wpc in range(8):
            P.dma("pool", lambda e, wpc=wpc: e.dma_start(out=w1.ap[:, :, wpc * 512:(wpc + 1) * 512],
                                                         in_=w13[:, :, wpc * 512:(wpc + 1) * 512]), w1_pc[wpc], writes=[w1_pc[wpc]])
        w2 = alloc([32, D], BF16, "w2")
        w23 = W["w2"][L].rearrange("(k p) n -> p k n", p=128)
        w2_pc = [P.buf("w2_pc%d" % i, sw=True) for i in range(8)]
        for wpc in range(8):
            for hf in range(2):
                P.dma("pool", lambda e, wpc=wpc, hf=hf: e.dma_start(out=w2.ap[:, wpc * 4:(wpc + 1) * 4, hf * 512:(hf + 1) * 512],
                                                                   in_=w23[:, wpc * 4:(wpc + 1) * 4, hf * 512:(hf + 1) * 512]),
                      w2_pc[wpc], writes=[w2_pc[wpc]])
        agb = alloc([D], F32, "agb3", dma=True)
        abb = alloc([D], F32, "abb3", dma=True)
        g1b = alloc([D], F32, "g1b3", dma=True)
        xin3 = [alloc([2, D], F32, "xin3_%d" % i, dma=True) for i in range(2)]
        h2T = [alloc([8, TB], BF16, "h2T0")] * 2
        hid = alloc([32, TB], BF16, "hid")
        rtmp = [alloc([TB], F32, "rtmp%d" % i) for i in range(2)]
        rr = alloc([D], F32, "rr3")
        xo3 = [alloc([D], F32, "xo3_0", dma=True)] * 2
        st12 = alloc([12], F32, "st12_3")
        mv = alloc([2], F32, "mv3")
        rs = alloc([1], F32, "rs3")
        fin = None
        if last:
            fg = alloc([D], F32, "fg", dma=True)
            fb = alloc([D], F32, "fb", dma=True)
            o1_, o2_ = PROW_OFF["ln2g"][0], PROW_OFF["ln2b"][0]
            P.dma("sp", lambda e: e.dma_start(out=fg.ap, in_=prow_in[L][:, o1_:o1_ + D].partition_broadcast(128)), fg.b,
                  writes=[fg.b])
            P.dma("sp", lambda e: e.dma_start(out=fb.ap, in_=prow_in[L][:, o2_:o2_ + D].partition_broadcast(128)), fb.b,
                  writes=[fb.b])
            fin = (fg, fb)
        cur_seg = -1
        blk0 = (CTX // TB) if last else 0
        for bi in range(blk0, nblk):
            tok0 = bi * TB
            seg = 0 if tok0 < CTX else 1
            if seg != cur_seg:
                load_bc(seg, 1)
                cur_seg = seg
            xi, h = xin3[bi % 2], h2T[bi % 2]
            for nb in ([blk0, blk0 + 1] if bi == blk0 else [bi + 1]):
                if nb < nblk:
                    xn = xin3[nb % 2]
                    P.dma("sp", lambda e, xn=xn, nb=nb: e.dma_start(
                        out=xn.ap, in_=sA[nb * TB:(nb + 1) * TB, :].rearrange("(j p) d -> p j d", p=128)), xn.b, writes=[xn.b])
            for k in range(8):
                pb = nextps()
                for j in range(2):
                    P.op("pe", lambda e, pb=pb, xi=xi, j=j, k=k: e.transpose(
                        out=pb.ap[:, j * 128:(j + 1) * 128], in_=xi.ap[:, j, k * 128:(k + 1) * 128], identity=ident.ap),
                        reads=[xi.b, ident.b], writes=[pb.b])
                P.op("act", lambda e, pb=pb, h=h, k=k, seg=seg: e.activation(
                    out=h.ap[:, k, :], in_=pb.ap[:, 0:TB], func=AF.Identity,
                    scale=AB.ap[:, 1, k, seg:seg + 1], bias=AB.ap[:, 1, k, 2 + seg:3 + seg]),
                    reads=[pb.b, AB.b], writes=[h.b])
            for f in range(32):
                pb = nextps()
                for k in range(8):
                    P.op("pe", lambda e, pb=pb, k=k, f=f, h=h: e.matmul(
                        pb.ap[:, 0:TB], lhsT=w1.ap[:, k, f * 128:(f + 1) * 128], rhs=h.ap[:, k, :],
                        start=(k == 0), stop=(k == 7)), reads=[w1_pc[f // 4], h.b], writes=[pb.b])
                rt = rtmp[f % 2]
                if f % 2 == 0:
                    P.op("act", lambda e, pb=pb, rt=rt: e.activation(out=rt.ap, in_=pb.ap[:, 0:TB], func=AF.Relu),
                         reads=[pb.b], writes=[rt.b])
                else:
                    P.op("dve", lambda e, pb=pb, rt=rt: e.tensor_scalar(out=rt.ap, in0=pb.ap[:, 0:TB], scalar1=0.0, scalar2=None,
                                                                        op0=ALU.max), reads=[pb.b], writes=[rt.b])
                P.op("pool", lambda e, f=f, rt=rt: e.tensor_tensor(out=hid.ap[:, f, :], in0=rt.ap, in1=rt.ap, op=ALU.mult),
                     reads=[rt.b], writes=[hid.b])
            for j in range(2):
                po = [nextps(), nextps()]
                for half in range(2):
                    for f in range(32):
                        P.op("pe", lambda e, half=half, f=f, j=j, po=po: e.matmul(
                            po[half].ap, lhsT=hid.ap[:, f, j * 128:(j + 1) * 128], rhs=w2.ap[:, f, half * 512:(half + 1) * 512],
                            start=(f == 0), stop=(f == 31)), reads=[hid.b, w2_pc[f // 4]], writes=[po[half].b])
                xo_ = xo3[j]
                xr = Tl(xi.ap[:, j, :], xi.b)
                residual_ln(po, xr, xo_, final=fin)
                t0_ = tok0 + j * 128
                if last:
                    P.dma("sp", lambda e, xo_=xo_, t0_=t0_: e.dma_start(out=y_out[t0_ - CTX:t0_ - CTX + 128, :], in_=xo_.ap),
                          xo_.b, reads=[xo_.b])
                else:
                    P.dma("sp", lambda e, xo_=xo_, t0_=t0_: e.dma_start(out=sB[t0_:t0_ + 128, :], in_=xo_.ap), xo_.b,
                          reads=[xo_.b])
        P.barrier()
        rel = w1_pc + w2_pc + [agb.b, abb.b, g1b.b, pf.b, hrow.b] + [t.b for t in xin3 + xo3[:1]]
        if last:
            rel += [fin[0].b, fin[1].b]
        P.release(rel)


_CACHE = {}


def kernel(**inp):
    inp = {k: np.asarray(v) for k, v in inp.items()}
    B, SEQ, _ = inp["x"].shape
    CTX = inp["ctx"].shape[1]
    depth = inp["w_in"].shape[0]
    key = (SEQ, CTX, depth)
    if key not in _CACHE:
        _CACHE[key] = build(SEQ, CTX, depth)
    nc, _P = _CACHE[key]
    packs = [pack_params(inp, L) for L in range(depth)]
    pfm = np.stack([p[0] for p in packs]).astype(np.float32)
    prow = np.stack([p[1] for p in packs]).astype(np.float32)
    shared = {k: np.ascontiguousarray(inp[k], dtype=np.float32) for k in ("w_mod", "w_in", "w_out", "w1", "w2")}
    in_maps = []
    for b in range(B):
        cf = np.stack([_fm(inp["c_ctx"]), _fm(inp["c"][b])], axis=-1).reshape(128, 16).astype(np.float32)
        m = {"x": np.ascontiguousarray(inp["x"][b], dtype=np.float32),
             "ctx": np.ascontiguousarray(inp["ctx"][b], dtype=np.float32),
             "c_fm": np.ascontiguousarray(cf), "pfm": pfm, "prow": prow}
        m.update(shared)
        in_maps.append(m)
    res = run_bass_kernel_spmd(nc, in_maps, core_ids=list(range(B)))
    return np.stack([np.asarray(r["y"], dtype=np.float32) for r in res.results], axis=0)
```

```python
import contextlib
import numpy as np
import concourse.bass as bass
import concourse.mybir as mybir
from concourse.bass_utils import run_bass_kernel_spmd

F32 = mybir.dt.float32
BF16 = mybir.dt.bfloat16
AF = mybir.ActivationFunctionType
ALU = mybir.AluOpType

D = 1024
DEPTH = 2
GRID_W = 64
CONV_DIM = 512
CONV_K = 31
D_INNER = 1024
HEADS = 16
HP = 64
GROUPS = 4
NST = 128
SK = 5
D_XBC = 2048
D_PROJ = 4112
D_FF = 4096
ALPHA = float((2 * DEPTH) ** 0.25)
LN_EPS = 1e-5
RMS_EPS = 1e-5
NEG = -30000.0

ENGS = ("pe", "act", "dve", "pool", "sp")
import os
MENG = os.environ.get("K_MENG", "pool")
NDIR = int(os.environ.get("K_NDIR", "2"))
NOC2 = int(os.environ.get("K_NOC2", "0"))


class Buf:
    __slots__ = ("name", "w", "r", "sem", "excl")

    def __init__(self, name):
        self.name = name
        self.w = None
        self.r = []
        self.sem = None
        self.excl = False


class Op:
    __slots__ = ("eng", "fn", "deps", "is_dma", "sem", "semval", "sig", "sigval", "idx", "eidx", "waits")

    def __init__(self, eng, fn):
        self.eng = eng
        self.fn = fn
        self.deps = []
        self.is_dma = False
        self.sem = None
        self.semval = 0
        self.sig = False
        self.sigval = 0
        self.idx = 0
        self.eidx = 0
        self.waits = []


class Prog:
    def __init__(self, nc, n_hw_sems=60, n_sw_sems=28, plan=None):
        self.nc = nc
        self.plan = plan
        self.ops = []
        self.eng_ops = {e: [] for e in ENGS}
        n_dma_sems = n_hw_sems + n_sw_sems
        self.n_dma_sems = n_dma_sems
        self.dma_cnt = [0] * n_dma_sems
        self.dma_free = list(range(n_hw_sems))
        self.sw_free = list(range(n_hw_sems, n_dma_sems))
        self.sw_map = {}
        self.all_bufs = []
        self.n = 0
        if plan is not None:
            self.engobj = {"pe": nc.tensor, "act": nc.scalar, "dve": nc.vector, "pool": nc.gpsimd, "sp": nc.sync}
            self.stack = contextlib.ExitStack()
            self.esem = {e: self.stack.enter_context(nc.semaphore("s_" + e)) for e in ENGS}
            self.dsem = [self.stack.enter_context(nc.semaphore("d%d" % i)) for i in range(n_dma_sems)]

    def buf(self, name, dma=False, sw=False):
        b = Buf(name)
        if sw:
            if name not in self.sw_map:
                assert self.sw_free, "out of sw dma semaphores"
                self.sw_map[name] = self.sw_free.pop(0)
            b.sem = self.sw_map[name]
        elif dma:
            assert self.dma_free, "out of dma semaphores"
            b.sem = self.dma_free.pop(0)
        if self.plan is None:
            self.all_bufs.append(b)
        return b

    def release(self, bufs):
        for b in bufs:
            if b.sem is not None:
                if b.sem not in self.sw_map.values():
                    self.dma_free.append(b.sem)
                b.sem = None

    def _emit(self, eng_name, fn, is_dma):
        rec = self.plan.ops[self.n]
        self.n += 1
        assert rec.eng == eng_name and rec.is_dma == is_dma, (rec.eng, eng_name)
        eng = self.engobj[eng_name]
        for kind, d in rec.waits:
            if kind == "dma":
                eng.wait_ge(self.dsem[d.sem], d.semval)
            else:
                eng.wait_ge(self.esem[d.eng], d.sigval)
        if fn is None:
            if rec.sig:
                eng.nop().then_inc(self.esem[eng_name], 1)
            return
        ins = fn(eng)
        if is_dma:
            ins.then_inc(self.dsem[rec.sem], 16)
        elif rec.sig:
            ins.then_inc(self.esem[eng_name], 1)

    def _add(self, op, reads, writes):
        deps = []
        for b in reads:
            if b.w is not None:
                deps.append(b.w)
            if b.excl:
                deps.extend(r for r in b.r if r.eng != op.eng)
        for b in writes:
            if b.w is not None:
                deps.append(b.w)
            deps.extend(b.r)
        for b in reads:
            b.r.append(op)
        for b in writes:
            b.w = op
            b.r = []
        seen = set()
        for d in deps:
            if d is op or id(d) in seen:
                continue
            seen.add(id(d))
            op.deps.append(d)
        op.idx = len(self.ops)
        op.eidx = len(self.eng_ops[op.eng])
        self.ops.append(op)
        self.eng_ops[op.eng].append(op)
        return op

    def op(self, eng, fn, reads=(), writes=()):
        if self.plan is not None:
            return self._emit(eng, fn, False)
        return self._add(Op(eng, None), reads, writes)

    def dma(self, eng, fn, sbuf, reads=(), writes=()):
        if self.plan is not None:
            return self._emit(eng, fn, True)
        o = Op(eng, None)
        o.is_dma = True
        assert sbuf.sem is not None, sbuf.name
        o.sem = sbuf.sem
        self.dma_cnt[o.sem] += 16
        o.semval = self.dma_cnt[o.sem]
        return self._add(o, reads, writes)

    def barrier(self):
        if self.plan is not None:
            for e in ENGS:
                self._emit(e, None, False)
            return
        last = [ops[-1] for ops in self.eng_ops.values() if ops]
        latest = {}
        for o in self.ops:
            if o.is_dma:
                latest[o.sem] = o
        deps = last + list(latest.values())
        for e in ENGS:
            o = Op(e, None)
            o.deps = list(deps)
            o.idx = len(self.ops)
            o.eidx = len(self.eng_ops[e])
            self.ops.append(o)
            self.eng_ops[e].append(o)
        for b in self.all_bufs:
            b.w = None
            b.r = []

    def analyze(self):
        seen_eng = {e: {f: 0 for f in ENGS} for e in ENGS}
        seen_dma = {e: [0] * self.n_dma_sems for e in ENGS}
        for o in self.ops:
            e = o.eng
            for d in o.deps:
                if d.is_dma:
                    if seen_dma[e][d.sem] >= d.semval:
                        continue
                    seen_dma[e][d.sem] = d.semval
                    o.waits.append(("dma", d))
                else:
                    if d.eng == e and e != "pool" and (o.eidx - d.eidx > 2 or e == "pe" or e == "sp"):
                        continue
                    if seen_eng[e][d.eng] >= d.eidx + 1:
                        continue
                    seen_eng[e][d.eng] = d.eidx + 1
                    d.sig = True
                    o.waits.append(("eng", d))
            o.deps = None
        cnt = {e: 0 for e in ENGS}
        for o in self.ops:
            if o.sig:
                cnt[o.eng] += 1
                o.sigval = cnt[o.eng]
        self.sig_counts = cnt

    def finish(self):
        assert self.n == len(self.plan.ops), (self.n, len(self.plan.ops))
        for e in ENGS:
            eng = self.engobj[e]
            for i in range(self.n_dma_sems):
                if self.plan.dma_cnt[i] > 0:
                    eng.wait_ge(self.dsem[i], self.plan.dma_cnt[i])
        self.stack.close()


class Tl:
    __slots__ = ("ap", "b")

    def __init__(self, ap, b):
        self.ap = ap
        self.b = b


PFM_OFF = {}
_o = 0
for _n, _w in (("bmod", 48), ("convb", 4), ("lng", 4), ("lnb", 4), ("sconvb", 16), ("normw", 8),
               ("ln1g", 8), ("ln1b", 8), ("ln2g", 8), ("ln2b", 8), ("convw", 4 * CONV_K), ("sconvw", 16 * SK)):
    PFM_OFF[_n] = (_o, _w)
    _o += _w
NPF = _o
PROW_OFF = {}
_o = 0
for _n, _w in (("bmod_g1", 1024), ("bmod_g2", 1024), ("convb", 512), ("sconvb", 1536), ("ln1g", 1024), ("ln1b", 1024),
               ("ln2g", 1024), ("ln2b", 1024), ("dtb", 32), ("alog", 32), ("dskip", 32)):
    PROW_OFF[_n] = (_o, _w)
    _o += _w
NPR = _o


def _fm(v):
    return np.ascontiguousarray(v.reshape(-1, 128).T)


def pack_params(inp, L):
    pf = np.zeros((128, NPF), np.float32)

    def put(name, arr):
        o, w = PFM_OFF[name]
        assert arr.shape == (128, w), (name, arr.shape)
        pf[:, o:o + w] = arr
    put("bmod", _fm(inp["b_mod"][L]))
    put("convb", _fm(inp["conv_b"][L]))
    put("lng", _fm(inp["conv_ln_g"][L]))
    put("lnb", _fm(inp["conv_ln_b"][L]))
    put("sconvb", _fm(inp["ssm_conv_b"][L]))
    put("normw", _fm(inp["ssm_norm_w"][L]))
    put("ln1g", _fm(inp["ln1_g"][L]))
    put("ln1b", _fm(inp["ln1_b"][L]))
    put("ln2g", _fm(inp["ln2_g"][L]))
    put("ln2b", _fm(inp["ln2_b"][L]))
    cw = inp["conv_w"][L]
    put("convw", np.ascontiguousarray(cw.reshape(CONV_K, 4, 128).transpose(2, 1, 0)).reshape(128, 4 * CONV_K))
    sw = inp["ssm_conv_w"][L]
    put("sconvw", np.ascontiguousarray(sw.reshape(SK, 16, 128).transpose(2, 1, 0)).reshape(128, 16 * SK))
    pr = np.zeros((1, NPR), np.float32)

    def putr(name, arr):
        o, w = PROW_OFF[name]
        pr[0, o:o + w] = arr.reshape(-1)
    putr("bmod_g1", inp["b_mod"][L][2048:3072])
    putr("bmod_g2", inp["b_mod"][L][5120:6144])
    putr("convb", inp["conv_b"][L])
    putr("sconvb", inp["ssm_conv_b"][L][:1536])
    putr("ln1g", inp["ln1_g"][L])
    putr("ln1b", inp["ln1_b"][L])
    putr("ln2g", inp["ln2_g"][L])
    putr("ln2b", inp["ln2_b"][L])
    putr("dtb", inp["dt_bias"][L])
    putr("alog", inp["a_log"][L])
    putr("dskip", inp["d_skip"][L])
    return pf, pr


def build(SEQ, CTX, depth=DEPTH, dbg=False):
    nc = bass.Bass("TRN2", target_bir_lowering=False)
    TALL = CTX + SEQ
    NCC = CTX // 128
    NLC = SEQ // 128
    NCH = NCC + NLC
    TB = 256
    assert CTX % TB == 0 and SEQ % TB == 0

    def dram_in(name, shape, dt=F32):
        return nc.dram_tensor(name, list(shape), dt, kind="ExternalInput").ap()

    def dram_tmp(name, shape, dt=F32):
        if dbg:
            return nc.dram_tensor(name, list(shape), dt, kind="ExternalOutput").ap()
        return nc.dram_tensor(name, list(shape), dt).ap()

    x_in = dram_in("x", [SEQ, D])
    ctx_in = dram_in("ctx", [CTX, D])
    c_fm = dram_in("c_fm", [128, 16])
    W = {}
    for nm, shp in (("w_mod", [depth, D, 6 * D]), ("w_in", [depth, D, D_PROJ]), ("w_out", [depth, 1536, D]),
                    ("w1", [depth, D, D_FF]), ("w2", [depth, D_FF, D])):
        W[nm] = dram_in(nm, shp)
    pfm_in = dram_in("pfm", [depth, 128, NPF])
    prow_in = dram_in("prow", [depth, 1, NPR])
    y_out = nc.dram_tensor("y", [SEQ, D], F32, kind="ExternalOutput").ap()

    sA = dram_tmp("sA", [TALL, D])
    sB = dram_tmp("sB", [TALL, D])
    TP = TALL + 8
    xbcT = dram_tmp("xbcT", [D_XBC, TP], BF16)
    convT_d = dram_tmp("convT", [CONV_DIM, TALL], BF16)
    siluz_d = dram_tmp("siluz", [TALL, D_INNER])
    xs_d = dram_tmp("xs_tok", [TALL, D_INNER], BF16)
    bt_d = dram_tmp("b_tok", [TALL, 512], BF16)
    BT_d = dram_tmp("BT", [512, TALL], BF16)
    CT_d = dram_tmp("CT", [512, TALL], BF16)
    yf_d = dram_tmp("y_f", [TALL, D_INNER])
    g_d = dram_tmp("grow", [2, 2, D])
    dbg_y = dram_tmp("dbg_y", [TALL, D_INNER]) if dbg else None
    dbg_r = dram_tmp("dbg_r", [TALL, D]) if dbg else None

    def seg_of_chunk(ch):
        return 0 if ch < NCC else 1

    def xbc_col(tok):
        return tok + 2 if tok < CTX else tok + 6

    ARENA_W = 53000
    arena = nc.alloc_sbuf_tensor("arena", [128, ARENA_W], F32)
    psum_t = [nc.alloc_psum_tensor("ps%d" % i, [128, 512], F32) for i in range(8)]
    plan = Prog(nc)
    program(nc, plan, locals())
    plan.analyze()
    em = Prog(nc, plan=plan)
    program(nc, em, locals())
    em.finish()
    return nc, plan


def program(nc, P, env):
    SEQ, CTX, depth, TALL, NCC, NLC, NCH, TB, TP, ARENA_W = (env[k] for k in (
        "SEQ", "CTX", "depth", "TALL", "NCC", "NLC", "NCH", "TB", "TP", "ARENA_W"))
    x_in, ctx_in, c_fm, W, pfm_in, prow_in, y_out = (env[k] for k in (
        "x_in", "ctx_in", "c_fm", "W", "pfm_in", "prow_in", "y_out"))
    sA, sB, xbcT, convT_d, siluz_d, xs_d, bt_d, BT_d, CT_d, yf_d, g_d = (env[k] for k in (
        "sA", "sB", "xbcT", "convT_d", "siluz_d", "xs_d", "bt_d", "BT_d", "CT_d", "yf_d", "g_d"))
    arena, psum_t, seg_of_chunk, xbc_col = env["arena"], env["psum_t"], env["seg_of_chunk"], env["xbc_col"]
    dbg, dbg_y, dbg_r = env["dbg"], env["dbg_y"], env["dbg_r"]
    aoff = [0]

    def alloc(shape, dt=F32, name="t", dma=False, parts=128):
        n = int(np.prod(shape))
        nw = n if dt == F32 else (n + 1) // 2
        nw = (nw + 7) // 8 * 8
        assert aoff[0] + nw <= ARENA_W, ("arena overflow", name, aoff[0], nw)
        v = arena[0:parts, aoff[0]:aoff[0] + nw]
        aoff[0] += nw
        if dt != F32:
            v = v.bitcast(dt)
        v = v[:, 0:n]
        if len(shape) == 2:
            v = v.rearrange("p (a b) -> p a b", a=shape[0])
        elif len(shape) == 3:
            v = v.rearrange("p (a b c) -> p a b c", a=shape[0], b=shape[1])
        return Tl(v, P.buf(name, dma=dma))

    psum = [Tl(psum_t[i][:, :], P.buf("ps%d" % i)) for i in range(8)]
    for t in psum:
        t.b.excl = True

    def psb(i):
        return psum[i].ap.bitcast(BF16)

    ident = alloc([128], F32, "ident")
    identb = alloc([128], BF16, "identb")
    triF = alloc([128], F32, "triF")
    triB = alloc([128], F32, "triB")
    ntriF = alloc([128], F32, "ntriF")
    ntriB = alloc([128], F32, "ntriB")
    maskF = alloc([4, 128], BF16, "maskF")
    maskB = alloc([4, 128], BF16, "maskB")
    onesb = alloc([128], BF16, "onesb")
    epsln = alloc([1], F32, "epsln")
    epsrms = alloc([1], F32, "epsrms")
    one_t = alloc([1], F32, "one_t")
    dt_sb = alloc([NCH, 2, HEADS], F32, "dt_sb")
    C0 = [ident, identb, triF, triB, ntriF, ntriB, maskF, maskB, onesb, epsln, epsrms, one_t]

    def pool_fill(t, ap, val):
        P.op("pool", lambda e, ap=ap, val=val: e.memset(ap, val), writes=[t.b])

    def pool_sel(t, ap, pattern, cmp, fill, base, cm):
        P.op("pool", lambda e: e.affine_select(out=ap, in_=ap, pattern=pattern, compare_op=cmp, fill=fill,
                                               base=base, channel_multiplier=cm), reads=[t.b], writes=[t.b])

    pool_fill(ident, ident.ap, 0.0)
    pool_sel(ident, ident.ap, [[-1, 128]], ALU.not_equal, 1.0, 0, 1)
    P.op("pool", lambda e: e.tensor_copy(out=identb.ap, in_=ident.ap), reads=[ident.b], writes=[identb.b])
    pool_fill(triF, triF.ap, 1.0)
    pool_sel(triF, triF.ap, [[1, 128]], ALU.is_ge, 0.0, 0, -1)
    pool_fill(triB, triB.ap, 1.0)
    pool_sel(triB, triB.ap, [[-1, 128]], ALU.is_ge, 0.0, 0, 1)
    P.op("pool", lambda e: e.tensor_scalar(out=ntriF.ap, in0=triF.ap, scalar1=-1.0, scalar2=None, op0=ALU.mult),
         reads=[triF.b], writes=[ntriF.b])
    P.op("pool", lambda e: e.tensor_scalar(out=ntriB.ap, in0=triB.ap, scalar1=-1.0, scalar2=None, op0=ALU.mult),
         reads=[triB.b], writes=[ntriB.b])
    pool_fill(maskF, maskF.ap, 0.0)
    pool_sel(maskF, maskF.ap, [[0, 4], [1, 128]], ALU.is_ge, NEG, 0, -1)
    pool_fill(maskB, maskB.ap, 0.0)
    pool_sel(maskB, maskB.ap, [[0, 4], [-1, 128]], ALU.is_ge, NEG, 0, 1)
    pool_fill(onesb, onesb.ap, 1.0)
    pool_fill(epsln, epsln.ap, LN_EPS)
    pool_fill(epsrms, epsrms.ap, RMS_EPS)
    pool_fill(one_t, one_t.ap, 1.0)

    zt = alloc([16, 8], BF16, "zt", dma=True)
    pool_fill(zt, zt.ap, 0.0)
    xb3 = xbcT.rearrange("(c p) t -> p c t", p=128)
    for c0 in (0, CTX + 2, CTX + 4, TP - 2):
        P.dma("sp", lambda e, c0=c0: e.dma_start(out=xb3[:, :, c0:c0 + 2], in_=zt.ap[:, :, 0:2]), zt.b, reads=[zt.b])
    persist_mark = aoff[0]

    def rstd_from(var_ap, out_t, eps_t, reads, n=1, scale=1.0):
        P.op("act", lambda e: e.activation(out=out_t.ap, in_=var_ap, func=AF.Ln, bias=eps_t.ap[:, 0:1], scale=scale),
             reads=reads + [eps_t.b], writes=[out_t.b])
        P.op("act", lambda e: e.activation(out=out_t.ap, in_=out_t.ap, func=AF.Exp, scale=-0.5), reads=[out_t.b],
             writes=[out_t.b])

    for L in range(depth):
        last = (L == depth - 1)
        src = None if L == 0 else sB

        def stream_rows(tok0, n, src=src):
            if src is None:
                return ctx_in[tok0:tok0 + n, :] if tok0 < CTX else x_in[tok0 - CTX:tok0 - CTX + n, :]
            return src[tok0:tok0 + n, :]

        P.barrier()
        aoff[0] = persist_mark
        pf = alloc([NPF], F32, "pf", dma=True)
        P.dma("sp", lambda e, L=L: e.dma_start(out=pf.ap, in_=pfm_in[L]), pf.b, writes=[pf.b])
        AB = alloc([2, 8, 4], F32, "AB")
        hrow = alloc([3, 2, HEADS], F32, "hrow", dma=True)
        negA = alloc([2, HEADS], F32, "negA")
        layer_mark = aoff[0]

        def pfv(name):
            o, w = PFM_OFF[name]
            return pf.ap[:, o:o + w]
        prow2 = alloc([NPR], F32, "prow2", dma=True, parts=2)
        P.dma("sp", lambda e, L=L: e.dma_start(out=prow2.ap, in_=prow_in[L].partition_broadcast(2)), prow2.b,
              writes=[prow2.b])
        csil = alloc([8, 2], F32, "csil", dma=True)
        P.dma("sp", lambda e: e.dma_start(out=csil.ap, in_=c_fm.rearrange("p (k s) -> p k s", s=2)), csil.b,
              writes=[csil.b])
        P.op("act", lambda e: e.activation(out=csil.ap, in_=csil.ap, func=AF.Silu), reads=[csil.b], writes=[csil.b])
        modfm = alloc([48, 2], F32, "modfm")
        grow = alloc([2, D], F32, "grow", dma=True, parts=2)
        wm = [alloc([8, 1024], F32, "wm%d" % i, dma=True) for i in range(2)]
        wmod3 = W["w_mod"][L].rearrange("(k p) n -> p k n", p=128)
        for cb in range(6):
            t = wm[cb % 2]
            P.dma("sp", lambda e, t=t, cb=cb: e.dma_start(out=t.ap, in_=wmod3[:, :, cb * 1024:(cb + 1) * 1024]),
                  t.b, writes=[t.b])
            if cb in (2, 5):
                which = 0 if cb == 2 else 1
                for half in range(2):
                    pb = psum[half]
                    for k in range(8):
                        P.op("pe", lambda e, t=t, k=k, half=half, pb=pb: e.matmul(
                            pb.ap[0:2, :], lhsT=csil.ap[:, k, :], rhs=t.ap[:, k, half * 512:(half + 1) * 512],
                            start=(k == 0), stop=(k == 7)), reads=[t.b, csil.b], writes=[pb.b])
                    o, _w = PROW_OFF["bmod_g1" if which == 0 else "bmod_g2"]
                    P.op("dve", lambda e, pb=pb, half=half, which=which, o=o: e.tensor_tensor(
                        out=grow.ap[:, which, half * 512:(half + 1) * 512], in0=pb.ap[0:2, :],
                        in1=prow2.ap[:, o + half * 512:o + (half + 1) * 512], op=ALU.add),
                        reads=[pb.b, prow2.b], writes=[grow.b])
            else:
                pb = psum[2 + (cb % 2)]
                for j in range(8):
                    for k in range(8):
                        P.op("pe", lambda e, t=t, k=k, j=j, pb=pb: e.matmul(
                            pb.ap[:, j * 2:j * 2 + 2], lhsT=t.ap[:, k, j * 128:(j + 1) * 128], rhs=csil.ap[:, k, :],
                            start=(k == 0), stop=(k == 7)), reads=[t.b, csil.b], writes=[pb.b])
                bo = PFM_OFF["bmod"][0] + cb * 8
                P.op("dve", lambda e, pb=pb, cb=cb, bo=bo: e.tensor_tensor(
                    out=modfm.ap[:, cb * 8:(cb + 1) * 8, :],
                    in0=pb.ap[:, 0:16].rearrange("p (j s) -> p j s", s=2),
                    in1=pf.ap[:, bo:bo + 8].unsqueeze(2).broadcast_to([128, 8, 2]), op=ALU.add),
                    reads=[pb.b, pf.b], writes=[modfm.b])
        P.dma("sp", lambda e: e.dma_start(out=g_d.rearrange("s w d -> s (w d)"),
                                          in_=grow.ap.rearrange("p w d -> p (w d)")), grow.b, reads=[grow.b])
        gp = alloc([8], F32, "gp")
        bp = alloc([8], F32, "bp")
        if L == 0:
            P.op("dve", lambda e: e.memset(gp.ap, 1.0), writes=[gp.b])
            P.op("dve", lambda e: e.memset(bp.ap, 0.0), writes=[bp.b])
        else:
            pfp = alloc([NPF], F32, "pfp", dma=True)
            P.dma("sp", lambda e, L=L: e.dma_start(out=pfp.ap, in_=pfm_in[L - 1]), pfp.b, writes=[pfp.b])
            o1, o2 = PFM_OFF["ln2g"][0], PFM_OFF["ln2b"][0]
            P.op("dve", lambda e: e.tensor_copy(out=gp.ap, in_=pfp.ap[:, o1:o1 + 8]), reads=[pfp.b], writes=[gp.b])
            P.op("dve", lambda e: e.tensor_copy(out=bp.ap, in_=pfp.ap[:, o2:o2 + 8]), reads=[pfp.b], writes=[bp.b])
        tmp1 = alloc([8, 2], F32, "tmp1")
        for stg in range(2):
            gsrc = gp.ap if stg == 0 else pfv("ln1g")
            bsrc = bp.ap if stg == 0 else pfv("ln1b")
            gb = [gp.b, bp.b] if stg == 0 else [pf.b]
            shc, scc = (0, 1) if stg == 0 else (3, 4)
            P.op("dve", lambda e, scc=scc: e.tensor_scalar(out=tmp1.ap, in0=modfm.ap[:, scc * 8:(scc + 1) * 8, :],
                                                           scalar1=1.0, scalar2=None, op0=ALU.add),
                 reads=[modfm.b], writes=[tmp1.b])
            P.op("dve", lambda e, stg=stg, gsrc=gsrc: e.tensor_tensor(
                out=AB.ap[:, stg, :, 0:2], in0=tmp1.ap, in1=gsrc.unsqueeze(2).broadcast_to([128, 8, 2]), op=ALU.mult),
                reads=[tmp1.b] + gb, writes=[AB.b])
            P.op("dve", lambda e, bsrc=bsrc: e.tensor_tensor(
                out=tmp1.ap, in0=tmp1.ap, in1=bsrc.unsqueeze(2).broadcast_to([128, 8, 2]), op=ALU.mult),
                reads=[tmp1.b] + gb, writes=[tmp1.b])
            P.op("dve", lambda e, stg=stg, shc=shc: e.tensor_tensor(
                out=AB.ap[:, stg, :, 2:4], in0=tmp1.ap, in1=modfm.ap[:, shc * 8:(shc + 1) * 8, :], op=ALU.add),
                reads=[tmp1.b, modfm.b], writes=[AB.b])
        o = PROW_OFF["dtb"][0]
        P.dma("sp", lambda e, L=L, o=o: e.dma_start(
            out=hrow.ap.rearrange("p a b c -> p (a b c)"), in_=prow_in[L][:, o:o + 96].partition_broadcast(128)),
            hrow.b, writes=[hrow.b])
        P.op("act", lambda e: e.activation(out=negA.ap, in_=hrow.ap[:, 1], func=AF.Exp), reads=[hrow.b], writes=[negA.b])
        P.op("dve", lambda e: e.tensor_scalar(out=negA.ap, in0=negA.ap, scalar1=-1.0, scalar2=None, op0=ALU.mult),
             reads=[negA.b], writes=[negA.b])

        P.barrier()
        aoff[0] = layer_mark
        w_in = alloc([8, D_PROJ], BF16, "w_in")
        win3 = W["w_in"][L].rearrange("(k p) n -> p k n", p=128)
        win_pc = [P.buf("w_in_pc%d" % i, sw=True) for i in range(9)]
        for wpc in (1, 0, 4, 5, 6, 7, 2, 3, 8):
            c0, c1 = wpc * 512, min((wpc + 1) * 512, D_PROJ)
            P.dma("pool", lambda e, c0=c0, c1=c1: e.dma_start(out=w_in.ap[:, :, c0:c1], in_=win3[:, :, c0:c1]),
                  win_pc[wpc], writes=[win_pc[wpc]])
        d31 = alloc([4, CONV_K, 128], BF16, "d31")
        o = PFM_OFF["convw"][0]
        for cc in range(4):
            P.op("dve", lambda e, cc=cc, o=o: e.tensor_tensor(
                out=d31.ap[:, cc], in0=ident.ap.unsqueeze(1).broadcast_to([128, CONV_K, 128]),
                in1=pf.ap[:, o + cc * CONV_K:o + (cc + 1) * CONV_K].unsqueeze(2).broadcast_to([128, CONV_K, 128]),
                op=ALU.mult), reads=[ident.b, pf.b], writes=[d31.b])
        convb_r = alloc([512], BF16, "convb_r", parts=1)
        cb_st = alloc([512], F32, "cb_st", dma=True, parts=1)
        o = PROW_OFF["convb"][0]
        P.dma("sp", lambda e, L=L, o=o: e.dma_start(out=cb_st.ap, in_=prow_in[L][:, o:o + 512]), cb_st.b, writes=[cb_st.b])
        P.op("dve", lambda e: e.tensor_copy(out=convb_r.ap, in_=cb_st.ap), reads=[cb_st.b], writes=[convb_r.b])
        dtb_bc = hrow.ap[:, 0]
        xin = [alloc([2, D], F32, "xin%d" % i, dma=True) for i in range(2)]
        hT = [alloc([8, TB], BF16, "hT%d" % i) for i in range(2)]
        vpad = [alloc([4, 4, 94], BF16, "vpad%d" % i) for i in range(2)]
        for t in vpad:
            pool_fill(t, t.ap, 0.0)
        sig = [alloc([TB], F32, "sig%d" % i) for i in range(2)]
        xbc_ev = [alloc([16, TB], BF16, "xbc_ev%d" % i, dma=True) for i in range(2)]
        yhat = [alloc([512], BF16, "yhat%d" % i) for i in range(2)]
        cT_ev = [alloc([4, TB], BF16, "cT_ev%d" % i, dma=True) for i in range(2)]
        sz_ev = [alloc([2, D_INNER], F32, "sz_ev%d" % i, dma=True) for i in range(2)]
        st6 = alloc([6], F32, "st6")
        mv = alloc([2], F32, "mv")
        rs = alloc([1], F32, "rs")
        dtt = alloc([2, HEADS], F32, "dtt")
        cvT3 = convT_d.rearrange("(c p) t -> p c t", p=128)
        pc = [0]

        def nextps():
            pc[0] += 1
            return psum[pc[0] % 8]

        nblk = TALL // TB
        for bi in range(nblk):
            tok0 = bi * TB
            seg = 0 if tok0 < CTX else 1
            full = not (last and seg == 0)
            xi, h, vp, xe, ce, se = xin[bi % 2], hT[bi % 2], vpad[bi % 2], xbc_ev[bi % 2], cT_ev[bi % 2], sz_ev[bi % 2]
            def load_x(nb):
                if nb < nblk:
                    xn = xin[nb % 2]
                    P.dma("sp", lambda e, xn=xn, nb=nb: e.dma_start(
                        out=xn.ap, in_=stream_rows(nb * TB, TB).rearrange("(j p) d -> p j d", p=128)), xn.b, writes=[xn.b])

            def transposes(nb):
                if nb >= nblk:
                    return
                xi_, h_ = xin[nb % 2], hT[nb % 2]
                seg_ = 0 if nb * TB < CTX else 1
                for k in range(8):
                    pb = nextps()
                    for j in range(2):
                        P.op("pe", lambda e, pb=pb, j=j, k=k: e.transpose(
                            out=pb.ap[:, j * 128:(j + 1) * 128], in_=xi_.ap[:, j, k * 128:(k + 1) * 128], identity=ident.ap),
                            reads=[xi_.b, ident.b], writes=[pb.b])
                    P.op("act", lambda e, pb=pb, k=k: e.activation(
                        out=h_.ap[:, k, :], in_=pb.ap[:, 0:TB], func=AF.Identity,
                        scale=AB.ap[:, 0, k, seg_:seg_ + 1], bias=AB.ap[:, 0, k, 2 + seg_:3 + seg_]),
                        reads=[pb.b, AB.b], writes=[h_.b])

            if bi == 0:
                load_x(0)
                load_x(1)
                transposes(0)
            if bi > 0:
                load_x(bi + 1)

            def ws_mm(col0, pb, h=h):
                for k in range(8):
                    P.op("pe", lambda e, k=k, pb=pb, col0=col0: e.matmul(
                        pb.ap[:, 0:TB], lhsT=w_in.ap[:, k, col0:col0 + 128], rhs=h.ap[:, k, :],
                        start=(k == 0), stop=(k == 7)), reads=[win_pc[col0 // 512], h.b], writes=[pb.b])
            if full:
                for cc in range(4):
                    sg = sig[cc % 2]
                    pg = nextps()
                    ws_mm(512 + cc * 128, pg)
                    P.op("act", lambda e, pg=pg, sg=sg: e.activation(out=sg.ap, in_=pg.ap[:, 0:TB], func=AF.Sigmoid),
                         reads=[pg.b], writes=[sg.b])
                    pa = nextps()
                    ws_mm(cc * 128, pa)
                    P.op("dve", lambda e, pa=pa, sg=sg, vp=vp, cc=cc: e.tensor_tensor(
                        out=vp.ap[:, cc, :, 15:79], in0=pa.ap[:, 0:TB].rearrange("p (r w) -> p r w", w=64),
                        in1=sg.ap.rearrange("p (r w) -> p r w", w=64), op=ALU.mult),
                        reads=[pa.b, sg.b], writes=[vp.b])
            for cc in range(16):
                pb = nextps()
                ws_mm(2048 + cc * 128, pb)
                if cc % 2 == 0:
                    P.op("act", lambda e, pb=pb, xe=xe, cc=cc: e.activation(out=xe.ap[:, cc, :], in_=pb.ap[:, 0:TB],
                                                                          func=AF.Identity), reads=[pb.b], writes=[xe.b])
                else:
                    P.op("dve", lambda e, pb=pb, xe=xe, cc=cc: e.tensor_copy(out=xe.ap[:, cc, :], in_=pb.ap[:, 0:TB]),
                         reads=[pb.b], writes=[xe.b])
            col0 = xbc_col(tok0)
            P.dma("act", lambda e, xe=xe, col0=col0: e.dma_start(out=xb3[:, :, col0:col0 + TB], in_=xe.ap), xe.b,
                  reads=[xe.b])
            transposes(bi + 1)
            for j in range(2):
                ch = (tok0 // 128) + j
                if full:
                    pcv = nextps()
                    for cc in range(4):
                        P.op("pe", lambda e, pcv=pcv, cc=cc: e.matmul(
                            pcv.ap[:, cc * 128:(cc + 1) * 128], lhsT=onesb.ap[0:1, :],
                            rhs=convb_r.ap[0:1, cc * 128:(cc + 1) * 128], start=True, stop=False, skip_group_check=True),
                            reads=[onesb.b, convb_r.b], writes=[pcv.b])
                        for tp in range(CONV_K):
                            for row in range(2):
                                P.op("pe", lambda e, pcv=pcv, cc=cc, tp=tp, vp=vp, j=j, row=row: e.matmul(
                                    pcv.ap[row * 64:(row + 1) * 64, cc * 128:(cc + 1) * 128], lhsT=vp.ap[:, cc, 2 * j + row, tp:tp + 64],
                                    rhs=d31.ap[:, cc, tp, :], start=False, stop=(tp == CONV_K - 1), skip_group_check=True),
                                    reads=[vp.b, d31.b], writes=[pcv.b])
                    yh = yhat[j]
                    P.op("dve", lambda e, pcv=pcv: e.bn_stats(out=st6.ap, in_=pcv.ap), reads=[pcv.b], writes=[st6.b])
                    P.op("dve", lambda e: e.bn_aggr(out=mv.ap, in_=st6.ap), reads=[st6.b], writes=[mv.b])
                    rstd_from(mv.ap[:, 1:2], rs, epsln, [mv.b])
                    P.op("dve", lambda e, pcv=pcv, yh=yh: e.tensor_scalar(
                        out=yh.ap, in0=pcv.ap, scalar1=mv.ap[:, 0:1], scalar2=rs.ap[:, 0:1], op0=ALU.subtract, op1=ALU.mult),
                        reads=[pcv.b, mv.b, rs.b], writes=[yh.b])
                    pt = nextps()
                    pti = psum.index(pt)
                    for cc in range(4):
                        P.op("pe", lambda e, pti=pti, yh=yh, cc=cc: e.transpose(
                            out=psb(pti)[:, cc * 128:(cc + 1) * 128], in_=yh.ap[:, cc * 128:(cc + 1) * 128],
                            identity=identb.ap), reads=[yh.b, identb.b], writes=[pt.b])
                    og, ob = PFM_OFF["lng"][0], PFM_OFF["lnb"][0]
                    for cc in range(4):
                        P.op("act", lambda e, pti=pti, ce=ce, cc=cc, j=j, og=og, ob=ob: e.activation(
                            out=ce.ap[:, cc, j * 128:(j + 1) * 128], in_=psb(pti)[:, cc * 128:(cc + 1) * 128], func=AF.Silu,
                            scale=pf.ap[:, og + cc:og + cc + 1], bias=pf.ap[:, ob + cc:ob + cc + 1]),
                            reads=[pt.b, pf.b], writes=[ce.b])
                    for half in range(2):
                        pz = nextps()
                        for k in range(8):
                            P.op("pe", lambda e, pz=pz, k=k, j=j, half=half, h=h: e.matmul(
                                pz.ap, lhsT=h.ap[:, k, j * 128:(j + 1) * 128],
                                rhs=w_in.ap[:, k, 1024 + half * 512:1024 + (half + 1) * 512],
                                start=(k == 0), stop=(k == 7)), reads=[h.b, win_pc[2 + half]], writes=[pz.b])
                        P.op("act", lambda e, pz=pz, se=se, j=j, half=half: e.activation(
                            out=se.ap[:, j, half * 512:(half + 1) * 512], in_=pz.ap, func=AF.Silu),
                            reads=[pz.b], writes=[se.b])
                pd = nextps()
                for k in range(8):
                    P.op("pe", lambda e, pd=pd, k=k, j=j, h=h: e.matmul(
                        pd.ap[:, 0:16], lhsT=h.ap[:, k, j * 128:(j + 1) * 128], rhs=w_in.ap[:, k, 4096:4112],
                        start=(k == 0), stop=(k == 7)), reads=[h.b, win_pc[8]], writes=[pd.b])
                P.op("dve", lambda e, pd=pd: e.tensor_tensor(
                    out=dtt.ap, in0=pd.ap[:, 0:16].unsqueeze(1).broadcast_to([128, 2, HEADS]), in1=dtb_bc, op=ALU.add),
                    reads=[pd.b, hrow.b], writes=[dtt.b])
                P.op("act", lambda e: e.activation(out=dtt.ap, in_=dtt.ap, func=AF.Exp), reads=[dtt.b], writes=[dtt.b])
                P.op("act", lambda e, ch=ch: e.activation(out=dt_sb.ap[:, ch], in_=dtt.ap, func=AF.Ln, bias=one_t.ap[:, 0:1]),
                     reads=[dtt.b, one_t.b], writes=[dt_sb.b])
            if full:
                P.dma("act", lambda e, ce=ce, tok0=tok0: e.dma_start(out=cvT3[:, :, tok0:tok0 + TB], in_=ce.ap), ce.b,
                      reads=[ce.b])
                P.dma("act", lambda e, se=se, tok0=tok0: e.dma_start(
                    out=siluz_d[tok0:tok0 + TB, :].rearrange("(j p) d -> p j d", p=128), in_=se.ap), se.b, reads=[se.b])

        P.barrier()
        P.release([t.b for t in xin + xbc_ev + cT_ev + sz_ev] + win_pc + [cb_st.b, prow2.b, csil.b, grow.b] + [t.b for t in wm] + ([pfp.b] if L > 0 else []))
        aoff[0] = layer_mark
        d5 = alloc([16, SK, 128], BF16, "d5")
        o = PFM_OFF["sconvw"][0]
        for c4 in range(4):
            P.op("dve", lambda e, c4=c4, o=o: e.tensor_tensor(
                out=d5.ap[:, c4 * 4:(c4 + 1) * 4].rearrange("p c k j -> p (c k) j"),
                in0=ident.ap.unsqueeze(1).broadcast_to([128, 4 * SK, 128]),
                in1=pf.ap[:, o + c4 * 4 * SK:o + (c4 + 1) * 4 * SK].unsqueeze(2).broadcast_to([128, 4 * SK, 128]),
                op=ALU.mult), reads=[ident.b, pf.b], writes=[d5.b])
        scb_r = alloc([1536], BF16, "scb_r", parts=1)
        sb_st = alloc([1536], F32, "sb_st", dma=True, parts=1)
        o = PROW_OFF["sconvb"][0]
        P.dma("sp", lambda e, L=L, o=o: e.dma_start(out=sb_st.ap, in_=prow_in[L][:, o:o + 1536]), sb_st.b, writes=[sb_st.b])
        P.op("dve", lambda e: e.tensor_copy(out=scb_r.ap, in_=sb_st.ap), reads=[sb_st.b], writes=[scb_r.b])
        xc = [alloc([16, 132], BF16, "xc%d" % i, dma=True) for i in range(3)]
        xs_ev = [alloc([D_INNER], BF16, "xs_ev%d" % i, dma=True) for i in range(2)]
        bt_ev = [alloc([512], BF16, "bt_ev%d" % i, dma=True) for i in range(2)]
        BT_ev = [alloc([4, 128], BF16, "BT_ev%d" % i, dma=True) for i in range(2)]
        CT_ev = [alloc([4, 128], BF16, "CT_ev%d" % i, dma=True) for i in range(2)]
        BT3 = BT_d.rearrange("(g p) t -> p g t", p=128)
        CT3 = CT_d.rearrange("(g p) t -> p g t", p=128)
        osb = PFM_OFF["sconvb"][0]
        for ch in range(NCH):
            tok0 = ch * 128
            x_, xs_, bt_, BT_, CT_ = xc[ch % 3], xs_ev[ch % 2], bt_ev[ch % 2], BT_ev[ch % 2], CT_ev[ch % 2]
            for nch in ([0, 1, 2] if ch == 0 else [ch + 2]):
                if nch < NCH:
                    xn = xc[nch % 3]
                    coln = xbc_col(nch * 128) - 2
                    P.dma("sp", lambda e, xn=xn, coln=coln: e.dma_start(out=xn.ap, in_=xb3[:, :, coln:coln + 132]), xn.b,
                          writes=[xn.b])
            for which, dst in ((0, BT_), (1, CT_)):
                pb = nextps()
                for g in range(4):
                    cc = 8 + which * 4 + g
                    for tp in range(SK):
                        P.op("pe", lambda e, pb=pb, g=g, cc=cc, tp=tp, x_=x_: e.matmul(
                            pb.ap[:, g * 128:(g + 1) * 128], lhsT=d5.ap[:, cc, tp, :], rhs=x_.ap[:, cc, tp:tp + 128],
                            start=(tp == 0), stop=(tp == SK - 1)), reads=[d5.b, x_.b], writes=[pb.b])
                for g in range(4):
                    cc = 8 + which * 4 + g
                    P.op("act", lambda e, pb=pb, g=g, cc=cc, dst=dst: e.activation(
                        out=dst.ap[:, g, :], in_=pb.ap[:, g * 128:(g + 1) * 128], func=AF.Silu,
                        bias=pf.ap[:, osb + cc:osb + cc + 1]), reads=[pb.b, pf.b], writes=[dst.b])
            for grp in range(3):
                pb = nextps()
                for q in range(4):
                    cc = grp * 4 + q
                    P.op("pe", lambda e, pb=pb, q=q, cc=cc: e.matmul(
                        pb.ap[:, q * 128:(q + 1) * 128], lhsT=onesb.ap[0:1, :], rhs=scb_r.ap[0:1, cc * 128:(cc + 1) * 128],
                        start=True, stop=False, skip_group_check=True), reads=[onesb.b, scb_r.b], writes=[pb.b])
                    for tp in range(SK):
                        P.op("pe", lambda e, pb=pb, q=q, cc=cc, tp=tp, x_=x_: e.matmul(
                            pb.ap[:, q * 128:(q + 1) * 128], lhsT=x_.ap[:, cc, tp:tp + 128], rhs=d5.ap[:, cc, tp, :],
                            start=False, stop=(tp == SK - 1), skip_group_check=True), reads=[d5.b, x_.b], writes=[pb.b])
                dst = xs_.ap[:, grp * 512:(grp + 1) * 512] if grp < 2 else bt_.ap
                dstb = xs_.b if grp < 2 else bt_.b
                P.op("act", lambda e, pb=pb, dst=dst: e.activation(out=dst, in_=pb.ap, func=AF.Silu),
                     reads=[pb.b], writes=[dstb])
            P.dma("act", lambda e, xs_=xs_, tok0=tok0: e.dma_start(out=xs_d[tok0:tok0 + 128, :], in_=xs_.ap), xs_.b,
                  reads=[xs_.b])
            P.dma("act", lambda e, bt_=bt_, tok0=tok0: e.dma_start(out=bt_d[tok0:tok0 + 128, :], in_=bt_.ap), bt_.b,
                  reads=[bt_.b])
            P.dma("act", lambda e, BT_=BT_, tok0=tok0: e.dma_start(out=BT3[:, :, tok0:tok0 + 128], in_=BT_.ap), BT_.b,
                  reads=[BT_.b])
            P.dma("act", lambda e, CT_=CT_, tok0=tok0: e.dma_start(out=CT3[:, :, tok0:tok0 + 128], in_=CT_.ap), CT_.b,
                  reads=[CT_.b])

        P.barrier()
        P.release([t.b for t in xc + xs_ev + bt_ev + BT_ev + CT_ev] + [sb_st.b])
        aoff[0] = layer_mark
        w_out = alloc([12, D], BF16, "w_out")
        w_out.b = P.buf("w_out_sw", sw=True)
        wo3 = W["w_out"][L].rearrange("(k p) n -> p k n", p=128)
        for k in range(12):
            P.dma("pool", lambda e, k=k: e.dma_start(out=w_out.ap[:, k, :], in_=wo3[:, k, :]), w_out.b, writes=[w_out.b])
        dsk = alloc([2 * HEADS, 128], BF16, "dsk")
        P.op("dve", lambda e: e.tensor_tensor(
            out=dsk.ap, in0=ident.ap.unsqueeze(1).broadcast_to([128, 2 * HEADS, 128]),
            in1=hrow.ap[:, 2].rearrange("p a b -> p (a b)").unsqueeze(2).broadcast_to([128, 2 * HEADS, 128]),
            op=ALU.mult), reads=[ident.b, hrow.b], writes=[dsk.b])
        NS = 3
        xs_l = [alloc([D_INNER], BF16, "xs_l%d" % i, dma=True) for i in range(NS)]
        bt_l = [alloc([512], BF16, "bt_l%d" % i, dma=True) for i in range(NS)]
        BT_l = [alloc([4, 128], BF16, "BT_l%d" % i, dma=True) for i in range(NS)]
        CT_l = [alloc([4, 128], BF16, "CT_l%d" % i, dma=True) for i in range(NS)]
        hst = alloc([D_INNER], F32, "hst")
        hbf_t = [alloc([D_INNER], BF16, "hbf%d" % i) for i in range(2)]
        htmp = alloc([D_INNER], F32, "htmp")
        dA_t = [alloc([HEADS], F32, "dA%d" % i) for i in range(NS)]
        cs_t = [alloc([HEADS], F32, "cs%d" % i) for i in range(NS)]
        cshi_t = [alloc([HEADS], BF16, "cshi%d" % i) for i in range(NS)]
        cslo_t = [alloc([HEADS], BF16, "cslo%d" % i) for i in range(NS)]
        wv_t = [alloc([HEADS], F32, "wv%d" % i) for i in range(NS)]
        cdec_t = [alloc([HEADS], F32, "cdec%d" % i) for i in range(NS)]
        E1_t = [alloc([4, 128], BF16, "E1_%d" % i) for i in range(3)]
        E2_t = [alloc([4, 128], BF16, "E2_%d" % i) for i in range(3)]
        Mo_t = [alloc([HEADS, 128], BF16, "Mo%d" % i) for i in range(NS)]
        Md_t = [alloc([HEADS, 128], BF16, "Md%d" % i) for i in range(NS)]
        cbm_t = [alloc([4, 128], BF16, "cbm%d" % i) for i in range(NS)]
        xdt_t = [alloc([D_INNER], BF16, "xdt%d" % i) for i in range(NS)]
        xw_t = [alloc([D_INNER], BF16, "xw%d" % i) for i in range(NS)]
        yfs = [alloc([D_INNER], F32, "yfs%d" % i, dma=True) for i in range(2)]
        szl = [alloc([D_INNER], F32, "szl%d" % i, dma=True) for i in range(2)]
        cvl = [alloc([4, 128], BF16, "cvl%d" % i, dma=True) for i in range(2)]
        xres = [alloc([D], F32, "xres%d" % i, dma=True) for i in range(2)]
        gbuf_t = [alloc([D_INNER], F32, "gbuf%d" % i) for i in range(2)]
        gsq = alloc([D_INNER], F32, "gsq")
        gnb = alloc([D_INNER], BF16, "gnb")
        gT = alloc([8, 128], BF16, "gT")
        ssq = alloc([4], F32, "ssq")
        rr = alloc([D], F32, "rr")
        xo = [alloc([D], F32, "xo%d" % i, dma=True) for i in range(2)]
        st12 = alloc([12], F32, "st12")
        agb = alloc([D], F32, "agb", dma=True)
        abb = alloc([D], F32, "abb", dma=True)
        g1b = alloc([D], F32, "g1b", dma=True)
        onw = PFM_OFF["normw"][0]

        def load_bc(seg, stage):
            if stage == 0:
                if L == 0:
                    P.op("pool", lambda e: e.memset(agb.ap, ALPHA), writes=[agb.b])
                    P.op("pool", lambda e: e.memset(abb.ap, 0.0), writes=[abb.b])
                    srcs = None
                else:
                    srcs = (prow_in[L - 1], PROW_OFF["ln2g"][0], PROW_OFF["ln2b"][0])
            else:
                srcs = (prow_in[L], PROW_OFF["ln1g"][0], PROW_OFF["ln1b"][0])
            if srcs is not None:
                pr, og_, ob_ = srcs
                P.dma("sp", lambda e: e.dma_start(out=agb.ap, in_=pr[:, og_:og_ + D].partition_broadcast(128)), agb.b,
                      writes=[agb.b])
                P.dma("sp", lambda e: e.dma_start(out=abb.ap, in_=pr[:, ob_:ob_ + D].partition_broadcast(128)), abb.b,
                      writes=[abb.b])
                P.op("pool", lambda e: e.tensor_scalar(out=agb.ap, in0=agb.ap, scalar1=ALPHA, scalar2=None, op0=ALU.mult),
                     reads=[agb.b], writes=[agb.b])
                P.op("pool", lambda e: e.tensor_scalar(out=abb.ap, in0=abb.ap, scalar1=ALPHA, scalar2=None, op0=ALU.mult),
                     reads=[abb.b], writes=[abb.b])
            P.dma("sp", lambda e: e.dma_start(out=g1b.ap, in_=g_d[seg, stage:stage + 1, :].partition_broadcast(128)), g1b.b,
                  writes=[g1b.b])

        def residual_pre(xr):
            P.op("pool", lambda e, xr=xr: e.tensor_tensor(out=xr.ap, in0=xr.ap, in1=agb.ap, op=ALU.mult),
                 reads=[xr.b, agb.b], writes=[xr.b])
            P.op("pool", lambda e, xr=xr: e.tensor_tensor(out=xr.ap, in0=xr.ap, in1=abb.ap, op=ALU.add),
                 reads=[xr.b, abb.b], writes=[xr.b])

        def residual_ln(pbs, xr, out_t, final=None, pre=True):
            if pre:
                residual_pre(xr)
            for half in range(2):
                sl = slice(half * 512, (half + 1) * 512)
                P.op("dve", lambda e, half=half, sl=sl: e.tensor_tensor(out=rr.ap[:, sl], in0=pbs[half].ap, in1=g1b.ap[:, sl],
                                                                        op=ALU.mult), reads=[pbs[half].b, g1b.b], writes=[rr.b])
            P.op("dve", lambda e, xr=xr: e.tensor_tensor(out=rr.ap, in0=rr.ap, in1=xr.ap, op=ALU.add),
                 reads=[rr.b, xr.b], writes=[rr.b])
            for half in range(2):
                P.op("dve", lambda e, half=half: e.bn_stats(out=st12.ap[:, half * 6:(half + 1) * 6],
                                                            in_=rr.ap[:, half * 512:(half + 1) * 512]),
                     reads=[rr.b], writes=[st12.b])
            P.op("dve", lambda e: e.bn_aggr(out=mv.ap, in_=st12.ap), reads=[st12.b], writes=[mv.b])
            rstd_from(mv.ap[:, 1:2], rs, epsln, [mv.b])
            P.op("dve", lambda e, out_t=out_t: e.tensor_scalar(
                out=out_t.ap, in0=rr.ap, scalar1=mv.ap[:, 0:1], scalar2=rs.ap[:, 0:1], op0=ALU.subtract, op1=ALU.mult),
                reads=[rr.b, mv.b, rs.b], writes=[out_t.b])
            if final is not None:
                fg, fb = final
                P.op("pool", lambda e, out_t=out_t: e.tensor_tensor(out=out_t.ap, in0=out_t.ap, in1=fg.ap, op=ALU.mult),
                     reads=[out_t.b, fg.b], writes=[out_t.b])
                P.op("pool", lambda e, out_t=out_t: e.tensor_tensor(out=out_t.ap, in0=out_t.ap, in1=fb.ap, op=ALU.add),
                     reads=[out_t.b, fb.b], writes=[out_t.b])

        st6 = alloc([6], F32, "st6b")
        mv = alloc([2], F32, "mvb")
        rs = alloc([1], F32, "rsb")
        pcs_b, pcb_b, pR, py, pst = psum[0], psum[1], [psum[2], psum[3], psum[4]], [psum[5], psum[6]], [psum[7], psum[1]]
        yf_bufs = [P.buf("yfd%d" % i) for i in range(NCH)]
        pt_b, pti = psum[7], 7
        for d in range(NDIR):
            tri, msk = (triF, maskF) if d == 0 else (triB, maskB)
            if d == 0:
                order = list(range(NCH))
            else:
                order = list(range(NCC - 1, -1, -1)) + list(range(NCH - 1, NCC - 1, -1))
            P.op("dve", lambda e: e.memset(hst.ap, 0.0), writes=[hst.b])
            for hb_ in hbf_t:
                P.op("pool", lambda e, hb_=hb_: e.memset(hb_.ap, 0.0), writes=[hb_.b])
            state = {"seg": -1, "rq": 0}

            def wanty(ch):
                return not (last and seg_of_chunk(ch) == 0)

            def stage_A_load(it, ch):
                tok0 = ch * 128
                b2 = it % NS
                xs_, bt_, BT_, CT_ = xs_l[b2], bt_l[b2], BT_l[b2], CT_l[b2]
                P.dma("sp", lambda e: e.dma_start(out=xs_.ap, in_=xs_d[tok0:tok0 + 128, :]), xs_.b, writes=[xs_.b])
                P.dma("sp", lambda e: e.dma_start(out=bt_.ap, in_=bt_d[tok0:tok0 + 128, :]), bt_.b, writes=[bt_.b])
                if wanty(ch):
                    P.dma("sp", lambda e: e.dma_start(out=BT_.ap, in_=BT3[:, :, tok0:tok0 + 128]), BT_.b, writes=[BT_.b])
                    P.dma("sp", lambda e: e.dma_start(out=CT_.ap, in_=CT3[:, :, tok0:tok0 + 128]), CT_.b, writes=[CT_.b])

            def stage_A(it, ch, d=d, tri=tri, msk=msk):
                tok0 = ch * 128
                b2 = it % NS
                want_y = wanty(ch)
                xs_, bt_, BT_, CT_ = xs_l[b2], bt_l[b2], BT_l[b2], CT_l[b2]
                dA, cs, wv, cdec, xdt, xw, cbm, Mo, Md = (dA_t[b2], cs_t[b2], wv_t[b2], cdec_t[b2], xdt_t[b2], xw_t[b2],
                                                         cbm_t[b2], Mo_t[b2], Md_t[b2])
                cshi, cslo = cshi_t[b2], cslo_t[b2]
                dtv = dt_sb.ap[:, ch, d, :]
                P.op("dve", lambda e: e.tensor_tensor(out=dA.ap, in0=dtv, in1=negA.ap[:, d, :], op=ALU.mult),
                     reads=[dt_sb.b, negA.b], writes=[dA.b])
                P.op("pe", lambda e: e.matmul(pcs_b.ap[:, 0:16], lhsT=tri.ap, rhs=dA.ap, start=True, stop=True),
                     reads=[tri.b, dA.b], writes=[pcs_b.b])
                P.op("pe", lambda e: e.matmul(pcs_b.ap[:, 16:32], lhsT=triF.ap[:, 127:128].broadcast_to([128, 128]),
                                              rhs=dA.ap, start=True, stop=True), reads=[triF.b, dA.b], writes=[pcs_b.b])
                P.op("dve", lambda e: e.tensor_scalar(out=cs.ap, in0=pcs_b.ap[:, 0:16], scalar1=-1.0, scalar2=None, op0=ALU.mult),
                     reads=[pcs_b.b], writes=[cs.b])
                P.op("dve", lambda e: e.tensor_tensor(out=wv.ap, in0=pcs_b.ap[:, 16:32], in1=cs.ap, op=ALU.add),
                     reads=[pcs_b.b, cs.b], writes=[wv.b])
                P.op("dve", lambda e: e.tensor_copy(out=cshi.ap, in_=cs.ap), reads=[cs.b], writes=[cshi.b])
                P.op("dve", lambda e: e.tensor_tensor(out=cslo.ap, in0=cs.ap, in1=cshi.ap, op=ALU.subtract),
                     reads=[cs.b, cshi.b], writes=[cslo.b])
                P.op("act", lambda e: e.activation(out=cdec.ap, in_=pcs_b.ap[:, 16:32], func=AF.Exp),
                     reads=[pcs_b.b], writes=[cdec.b])
                P.op("act", lambda e: e.activation(out=wv.ap, in_=wv.ap, func=AF.Exp), reads=[wv.b], writes=[wv.b])
                P.op("dve", lambda e: e.tensor_tensor(out=wv.ap, in0=wv.ap, in1=dtv, op=ALU.mult),
                     reads=[wv.b, dt_sb.b], writes=[wv.b])
                yield
                xs3 = xs_.ap.rearrange("p (h q) -> p h q", q=HP)
                P.op("pool", lambda e: e.tensor_tensor(
                    out=xw.ap.rearrange("p (h q) -> p h q", q=HP), in0=xs3,
                    in1=wv.ap.unsqueeze(2).broadcast_to([128, HEADS, HP]), op=ALU.mult),
                    reads=[xs_.b, wv.b], writes=[xw.b])
                if not want_y:
                    return
                P.op("dve", lambda e: e.tensor_tensor(
                    out=xdt.ap.rearrange("p (h q) -> p h q", q=HP), in0=xs3,
                    in1=dtv.unsqueeze(2).broadcast_to([128, HEADS, HP]), op=ALU.mult),
                    reads=[xs_.b, dt_sb.b], writes=[xdt.b])
                for g in range(4):
                    P.op("pe", lambda e, g=g: e.matmul(
                        pcb_b.ap[:, g * 128:(g + 1) * 128], lhsT=BT_.ap[:, g, :], rhs=CT_.ap[:, g, :], start=True, stop=True),
                        reads=[BT_.b, CT_.b], writes=[pcb_b.b])
                P.op("dve", lambda e: e.tensor_copy(out=cbm.ap.rearrange("p g l -> p (g l)"), in_=pcb_b.ap),
                     reads=[pcb_b.b], writes=[cbm.b])
                qinfo = []

                def part2(q, pb, E2):
                    P.op("pe", lambda e: e.matmul(pb.ap, lhsT=identb.ap, rhs=msk.ap.rearrange("p g l -> p (g l)"),
                                                  start=False, stop=False, skip_group_check=True),
                         reads=[identb.b, msk.b], writes=[pb.b])
                    P.op("pe", lambda e: e.matmul(pb.ap, lhsT=identb.ap,
                                                  rhs=cshi.ap[:, q * 4:(q + 1) * 4].unsqueeze(2).broadcast_to([128, 4, 128]),
                                                  start=False, stop=False, skip_group_check=True),
                         reads=[identb.b, cshi.b], writes=[pb.b])
                    P.op("pe", lambda e: e.matmul(pb.ap, lhsT=identb.ap,
                                                  rhs=cslo.ap[:, q * 4:(q + 1) * 4].unsqueeze(2).broadcast_to([128, 4, 128]),
                                                  start=False, stop=True, skip_group_check=True),
                         reads=[identb.b, cslo.b], writes=[pb.b])
                    P.op("act", lambda e: e.activation(out=E2.ap.rearrange("p r l -> p (r l)"), in_=pb.ap, func=AF.Exp),
                         reads=[pb.b], writes=[E2.b])
                    P.op("dve", lambda e: e.tensor_tensor(
                        out=Md.ap[:, q * 4:(q + 1) * 4, :], in0=E2.ap,
                        in1=cbm.ap[:, q:q + 1, :].broadcast_to([128, 4, 128]), op=ALU.mult),
                        reads=[E2.b, cbm.b], writes=[Md.b])

                for q in range(4):
                    state["rq"] += 1
                    pb = pR[state["rq"] % 3]
                    E1, E2 = E1_t[state["rq"] % 3], E2_t[state["rq"] % 3]
                    for r in range(4):
                        h = q * 4 + r
                        P.op("pe", lambda e, pb=pb, h=h, r=r: e.matmul(
                            pb.ap[:, r * 128:(r + 1) * 128], lhsT=dA.ap[:, h:h + 1].broadcast_to([128, 128]), rhs=tri.ap,
                            start=(r == 0), stop=False, skip_group_check=True), reads=[dA.b, tri.b], writes=[pb.b])
                    P.op("act", lambda e, pb=pb, E1=E1: e.activation(out=E1.ap.rearrange("p r l -> p (r l)"), in_=pb.ap,
                                                                    func=AF.Exp), reads=[pb.b], writes=[E1.b])
                    P.op("pool", lambda e, E1=E1, q=q: e.tensor_tensor(
                        out=Mo.ap[:, q * 4:(q + 1) * 4, :], in0=E1.ap,
                        in1=CT_.ap[:, q:q + 1, :].broadcast_to([128, 4, 128]), op=ALU.mult),
                        reads=[E1.b, CT_.b], writes=[Mo.b])
                    if qinfo:
                        part2(*qinfo.pop())
                    qinfo.append((q, pb, E2))
                    yield
                part2(*qinfo.pop())

            def stage_B(it, ch, d=d):
                tok0 = ch * 128
                b2 = it % NS
                want_y = wanty(ch)
                xs_, bt_ = xs_l[b2], bt_l[b2]
                cdec, xdt, xw, Mo, Md = cdec_t[b2], xdt_t[b2], xw_t[b2], Mo_t[b2], Md_t[b2]
                htm = htmp
                hb_prev, hb_new = hbf_t[(it + 1) % 2], hbf_t[it % 2]
                for g in range(4):
                    pb = pst[g // 2]
                    P.op("pe", lambda e, pb=pb, g=g: e.matmul(
                        pb.ap[:, (g % 2) * 256:(g % 2 + 1) * 256], lhsT=bt_.ap[:, g * 128:(g + 1) * 128],
                        rhs=xw.ap[:, g * 256:(g + 1) * 256], start=True, stop=True), reads=[bt_.b, xw.b], writes=[pb.b])
                P.op("pool", lambda e: e.tensor_tensor(
                    out=htm.ap.rearrange("p (h q) -> p h q", q=HP), in0=hst.ap.rearrange("p (h q) -> p h q", q=HP),
                    in1=cdec.ap.unsqueeze(2).broadcast_to([128, HEADS, HP]), op=ALU.mult),
                    reads=[hst.b, cdec.b], writes=[htm.b])
                if want_y:
                    for h in range(HEADS):
                        pyb = py[h // 8]
                        osl = slice((h % 8) * HP, (h % 8 + 1) * HP)
                        P.op("pe", lambda e, pyb=pyb, h=h, osl=osl: e.matmul(
                            pyb.ap[:, osl], lhsT=Md.ap[:, h, :], rhs=xdt.ap[:, h * HP:(h + 1) * HP],
                            start=(h % 8 == 0), stop=False, skip_group_check=True), reads=[Md.b, xdt.b], writes=[pyb.b])
                        P.op("pe", lambda e, pyb=pyb, h=h, osl=osl: e.matmul(
                            pyb.ap[:, osl], lhsT=dsk.ap[:, d * HEADS + h, :], rhs=xs_.ap[:, h * HP:(h + 1) * HP],
                            start=False, stop=False, skip_group_check=True), reads=[dsk.b, xs_.b], writes=[pyb.b])
                    for h in range(HEADS):
                        pyb = py[h // 8]
                        osl = slice((h % 8) * HP, (h % 8 + 1) * HP)
                        P.op("pe", lambda e, pyb=pyb, h=h, osl=osl: e.matmul(
                            pyb.ap[:, osl], lhsT=Mo.ap[:, h, :], rhs=hb_prev.ap[:, h * HP:(h + 1) * HP],
                            start=False, stop=True, skip_group_check=True), reads=[Mo.b, hb_prev.b], writes=[pyb.b])
                yield
                for half in range(2):
                    sl = slice(half * 512, (half + 1) * 512)
                    P.op("dve", lambda e, half=half, sl=sl: e.tensor_tensor(
                        out=hst.ap[:, sl], in0=htm.ap[:, sl], in1=pst[half].ap, op=ALU.add),
                        reads=[htm.b, pst[half].b], writes=[hst.b])
                yield
                P.op("act", lambda e: e.activation(out=hb_new.ap, in_=hst.ap, func=AF.Identity), reads=[hst.b],
                     writes=[hb_new.b])
                yield
                if not want_y:
                    return
                c2 = it % 2
                yf = yfs[c2]
                if d == 0:
                    P.op("act", lambda e: e.activation(out=yf.ap[:, 0:512], in_=py[0].ap, func=AF.Identity),
                         reads=[py[0].b], writes=[yf.b])
                    P.op("dve", lambda e: e.tensor_copy(out=yf.ap[:, 512:1024], in_=py[1].ap), reads=[py[1].b], writes=[yf.b])
                    P.dma("act", lambda e: e.dma_start(out=yf_d[tok0:tok0 + 128, :], in_=yf.ap), yf.b, reads=[yf.b],
                          writes=[yf_bufs[ch]])
                    return
                seg = seg_of_chunk(ch)
                sz, cv, xr, gbuf = szl[c2], cvl[c2], xres[c2], gbuf_t[c2]
                P.dma("sp", lambda e: e.dma_start(out=yf.ap, in_=yf_d[tok0:tok0 + 128, :]), yf.b, writes=[yf.b],
                      reads=[yf_bufs[ch]])
                P.dma("sp", lambda e: e.dma_start(out=sz.ap, in_=siluz_d[tok0:tok0 + 128, :]), sz.b, writes=[sz.b])
                P.dma("sp", lambda e: e.dma_start(out=cv.ap, in_=cvT3[:, :, tok0:tok0 + 128]), cv.b, writes=[cv.b])
                P.dma("sp", lambda e: e.dma_start(out=xr.ap, in_=stream_rows(tok0, 128)), xr.b, writes=[xr.b])
                for half in range(2):
                    sl = slice(half * 512, (half + 1) * 512)
                    P.op("dve", lambda e, sl=sl, half=half: e.tensor_tensor(
                        out=gbuf.ap[:, sl], in0=py[half].ap, in1=yf.ap[:, sl], op=ALU.add),
                        reads=[py[half].b, yf.b], writes=[gbuf.b])
                if dbg:
                    P.dma("sp", lambda e: e.dma_start(out=dbg_y[tok0:tok0 + 128, :], in_=gbuf.ap), sz.b, reads=[gbuf.b])
                P.op("pool", lambda e: e.tensor_tensor(out=gbuf.ap, in0=gbuf.ap, in1=sz.ap, op=ALU.mult),
                     reads=[gbuf.b, sz.b], writes=[gbuf.b])

            def stage_C2(it, ch):
                tok0 = ch * 128
                b2 = it % 2
                seg = seg_of_chunk(ch)
                if seg != state["seg"]:
                    load_bc(seg, 0)
                    state["seg"] = seg
                cv, xr, gbuf, xo_ = cvl[b2], xres[b2], gbuf_t[b2], xo[b2]
                residual_pre(xr)
                for g in range(4):
                    P.op("act", lambda e, g=g: e.activation(out=gsq.ap[:, g * 256:(g + 1) * 256], in_=gbuf.ap[:, g * 256:(g + 1) * 256],
                                                            func=AF.Square, accum_out=ssq.ap[:, g:g + 1]),
                         reads=[gbuf.b], writes=[gsq.b, ssq.b])
                rstd_from(ssq.ap, ssq, epsrms, [ssq.b], scale=1.0 / 256.0)
                yield
                P.op("dve", lambda e: e.tensor_tensor(
                    out=gnb.ap.rearrange("p (g q) -> p g q", q=256), in0=gbuf.ap.rearrange("p (g q) -> p g q", q=256),
                    in1=ssq.ap.unsqueeze(2).broadcast_to([128, 4, 256]), op=ALU.mult), reads=[gbuf.b, ssq.b], writes=[gnb.b])
                yield
                po = [py[0], py[1]]
                for k in range(8):
                    P.op("pe", lambda e, k=k: e.transpose(out=psb(pti)[:, k * 128:(k + 1) * 128],
                                                          in_=gnb.ap[:, k * 128:(k + 1) * 128], identity=identb.ap),
                         reads=[gnb.b, identb.b], writes=[pt_b.b])
                yield
                for k in range(8):
                    P.op("act", lambda e, k=k: e.activation(out=gT.ap[:, k, :], in_=psb(pti)[:, k * 128:(k + 1) * 128],
                                                            func=AF.Identity, scale=pf.ap[:, onw + k:onw + k + 1]),
                         reads=[pt_b.b, pf.b], writes=[gT.b])
                yield
                for k in range(12):
                    for half in range(2):
                        lhs = cv.ap[:, k, :] if k < 4 else gT.ap[:, k - 4, :]
                        lb = cv.b if k < 4 else gT.b
                        P.op("pe", lambda e, half=half, k=k, lhs=lhs: e.matmul(
                            po[half].ap, lhsT=lhs, rhs=w_out.ap[:, k, half * 512:(half + 1) * 512],
                            start=(k == 0), stop=(k == 11), skip_group_check=True), reads=[lb, w_out.b], writes=[po[half].b])
                yield
                residual_ln(po, xr, xo_, pre=False)
                P.dma("act", lambda e: e.dma_start(out=sA[tok0:tok0 + 128, :], in_=xo_.ap), xo_.b, reads=[xo_.b])
                if dbg:
                    P.dma("sp", lambda e: e.dma_start(out=dbg_r[tok0:tok0 + 128, :], in_=rr.ap), xo_.b, reads=[rr.b])

            n = len(order)

            def run(g):
                for _ in g:
                    pass

            def step(g):
                if g is not None:
                    next(g, None)

            for i0 in range(min(3, n)):
                stage_A_load(i0, order[i0])
            run(stage_A(0, order[0]))
            if n > 1:
                run(stage_A(1, order[1]))
            for it in range(n):
                gens = [stage_B(it, order[it])]
                if it + 2 < n:
                    gens.append(stage_A(it + 2, order[it + 2]))
                if d == 1 and it >= 1 and wanty(order[it - 1]):
                    gens.append(stage_C2(it - 1, order[it - 1]))
                while gens:
                    for g in list(gens):
                        try:
                            next(g)
                        except StopIteration:
                            gens.remove(g)
                if it + 3 < n:
                    stage_A_load(it + 3, order[it + 3])
            if d == 1 and wanty(order[n - 1]):
                run(stage_C2(n - 1, order[n - 1]))

        P.barrier()
        P.release([t.b for t in xs_l + bt_l + BT_l + CT_l + yfs + szl + cvl + xres + xo] + [w_out.b, agb.b, abb.b, g1b.b])
        aoff[0] = layer_mark
        w1 = alloc([8, D_FF], BF16, "w1")
        w13 = W["w1"][L].rearrange("(k p) n -> p k n", p=128)
        w1_pc = [P.buf("w1_pc%d" % i, sw=True) for i in range(8)]
        for wpc in range(8):
            P.dma("pool", lambda e, wpc=wpc: e.dma_start(out=w1.ap[:, :, wpc * 512:(wpc + 1) * 512],
                                                         in_=w13[:, :, wpc * 512:(wpc + 1) * 512]), w1_pc[wpc], writes=[w1_pc[wpc]])
        w2 = alloc([32, D], BF16, "w2")
        w23 = W["w2"][L].rearrange("(k p) n -> p k n", p=128)
        w2_pc = [P.buf("w2_pc%d" % i, sw=True) for i in range(8)]
        for wpc in range(8):
            for hf in range(2):
                P.dma("pool", lambda e, wpc=wpc, hf=hf: e.dma_start(out=w2.ap[:, wpc * 4:(wpc + 1) * 4, hf * 512:(hf + 1) * 512],
                                                                   in_=w23[:, wpc * 4:(wpc + 1) * 4, hf * 512:(hf + 1) * 512]),
                      w2_pc[wpc], writes=[w2_pc[wpc]])
        agb = alloc([D], F32, "agb3", dma=True)
        abb = alloc([D], F32, "abb3", dma=True)
        g1b = alloc([D], F32, "g1b3", dma=True)
        xin3 = [alloc([2, D], F32, "xin3_%d" % i, dma=True) for i in range(2)]
        h2T = [alloc([8, TB], BF16, "h2T0")] * 2
        hid = alloc([32, TB], BF16, "hid")
        rtmp = [alloc([TB], F32, "rtmp%d" % i) for i in range(2)]
        rr = alloc([D], F32, "rr3")
        xo3 = [alloc([D], F32, "xo3_0", dma=True)] * 2
        st12 = alloc([12], F32, "st12_3")
        mv = alloc([2], F32, "mv3")
        rs = alloc([1], F32, "rs3")
        fin = None
        if last:
            fg = alloc([D], F32, "fg", dma=True)
            fb = alloc([D], F32, "fb", dma=True)
            o1_, o2_ = PROW_OFF["ln2g"][0], PROW_OFF["ln2b"][0]
            P.dma("sp", lambda e: e.dma_start(out=fg.ap, in_=prow_in[L][:, o1_:o1_ + D].partition_broadcast(128)), fg.b,
                  writes=[fg.b])
            P.dma("sp", lambda e: e.dma_start(out=fb.ap, in_=prow_in[L][:, o2_:o2_ + D].partition_broadcast(128)), fb.b,
                  writes=[fb.b])
            fin = (fg, fb)
        cur_seg = -1
        blk0 = (CTX // TB) if last else 0
        for bi in range(blk0, nblk):
            tok0 = bi * TB
            seg = 0 if tok0 < CTX else 1
            if seg != cur_seg:
                load_bc(seg, 1)
                cur_seg = seg
            xi, h = xin3[bi % 2], h2T[bi % 2]
            for nb in ([blk0, blk0 + 1] if bi == blk0 else [bi + 1]):
                if nb < nblk:
                    xn = xin3[nb % 2]
                    P.dma("sp", lambda e, xn=xn, nb=nb: e.dma_start(
                        out=xn.ap, in_=sA[nb * TB:(nb + 1) * TB, :].rearrange("(j p) d -> p j d", p=128)), xn.b, writes=[xn.b])
            for k in range(8):
                pb = nextps()
                for j in range(2):
                    P.op("pe", lambda e, pb=pb, xi=xi, j=j, k=k: e.transpose(
                        out=pb.ap[:, j * 128:(j + 1) * 128], in_=xi.ap[:, j, k * 128:(k + 1) * 128], identity=ident.ap),
                        reads=[xi.b, ident.b], writes=[pb.b])
                P.op("act", lambda e, pb=pb, h=h, k=k, seg=seg: e.activation(
                    out=h.ap[:, k, :], in_=pb.ap[:, 0:TB], func=AF.Identity,
                    scale=AB.ap[:, 1, k, seg:seg + 1], bias=AB.ap[:, 1, k, 2 + seg:3 + seg]),
                    reads=[pb.b, AB.b], writes=[h.b])
            for f in range(32):
                pb = nextps()
                for k in range(8):
                    P.op("pe", lambda e, pb=pb, k=k, f=f, h=h: e.matmul(
                        pb.ap[:, 0:TB], lhsT=w1.ap[:, k, f * 128:(f + 1) * 128], rhs=h.ap[:, k, :],
                        start=(k == 0), stop=(k == 7)), reads=[w1_pc[f // 4], h.b], writes=[pb.b])
                rt = rtmp[f % 2]
                if f % 2 == 0:
                    P.op("act", lambda e, pb=pb, rt=rt: e.activation(out=rt.ap, in_=pb.ap[:, 0:TB], func=AF.Relu),
                         reads=[pb.b], writes=[rt.b])
                else:
                    P.op("dve", lambda e, pb=pb, rt=rt: e.tensor_scalar(out=rt.ap, in0=pb.ap[:, 0:TB], scalar1=0.0, scalar2=None,
                                                                        op0=ALU.max), reads=[pb.b], writes=[rt.b])
                P.op("pool", lambda e, f=f, rt=rt: e.tensor_tensor(out=hid.ap[:, f, :], in0=rt.ap, in1=rt.ap, op=ALU.mult),
                     reads=[rt.b], writes=[hid.b])
            for j in range(2):
                po = [nextps(), nextps()]
                for half in range(2):
                    for f in range(32):
                        P.op("pe", lambda e, half=half, f=f, j=j, po=po: e.matmul(
                            po[half].ap, lhsT=hid.ap[:, f, j * 128:(j + 1) * 128], rhs=w2.ap[:, f, half * 512:(half + 1) * 512],
                            start=(f == 0), stop=(f == 31)), reads=[hid.b, w2_pc[f // 4]], writes=[po[half].b])
                xo_ = xo3[j]
                xr = Tl(xi.ap[:, j, :], xi.b)
                residual_ln(po, xr, xo_, final=fin)
                t0_ = tok0 + j * 128
                if last:
                    P.dma("sp", lambda e, xo_=xo_, t0_=t0_: e.dma_start(out=y_out[t0_ - CTX:t0_ - CTX + 128, :], in_=xo_.ap),
                          xo_.b, reads=[xo_.b])
                else:
                    P.dma("sp", lambda e, xo_=xo_, t0_=t0_: e.dma_start(out=sB[t0_:t0_ + 128, :], in_=xo_.ap), xo_.b,
                          reads=[xo_.b])
        P.barrier()
        rel = w1_pc + w2_pc + [agb.b, abb.b, g1b.b, pf.b, hrow.b] + [t.b for t in xin3 + xo3[:1]]
        if last:
            rel += [fin[0].b, fin[1].b]
        P.release(rel)


_CACHE = {}


def kernel(**inp):
    inp = {k: np.asarray(v) for k, v in inp.items()}
    B, SEQ, _ = inp["x"].shape
    CTX = inp["ctx"].shape[1]
    depth = inp["w_in"].shape[0]
    key = (SEQ, CTX, depth)
    if key not in _CACHE:
        _CACHE[key] = build(SEQ, CTX, depth)
    nc, _P = _CACHE[key]
    packs = [pack_params(inp, L) for L in range(depth)]
    pfm = np.stack([p[0] for p in packs]).astype(np.float32)
    prow = np.stack([p[1] for p in packs]).astype(np.float32)
    shared = {k: np.ascontiguousarray(inp[k], dtype=np.float32) for k in ("w_mod", "w_in", "w_out", "w1", "w2")}
    in_maps = []
    for b in range(B):
        cf = np.stack([_fm(inp["c_ctx"]), _fm(inp["c"][b])], axis=-1).reshape(128, 16).astype(np.float32)
        m = {"x": np.ascontiguousarray(inp["x"][b], dtype=np.float32),
             "ctx": np.ascontiguousarray(inp["ctx"][b], dtype=np.float32),
             "c_fm": np.ascontiguousarray(cf), "pfm": pfm, "prow": prow}
        m.update(shared)
        in_maps.append(m)
    res = run_bass_kernel_spmd(nc, in_maps, core_ids=list(range(B)))
    return np.stack([np.asarray(r["y"], dtype=np.float32) for r in res.results], axis=0)
```

```python
import contextlib
import numpy as np
import concourse.bass as bass
import concourse.mybir as mybir
from concourse.bass_utils import run_bass_kernel_spmd

F32 = mybir.dt.float32
BF16 = mybir.dt.bfloat16
AF = mybir.ActivationFunctionType
ALU = mybir.AluOpType

D = 1024
DEPTH = 2
GRID_W = 64
CONV_DIM = 512
CONV_K = 31
D_INNER = 1024
HEADS = 16
HP = 64
GROUPS = 4
NST = 128
SK = 5
D_XBC = 2048
D_PROJ = 4112
D_FF = 4096
ALPHA = float((2 * DEPTH) ** 0.25)
LN_EPS = 1e-5
RMS_EPS = 1e-5
NEG = -30000.0

ENGS = ("pe", "act", "dve", "pool", "sp")
import os
MENG = os.environ.get("K_MENG", "pool")
NDIR = int(os.environ.get("K_NDIR", "2"))
NOC2 = int(os.environ.get("K_NOC2", "0"))


class Buf:
    __slots__ = ("name", "w", "r", "sem", "excl")

    def __init__(self, name):
        self.name = name
        self.w = None
        self.r = []
        self.sem = None
        self.excl = False


class Op:
    __slots__ = ("eng", "fn", "deps", "is_dma", "sem", "semval", "sig", "sigval", "idx", "eidx", "waits")

    def __init__(self, eng, fn):
        self.eng = eng
        self.fn = fn
        self.deps = []
        self.is_dma = False
        self.sem = None
        self.semval = 0
        self.sig = False
        self.sigval = 0
        self.idx = 0
        self.eidx = 0
        self.waits = []


class Prog:
    def __init__(self, nc, n_hw_sems=60, n_sw_sems=28, plan=None):
        self.nc = nc
        self.plan = plan
        self.ops = []
        self.eng_ops = {e: [] for e in ENGS}
        n_dma_sems = n_hw_sems + n_sw_sems
        self.n_dma_sems = n_dma_sems
        self.dma_cnt = [0] * n_dma_sems
        self.dma_free = list(range(n_hw_sems))
        self.sw_free = list(range(n_hw_sems, n_dma_sems))
        self.sw_map = {}
        self.all_bufs = []
        self.n = 0
        if plan is not None:
            self.engobj = {"pe": nc.tensor, "act": nc.scalar, "dve": nc.vector, "pool": nc.gpsimd, "sp": nc.sync}
            self.stack = contextlib.ExitStack()
            self.esem = {e: self.stack.enter_context(nc.semaphore("s_" + e)) for e in ENGS}
            self.dsem = [self.stack.enter_context(nc.semaphore("d%d" % i)) for i in range(n_dma_sems)]

    def buf(self, name, dma=False, sw=False):
        b = Buf(name)
        if sw:
            if name not in self.sw_map:
                assert self.sw_free, "out of sw dma semaphores"
                self.sw_map[name] = self.sw_free.pop(0)
            b.sem = self.sw_map[name]
        elif dma:
            assert self.dma_free, "out of dma semaphores"
            b.sem = self.dma_free.pop(0)
        if self.plan is None:
            self.all_bufs.append(b)
        return b

    def release(self, bufs):
        for b in bufs:
            if b.sem is not None:
                if b.sem not in self.sw_map.values():
                    self.dma_free.append(b.sem)
                b.sem = None

    def _emit(self, eng_name, fn, is_dma):
        rec = self.plan.ops[self.n]
        self.n += 1
        assert rec.eng == eng_name and rec.is_dma == is_dma, (rec.eng, eng_name)
        eng = self.engobj[eng_name]
        for kind, d in rec.waits:
            if kind == "dma":
                eng.wait_ge(self.dsem[d.sem], d.semval)
            else:
                eng.wait_ge(self.esem[d.eng], d.sigval)
        if fn is None:
            if rec.sig:
                eng.nop().then_inc(self.esem[eng_name], 1)
            return
        ins = fn(eng)
        if is_dma:
            ins.then_inc(self.dsem[rec.sem], 16)
        elif rec.sig:
            ins.then_inc(self.esem[eng_name], 1)

    def _add(self, op, reads, writes):
        deps = []
        for b in reads:
            if b.w is not None:
                deps.append(b.w)
            if b.excl:
                deps.extend(r for r in b.r if r.eng != op.eng)
        for b in writes:
            if b.w is not None:
                deps.append(b.w)
            deps.extend(b.r)
        for b in reads:
            b.r.append(op)
        for b in writes:
            b.w = op
            b.r = []
        seen = set()
        for d in deps:
            if d is op or id(d) in seen:
                continue
            seen.add(id(d))
            op.deps.append(d)
        op.idx = len(self.ops)
        op.eidx = len(self.eng_ops[op.eng])
        self.ops.append(op)
        self.eng_ops[op.eng].append(op)
        return op

    def op(self, eng, fn, reads=(), writes=()):
        if self.plan is not None:
            return self._emit(eng, fn, False)
        return self._add(Op(eng, None), reads, writes)

    def dma(self, eng, fn, sbuf, reads=(), writes=()):
        if self.plan is not None:
            return self._emit(eng, fn, True)
        o = Op(eng, None)
        o.is_dma = True
        assert sbuf.sem is not None, sbuf.name
        o.sem = sbuf.sem
        self.dma_cnt[o.sem] += 16
        o.semval = self.dma_cnt[o.sem]
        return self._add(o, reads, writes)

    def barrier(self):
        if self.plan is not None:
            for e in ENGS:
                self._emit(e, None, False)
            return
        last = [ops[-1] for ops in self.eng_ops.values() if ops]
        latest = {}
        for o in self.ops:
            if o.is_dma:
                latest[o.sem] = o
        deps = last + list(latest.values())
        for e in ENGS:
            o = Op(e, None)
            o.deps = list(deps)
            o.idx = len(self.ops)
            o.eidx = len(self.eng_ops[e])
            self.ops.append(o)
            self.eng_ops[e].append(o)
        for b in self.all_bufs:
            b.w = None
            b.r = []

    def analyze(self):
        seen_eng = {e: {f: 0 for f in ENGS} for e in ENGS}
        seen_dma = {e: [0] * self.n_dma_sems for e in ENGS}
        for o in self.ops:
            e = o.eng
            for d in o.deps:
                if d.is_dma:
                    if seen_dma[e][d.sem] >= d.semval:
                        continue
                    seen_dma[e][d.sem] = d.semval
                    o.waits.append(("dma", d))
                else:
                    if d.eng == e and e != "pool" and (o.eidx - d.eidx > 2 or e == "pe" or e == "sp"):
                        continue
                    if seen_eng[e][d.eng] >= d.eidx + 1:
                        continue
                    seen_eng[e][d.eng] = d.eidx + 1
                    d.sig = True
                    o.waits.append(("eng", d))
            o.deps = None
        cnt = {e: 0 for e in ENGS}
        for o in self.ops:
            if o.sig:
                cnt[o.eng] += 1
                o.sigval = cnt[o.eng]
        self.sig_counts = cnt

    def finish(self):
        assert self.n == len(self.plan.ops), (self.n, len(self.plan.ops))
        for e in ENGS:
            eng = self.engobj[e]
            for i in range(self.n_dma_sems):
                if self.plan.dma_cnt[i] > 0:
                    eng.wait_ge(self.dsem[i], self.plan.dma_cnt[i])
        self.stack.close()


class Tl:
    __slots__ = ("ap", "b")

    def __init__(self, ap, b):
        self.ap = ap
        self.b = b


PFM_OFF = {}
_o = 0
for _n, _w in (("bmod", 48), ("convb", 4), ("lng", 4), ("lnb", 4), ("sconvb", 16), ("normw", 8),
               ("ln1g", 8), ("ln1b", 8), ("ln2g", 8), ("ln2b", 8), ("convw", 4 * CONV_K), ("sconvw", 16 * SK)):
    PFM_OFF[_n] = (_o, _w)
    _o += _w
NPF = _o
PROW_OFF = {}
_o = 0
for _n, _w in (("bmod_g1", 1024), ("bmod_g2", 1024), ("convb", 512), ("sconvb", 1536), ("ln1g", 1024), ("ln1b", 1024),
               ("ln2g", 1024), ("ln2b", 1024), ("dtb", 32), ("alog", 32), ("dskip", 32)):
    PROW_OFF[_n] = (_o, _w)
    _o += _w
NPR = _o


def _fm(v):
    return np.ascontiguousarray(v.reshape(-1, 128).T)


def pack_params(inp, L):
    pf = np.zeros((128, NPF), np.float32)

    def put(name, arr):
        o, w = PFM_OFF[name]
        assert arr.shape == (128, w), (name, arr.shape)
        pf[:, o:o + w] = arr
    put("bmod", _fm(inp["b_mod"][L]))
    put("convb", _fm(inp["conv_b"][L]))
    put("lng", _fm(inp["conv_ln_g"][L]))
    put("lnb", _fm(inp["conv_ln_b"][L]))
    put("sconvb", _fm(inp["ssm_conv_b"][L]))
    put("normw", _fm(inp["ssm_norm_w"][L]))
    put("ln1g", _fm(inp["ln1_g"][L]))
    put("ln1b", _fm(inp["ln1_b"][L]))
    put("ln2g", _fm(inp["ln2_g"][L]))
    put("ln2b", _fm(inp["ln2_b"][L]))
    cw = inp["conv_w"][L]
    put("convw", np.ascontiguousarray(cw.reshape(CONV_K, 4, 128).transpose(2, 1, 0)).reshape(128, 4 * CONV_K))
    sw = inp["ssm_conv_w"][L]
    put("sconvw", np.ascontiguousarray(sw.reshape(SK, 16, 128).transpose(2, 1, 0)).reshape(128, 16 * SK))
    pr = np.zeros((1, NPR), np.float32)

    def putr(name, arr):
        o, w = PROW_OFF[name]
        pr[0, o:o + w] = arr.reshape(-1)
    putr("bmod_g1", inp["b_mod"][L][2048:3072])
    putr("bmod_g2", inp["b_mod"][L][5120:6144])
    putr("convb", inp["conv_b"][L])
    putr("sconvb", inp["ssm_conv_b"][L][:1536])
    putr("ln1g", inp["ln1_g"][L])
    putr("ln1b", inp["ln1_b"][L])
    putr("ln2g", inp["ln2_g"][L])
    putr("ln2b", inp["ln2_b"][L])
    putr("dtb", inp["dt_bias"][L])
    putr("alog", inp["a_log"][L])
    putr("dskip", inp["d_skip"][L])
    return pf, pr


def build(SEQ, CTX, depth=DEPTH, dbg=False):
    nc = bass.Bass("TRN2", target_bir_lowering=False)
    TALL = CTX + SEQ
    NCC = CTX // 128
    NLC = SEQ // 128
    NCH = NCC + NLC
    TB = 256
    assert CTX % TB == 0 and SEQ % TB == 0

    def dram_in(name, shape, dt=F32):
        return nc.dram_tensor(name, list(shape), dt, kind="ExternalInput").ap()

    def dram_tmp(name, shape, dt=F32):
        if dbg:
            return nc.dram_tensor(name, list(shape), dt, kind="ExternalOutput").ap()
        return nc.dram_tensor(name, list(shape), dt).ap()

    x_in = dram_in("x", [SEQ, D])
    ctx_in = dram_in("ctx", [CTX, D])
    c_fm = dram_in("c_fm", [128, 16])
    W = {}
    for nm, shp in (("w_mod", [depth, D, 6 * D]), ("w_in", [depth, D, D_PROJ]), ("w_out", [depth, 1536, D]),
                    ("w1", [depth, D, D_FF]), ("w2", [depth, D_FF, D])):
        W[nm] = dram_in(nm, shp)
    pfm_in = dram_in("pfm", [depth, 128, NPF])
    prow_in = dram_in("prow", [depth, 1, NPR])
    y_out = nc.dram_tensor("y", [SEQ, D], F32, kind="ExternalOutput").ap()

    sA = dram_tmp("sA", [TALL, D])
    sB = dram_tmp("sB", [TALL, D])
    TP = TALL + 8
    xbcT = dram_tmp("xbcT", [D_XBC, TP], BF16)
    convT_d = dram_tmp("convT", [CONV_DIM, TALL], BF16)
    siluz_d = dram_tmp("siluz", [TALL, D_INNER])
    xs_d = dram_tmp("xs_tok", [TALL, D_INNER], BF16)
    bt_d = dram_tmp("b_tok", [TALL, 512], BF16)
    BT_d = dram_tmp("BT", [512, TALL], BF16)
    CT_d = dram_tmp("CT", [512, TALL], BF16)
    yf_d = dram_tmp("y_f", [TALL, D_INNER])
    g_d = dram_tmp("grow", [2, 2, D])
    dbg_y = dram_tmp("dbg_y", [TALL, D_INNER]) if dbg else None
    dbg_r = dram_tmp("dbg_r", [TALL, D]) if dbg else None

    def seg_of_chunk(ch):
        return 0 if ch < NCC else 1

    def xbc_col(tok):
        return tok + 2 if tok < CTX else tok + 6

    ARENA_W = 53000
    arena = nc.alloc_sbuf_tensor("arena", [128, ARENA_W], F32)
    psum_t = [nc.alloc_psum_tensor("ps%d" % i, [128, 512], F32) for i in range(8)]
    plan = Prog(nc)
    program(nc, plan, locals())
    plan.analyze()
    em = Prog(nc, plan=plan)
    program(nc, em, locals())
    em.finish()
    return nc, plan


def program(nc, P, env):
    SEQ, CTX, depth, TALL, NCC, NLC, NCH, TB, TP, ARENA_W = (env[k] for k in (
        "SEQ", "CTX", "depth", "TALL", "NCC", "NLC", "NCH", "TB", "TP", "ARENA_W"))
    x_in, ctx_in, c_fm, W, pfm_in, prow_in, y_out = (env[k] for k in (
        "x_in", "ctx_in", "c_fm", "W", "pfm_in", "prow_in", "y_out"))
    sA, sB, xbcT, convT_d, siluz_d, xs_d, bt_d, BT_d, CT_d, yf_d, g_d = (env[k] for k in (
        "sA", "sB", "xbcT", "convT_d", "siluz_d", "xs_d", "bt_d", "BT_d", "CT_d", "yf_d", "g_d"))
    arena, psum_t, seg_of_chunk, xbc_col = env["arena"], env["psum_t"], env["seg_of_chunk"], env["xbc_col"]
    dbg, dbg_y, dbg_r = env["dbg"], env["dbg_y"], env["dbg_r"]
    aoff = [0]

    def alloc(shape, dt=F32, name="t", dma=False, parts=128):
        n = int(np.prod(shape))
        nw = n if dt == F32 else (n + 1) // 2
        nw = (nw + 7) // 8 * 8
        assert aoff[0] + nw <= ARENA_W, ("arena overflow", name, aoff[0], nw)
        v = arena[0:parts, aoff[0]:aoff[0] + nw]
        aoff[0] += nw
        if dt != F32:
            v = v.bitcast(dt)
        v = v[:, 0:n]
        if len(shape) == 2:
            v = v.rearrange("p (a b) -> p a b", a=shape[0])
        elif len(shape) == 3:
            v = v.rearrange("p (a b c) -> p a b c", a=shape[0], b=shape[1])
        return Tl(v, P.buf(name, dma=dma))

    psum = [Tl(psum_t[i][:, :], P.buf("ps%d" % i)) for i in range(8)]
    for t in psum:
        t.b.excl = True

    def psb(i):
        return psum[i].ap.bitcast(BF16)

    ident = alloc([128], F32, "ident")
    identb = alloc([128], BF16, "identb")
    triF = alloc([128], F32, "triF")
    triB = alloc([128], F32, "triB")
    ntriF = alloc([128], F32, "ntriF")
    ntriB = alloc([128], F32, "ntriB")
    maskF = alloc([4, 128], BF16, "maskF")
    maskB = alloc([4, 128], BF16, "maskB")
    onesb = alloc([128], BF16, "onesb")
    epsln = alloc([1], F32, "epsln")
    epsrms = alloc([1], F32, "epsrms")
    one_t = alloc([1], F32, "one_t")
    dt_sb = alloc([NCH, 2, HEADS], F32, "dt_sb")
    C0 = [ident, identb, triF, triB, ntriF, ntriB, maskF, maskB, onesb, epsln, epsrms, one_t]

    def pool_fill(t, ap, val):
        P.op("pool", lambda e, ap=ap, val=val: e.memset(ap, val), writes=[t.b])

    def pool_sel(t, ap, pattern, cmp, fill, base, cm):
        P.op("pool", lambda e: e.affine_select(out=ap, in_=ap, pattern=pattern, compare_op=cmp, fill=fill,
                                               base=base, channel_multiplier=cm), reads=[t.b], writes=[t.b])

    pool_fill(ident, ident.ap, 0.0)
    pool_sel(ident, ident.ap, [[-1, 128]], ALU.not_equal, 1.0, 0, 1)
    P.op("pool", lambda e: e.tensor_copy(out=identb.ap, in_=ident.ap), reads=[ident.b], writes=[identb.b])
    pool_fill(triF, triF.ap, 1.0)
    pool_sel(triF, triF.ap, [[1, 128]], ALU.is_ge, 0.0, 0, -1)
    pool_fill(triB, triB.ap, 1.0)
    pool_sel(triB, triB.ap, [[-1, 128]], ALU.is_ge, 0.0, 0, 1)
    P.op("pool", lambda e: e.tensor_scalar(out=ntriF.ap, in0=triF.ap, scalar1=-1.0, scalar2=None, op0=ALU.mult),
         reads=[triF.b], writes=[ntriF.b])
    P.op("pool", lambda e: e.tensor_scalar(out=ntriB.ap, in0=triB.ap, scalar1=-1.0, scalar2=None, op0=ALU.mult),
         reads=[triB.b], writes=[ntriB.b])
    pool_fill(maskF, maskF.ap, 0.0)
    pool_sel(maskF, maskF.ap, [[0, 4], [1, 128]], ALU.is_ge, NEG, 0, -1)
    pool_fill(maskB, maskB.ap, 0.0)
    pool_sel(maskB, maskB.ap, [[0, 4], [-1, 128]], ALU.is_ge, NEG, 0, 1)
    pool_fill(onesb, onesb.ap, 1.0)
    pool_fill(epsln, epsln.ap, LN_EPS)
    pool_fill(epsrms, epsrms.ap, RMS_EPS)
    pool_fill(one_t, one_t.ap, 1.0)

    zt = alloc([16, 8], BF16, "zt", dma=True)
    pool_fill(zt, zt.ap, 0.0)
    xb3 = xbcT.rearrange("(c p) t -> p c t", p=128)
    for c0 in (0, CTX + 2, CTX + 4, TP - 2):
        P.dma("sp", lambda e, c0=c0: e.dma_start(out=xb3[:, :, c0:c0 + 2], in_=zt.ap[:, :, 0:2]), zt.b, reads=[zt.b])
    persist_mark = aoff[0]

    def rstd_from(var_ap, out_t, eps_t, reads, n=1, scale=1.0):
        P.op("act", lambda e: e.activation(out=out_t.ap, in_=var_ap, func=AF.Ln, bias=eps_t.ap[:, 0:1], scale=scale),
             reads=reads + [eps_t.b], writes=[out_t.b])
        P.op("act", lambda e: e.activation(out=out_t.ap, in_=out_t.ap, func=AF.Exp, scale=-0.5), reads=[out_t.b],
             writes=[out_t.b])

    for L in range(depth):
        last = (L == depth - 1)
        src = None if L == 0 else sB

        def stream_rows(tok0, n, src=src):
            if src is None:
                return ctx_in[tok0:tok0 + n, :] if tok0 < CTX else x_in[tok0 - CTX:tok0 - CTX + n, :]
            return src[tok0:tok0 + n, :]

        P.barrier()
        aoff[0] = persist_mark
        pf = alloc([NPF], F32, "pf", dma=True)
        P.dma("sp", lambda e, L=L: e.dma_start(out=pf.ap, in_=pfm_in[L]), pf.b, writes=[pf.b])
        AB = alloc([2, 8, 4], F32, "AB")
        hrow = alloc([3, 2, HEADS], F32, "hrow", dma=True)
        negA = alloc([2, HEADS], F32, "negA")
        layer_mark = aoff[0]

        def pfv(name):
            o, w = PFM_OFF[name]
            return pf.ap[:, o:o + w]
        prow2 = alloc([NPR], F32, "prow2", dma=True, parts=2)
        P.dma("sp", lambda e, L=L: e.dma_start(out=prow2.ap, in_=prow_in[L].partition_broadcast(2)), prow2.b,
              writes=[prow2.b])
        csil = alloc([8, 2], F32, "csil", dma=True)
        P.dma("sp", lambda e: e.dma_start(out=csil.ap, in_=c_fm.rearrange("p (k s) -> p k s", s=2)), csil.b,
              writes=[csil.b])
        P.op("act", lambda e: e.activation(out=csil.ap, in_=csil.ap, func=AF.Silu), reads=[csil.b], writes=[csil.b])
        modfm = alloc([48, 2], F32, "modfm")
        grow = alloc([2, D], F32, "grow", dma=True, parts=2)
        wm = [alloc([8, 1024], F32, "wm%d" % i, dma=True) for i in range(2)]
        wmod3 = W["w_mod"][L].rearrange("(k p) n -> p k n", p=128)
        for cb in range(6):
            t = wm[cb % 2]
            P.dma("sp", lambda e, t=t, cb=cb: e.dma_start(out=t.ap, in_=wmod3[:, :, cb * 1024:(cb + 1) * 1024]),
                  t.b, writes=[t.b])
            if cb in (2, 5):
                which = 0 if cb == 2 else 1
                for half in range(2):
                    pb = psum[half]
                    for k in range(8):
                        P.op("pe", lambda e, t=t, k=k, half=half, pb=pb: e.matmul(
                            pb.ap[0:2, :], lhsT=csil.ap[:, k, :], rhs=t.ap[:, k, half * 512:(half + 1) * 512],
                            start=(k == 0), stop=(k == 7)), reads=[t.b, csil.b], writes=[pb.b])
                    o, _w = PROW_OFF["bmod_g1" if which == 0 else "bmod_g2"]
                    P.op("dve", lambda e, pb=pb, half=half, which=which, o=o: e.tensor_tensor(
                        out=grow.ap[:, which, half * 512:(half + 1) * 512], in0=pb.ap[0:2, :],
                        in1=prow2.ap[:, o + half * 512:o + (half + 1) * 512], op=ALU.add),
                        reads=[pb.b, prow2.b], writes=[grow.b])
            else:
                pb = psum[2 + (cb % 2)]
                for j in range(8):
                    for k in range(8):
                        P.op("pe", lambda e, t=t, k=k, j=j, pb=pb: e.matmul(
                            pb.ap[:, j * 2:j * 2 + 2], lhsT=t.ap[:, k, j * 128:(j + 1) * 128], rhs=csil.ap[:, k, :],
                            start=(k == 0), stop=(k == 7)), reads=[t.b, csil.b], writes=[pb.b])
                bo = PFM_OFF["bmod"][0] + cb * 8
                P.op("dve", lambda e, pb=pb, cb=cb, bo=bo: e.tensor_tensor(
                    out=modfm.ap[:, cb * 8:(cb + 1) * 8, :],
                    in0=pb.ap[:, 0:16].rearrange("p (j s) -> p j s", s=2),
                    in1=pf.ap[:, bo:bo + 8].unsqueeze(2).broadcast_to([128, 8, 2]), op=ALU.add),
                    reads=[pb.b, pf.b], writes=[modfm.b])
        P.dma("sp", lambda e: e.dma_start(out=g_d.rearrange("s w d -> s (w d)"),
                                          in_=grow.ap.rearrange("p w d -> p (w d)")), grow.b, reads=[grow.b])
        gp = alloc([8], F32, "gp")
        bp = alloc([8], F32, "bp")
        if L == 0:
            P.op("dve", lambda e: e.memset(gp.ap, 1.0), writes=[gp.b])
            P.op("dve", lambda e: e.memset(bp.ap, 0.0), writes=[bp.b])
        else:
            pfp = alloc([NPF], F32, "pfp", dma=True)
            P.dma("sp", lambda e, L=L: e.dma_start(out=pfp.ap, in_=pfm_in[L - 1]), pfp.b, writes=[pfp.b])
            o1, o2 = PFM_OFF["ln2g"][0], PFM_OFF["ln2b"][0]
            P.op("dve", lambda e: e.tensor_copy(out=gp.ap, in_=pfp.ap[:, o1:o1 + 8]), reads=[pfp.b], writes=[gp.b])
            P.op("dve", lambda e: e.tensor_copy(out=bp.ap, in_=pfp.ap[:, o2:o2 + 8]), reads=[pfp.b], writes=[bp.b])
        tmp1 = alloc([8, 2], F32, "tmp1")
        for stg in range(2):
            gsrc = gp.ap if stg == 0 else pfv("ln1g")
            bsrc = bp.ap if stg == 0 else pfv("ln1b")
            gb = [gp.b, bp.b] if stg == 0 else [pf.b]
            shc, scc = (0, 1) if stg == 0 else (3, 4)
            P.op("dve", lambda e, scc=scc: e.tensor_scalar(out=tmp1.ap, in0=modfm.ap[:, scc * 8:(scc + 1) * 8, :],
                                                           scalar1=1.0, scalar2=None, op0=ALU.add),
                 reads=[modfm.b], writes=[tmp1.b])
            P.op("dve", lambda e, stg=stg, gsrc=gsrc: e.tensor_tensor(
                out=AB.ap[:, stg, :, 0:2], in0=tmp1.ap, in1=gsrc.unsqueeze(2).broadcast_to([128, 8, 2]), op=ALU.mult),
                reads=[tmp1.b] + gb, writes=[AB.b])
            P.op("dve", lambda e, bsrc=bsrc: e.tensor_tensor(
                out=tmp1.ap, in0=tmp1.ap, in1=bsrc.unsqueeze(2).broadcast_to([128, 8, 2]), op=ALU.mult),
                reads=[tmp1.b] + gb, writes=[tmp1.b])
            P.op("dve", lambda e, stg=stg, shc=shc: e.tensor_tensor(
                out=AB.ap[:, stg, :, 2:4], in0=tmp1.ap, in1=modfm.ap[:, shc * 8:(shc + 1) * 8, :], op=ALU.add),
                reads=[tmp1.b, modfm.b], writes=[AB.b])
        o = PROW_OFF["dtb"][0]
        P.dma("sp", lambda e, L=L, o=o: e.dma_start(
            out=hrow.ap.rearrange("p a b c -> p (a b c)"), in_=prow_in[L][:, o:o + 96].partition_broadcast(128)),
            hrow.b, writes=[hrow.b])
        P.op("act", lambda e: e.activation(out=negA.ap, in_=hrow.ap[:, 1], func=AF.Exp), reads=[hrow.b], writes=[negA.b])
        P.op("dve", lambda e: e.tensor_scalar(out=negA.ap, in0=negA.ap, scalar1=-1.0, scalar2=None, op0=ALU.mult),
             reads=[negA.b], writes=[negA.b])

        P.barrier()
        aoff[0] = layer_mark
        w_in = alloc([8, D_PROJ], BF16, "w_in")
        win3 = W["w_in"][L].rearrange("(k p) n -> p k n", p=128)
        win_pc = [P.buf("w_in_pc%d" % i, sw=True) for i in range(9)]
        for wpc in (1, 0, 4, 5, 6, 7, 2, 3, 8):
            c0, c1 = wpc * 512, min((wpc + 1) * 512, D_PROJ)
            P.dma("pool", lambda e, c0=c0, c1=c1: e.dma_start(out=w_in.ap[:, :, c0:c1], in_=win3[:, :, c0:c1]),
                  win_pc[wpc], writes=[win_pc[wpc]])
        d31 = alloc([4, CONV_K, 128], BF16, "d31")
        o = PFM_OFF["convw"][0]
        for cc in range(4):
            P.op("dve", lambda e, cc=cc, o=o: e.tensor_tensor(
                out=d31.ap[:, cc], in0=ident.ap.unsqueeze(1).broadcast_to([128, CONV_K, 128]),
                in1=pf.ap[:, o + cc * CONV_K:o + (cc + 1) * CONV_K].unsqueeze(2).broadcast_to([128, CONV_K, 128]),
                op=ALU.mult), reads=[ident.b, pf.b], writes=[d31.b])
        convb_r = alloc([512], BF16, "convb_r", parts=1)
        cb_st = alloc([512], F32, "cb_st", dma=True, parts=1)
        o = PROW_OFF["convb"][0]
        P.dma("sp", lambda e, L=L, o=o: e.dma_start(out=cb_st.ap, in_=prow_in[L][:, o:o + 512]), cb_st.b, writes=[cb_st.b])
        P.op("dve", lambda e: e.tensor_copy(out=convb_r.ap, in_=cb_st.ap), reads=[cb_st.b], writes=[convb_r.b])
        dtb_bc = hrow.ap[:, 0]
        xin = [alloc([2, D], F32, "xin%d" % i, dma=True) for i in range(2)]
        hT = [alloc([8, TB], BF16, "hT%d" % i) for i in range(2)]
        vpad = [alloc([4, 4, 94], BF16, "vpad%d" % i) for i in range(2)]
        for t in vpad:
            pool_fill(t, t.ap, 0.0)
        sig = [alloc([TB], F32, "sig%d" % i) for i in range(2)]
        xbc_ev = [alloc([16, TB], BF16, "xbc_ev%d" % i, dma=True) for i in range(2)]
        yhat = [alloc([512], BF16, "yhat%d" % i) for i in range(2)]
        cT_ev = [alloc([4, TB], BF16, "cT_ev%d" % i, dma=True) for i in range(2)]
        sz_ev = [alloc([2, D_INNER], F32, "sz_ev%d" % i, dma=True) for i in range(2)]
        st6 = alloc([6], F32, "st6")
        mv = alloc([2], F32, "mv")
        rs = alloc([1], F32, "rs")
        dtt = alloc([2, HEADS], F32, "dtt")
        cvT3 = convT_d.rearrange("(c p) t -> p c t", p=128)
        pc = [0]

        def nextps():
            pc[0] += 1
            return psum[pc[0] % 8]

        nblk = TALL // TB
        for bi in range(nblk):
            tok0 = bi * TB
            seg = 0 if tok0 < CTX else 1
            full = not (last and seg == 0)
            xi, h, vp, xe, ce, se = xin[bi % 2], hT[bi % 2], vpad[bi % 2], xbc_ev[bi % 2], cT_ev[bi % 2], sz_ev[bi % 2]
            def load_x(nb):
                if nb < nblk:
                    xn = xin[nb % 2]
                    P.dma("sp", lambda e, xn=xn, nb=nb: e.dma_start(
                        out=xn.ap, in_=stream_rows(nb * TB, TB).rearrange("(j p) d -> p j d", p=128)), xn.b, writes=[xn.b])

            def transposes(nb):
                if nb >= nblk:
                    return
                xi_, h_ = xin[nb % 2], hT[nb % 2]
                seg_ = 0 if nb * TB < CTX else 1
                for k in range(8):
                    pb = nextps()
                    for j in range(2):
                        P.op("pe", lambda e, pb=pb, j=j, k=k: e.transpose(
                            out=pb.ap[:, j * 128:(j + 1) * 128], in_=xi_.ap[:, j, k * 128:(k + 1) * 128], identity=ident.ap),
                            reads=[xi_.b, ident.b], writes=[pb.b])
                    P.op("act", lambda e, pb=pb, k=k: e.activation(
                        out=h_.ap[:, k, :], in_=pb.ap[:, 0:TB], func=AF.Identity,
                        scale=AB.ap[:, 0, k, seg_:seg_ + 1], bias=AB.ap[:, 0, k, 2 + seg_:3 + seg_]),
                        reads=[pb.b, AB.b], writes=[h_.b])

            if bi == 0:
                load_x(0)
                load_x(1)
                transposes(0)
            if bi > 0:
                load_x(bi + 1)

            def ws_mm(col0, pb, h=h):
                for k in range(8):
                    P.op("pe", lambda e, k=k, pb=pb, col0=col0: e.matmul(
                        pb.ap[:, 0:TB], lhsT=w_in.ap[:, k, col0:col0 + 128], rhs=h.ap[:, k, :],
                        start=(k == 0), stop=(k == 7)), reads=[win_pc[col0 // 512], h.b], writes=[pb.b])
            if full:
                for cc in range(4):
                    sg = sig[cc % 2]
                    pg = nextps()
                    ws_mm(512 + cc * 128, pg)
                    P.op("act", lambda e, pg=pg, sg=sg: e.activation(out=sg.ap, in_=pg.ap[:, 0:TB], func=AF.Sigmoid),
                         reads=[pg.b], writes=[sg.b])
                    pa = nextps()
                    ws_mm(cc * 128, pa)
                    P.op("dve", lambda e, pa=pa, sg=sg, vp=vp, cc=cc: e.tensor_tensor(
                        out=vp.ap[:, cc, :, 15:79], in0=pa.ap[:, 0:TB].rearrange("p (r w) -> p r w", w=64),
                        in1=sg.ap.rearrange("p (r w) -> p r w", w=64), op=ALU.mult),
                        reads=[pa.b, sg.b], writes=[vp.b])
            for cc in range(16):
                pb = nextps()
                ws_mm(2048 + cc * 128, pb)
                if cc % 4 == 0:
                    P.op("act", lambda e, pb=pb, xe=xe, cc=cc: e.activation(out=xe.ap[:, cc, :], in_=pb.ap[:, 0:TB],
                                                                          func=AF.Identity), reads=[pb.b], writes=[xe.b])
                else:
                    P.op("dve", lambda e, pb=pb, xe=xe, cc=cc: e.tensor_copy(out=xe.ap[:, cc, :], in_=pb.ap[:, 0:TB]),
                         reads=[pb.b], writes=[xe.b])
            col0 = xbc_col(tok0)
            P.dma("act", lambda e, xe=xe, col0=col0: e.dma_start(out=xb3[:, :, col0:col0 + TB], in_=xe.ap), xe.b,
                  reads=[xe.b])
            transposes(bi + 1)
            for j in range(2):
                ch = (tok0 // 128) + j
                if full:
                    pcv = nextps()
                    for cc in range(4):
                        P.op("pe", lambda e, pcv=pcv, cc=cc: e.matmul(
                            pcv.ap[:, cc * 128:(cc + 1) * 128], lhsT=onesb.ap[0:1, :],
                            rhs=convb_r.ap[0:1, cc * 128:(cc + 1) * 128], start=True, stop=False, skip_group_check=True),
                            reads=[onesb.b, convb_r.b], writes=[pcv.b])
                        for tp in range(CONV_K):
                            for row in range(2):
                                P.op("pe", lambda e, pcv=pcv, cc=cc, tp=tp, vp=vp, j=j, row=row: e.matmul(
                                    pcv.ap[row * 64:(row + 1) * 64, cc * 128:(cc + 1) * 128], lhsT=vp.ap[:, cc, 2 * j + row, tp:tp + 64],
                                    rhs=d31.ap[:, cc, tp, :], start=False, stop=(tp == CONV_K - 1), skip_group_check=True),
                                    reads=[vp.b, d31.b], writes=[pcv.b])
                    yh = yhat[j]
                    P.op("dve", lambda e, pcv=pcv: e.bn_stats(out=st6.ap, in_=pcv.ap), reads=[pcv.b], writes=[st6.b])
                    P.op("dve", lambda e: e.bn_aggr(out=mv.ap, in_=st6.ap), reads=[st6.b], writes=[mv.b])
                    rstd_from(mv.ap[:, 1:2], rs, epsln, [mv.b])
                    P.op("dve", lambda e, pcv=pcv, yh=yh: e.tensor_scalar(
                        out=yh.ap, in0=pcv.ap, scalar1=mv.ap[:, 0:1], scalar2=rs.ap[:, 0:1], op0=ALU.subtract, op1=ALU.mult),
                        reads=[pcv.b, mv.b, rs.b], writes=[yh.b])
                    pt = nextps()
                    pti = psum.index(pt)
                    for cc in range(4):
                        P.op("pe", lambda e, pti=pti, yh=yh, cc=cc: e.transpose(
                            out=psb(pti)[:, cc * 128:(cc + 1) * 128], in_=yh.ap[:, cc * 128:(cc + 1) * 128],
                            identity=identb.ap), reads=[yh.b, identb.b], writes=[pt.b])
                    og, ob = PFM_OFF["lng"][0], PFM_OFF["lnb"][0]
                    for cc in range(4):
                        P.op("act", lambda e, pti=pti, ce=ce, cc=cc, j=j, og=og, ob=ob: e.activation(
                            out=ce.ap[:, cc, j * 128:(j + 1) * 128], in_=psb(pti)[:, cc * 128:(cc + 1) * 128], func=AF.Silu,
                            scale=pf.ap[:, og + cc:og + cc + 1], bias=pf.ap[:, ob + cc:ob + cc + 1]),
                            reads=[pt.b, pf.b], writes=[ce.b])
                    for half in range(2):
                        pz = nextps()
                        for k in range(8):
                            P.op("pe", lambda e, pz=pz, k=k, j=j, half=half, h=h: e.matmul(
                                pz.ap, lhsT=h.ap[:, k, j * 128:(j + 1) * 128],
                                rhs=w_in.ap[:, k, 1024 + half * 512:1024 + (half + 1) * 512],
                                start=(k == 0), stop=(k == 7)), reads=[h.b, win_pc[2 + half]], writes=[pz.b])
                        P.op("act", lambda e, pz=pz, se=se, j=j, half=half: e.activation(
                            out=se.ap[:, j, half * 512:(half + 1) * 512], in_=pz.ap, func=AF.Silu),
                            reads=[pz.b], writes=[se.b])
                pd = nextps()
                for k in range(8):
                    P.op("pe", lambda e, pd=pd, k=k, j=j, h=h: e.matmul(
                        pd.ap[:, 0:16], lhsT=h.ap[:, k, j * 128:(j + 1) * 128], rhs=w_in.ap[:, k, 4096:4112],
                        start=(k == 0), stop=(k == 7)), reads=[h.b, win_pc[8]], writes=[pd.b])
                P.op("dve", lambda e, pd=pd: e.tensor_tensor(
                    out=dtt.ap, in0=pd.ap[:, 0:16].unsqueeze(1).broadcast_to([128, 2, HEADS]), in1=dtb_bc, op=ALU.add),
                    reads=[pd.b, hrow.b], writes=[dtt.b])
                P.op("act", lambda e: e.activation(out=dtt.ap, in_=dtt.ap, func=AF.Exp), reads=[dtt.b], writes=[dtt.b])
                P.op("act", lambda e, ch=ch: e.activation(out=dt_sb.ap[:, ch], in_=dtt.ap, func=AF.Ln, bias=one_t.ap[:, 0:1]),
                     reads=[dtt.b, one_t.b], writes=[dt_sb.b])
            if full:
                P.dma("act", lambda e, ce=ce, tok0=tok0: e.dma_start(out=cvT3[:, :, tok0:tok0 + TB], in_=ce.ap), ce.b,
                      reads=[ce.b])
                P.dma("act", lambda e, se=se, tok0=tok0: e.dma_start(
                    out=siluz_d[tok0:tok0 + TB, :].rearrange("(j p) d -> p j d", p=128), in_=se.ap), se.b, reads=[se.b])

        P.barrier()
        P.release([t.b for t in xin + xbc_ev + cT_ev + sz_ev] + win_pc + [cb_st.b, prow2.b, csil.b, grow.b] + [t.b for t in wm] + ([pfp.b] if L > 0 else []))
        aoff[0] = layer_mark
        d5 = alloc([16, SK, 128], BF16, "d5")
        o = PFM_OFF["sconvw"][0]
        for c4 in range(4):
            P.op("dve", lambda e, c4=c4, o=o: e.tensor_tensor(
                out=d5.ap[:, c4 * 4:(c4 + 1) * 4].rearrange("p c k j -> p (c k) j"),
                in0=ident.ap.unsqueeze(1).broadcast_to([128, 4 * SK, 128]),
                in1=pf.ap[:, o + c4 * 4 * SK:o + (c4 + 1) * 4 * SK].unsqueeze(2).broadcast_to([128, 4 * SK, 128]),
                op=ALU.mult), reads=[ident.b, pf.b], writes=[d5.b])
        scb_r = alloc([1536], BF16, "scb_r", parts=1)
        sb_st = alloc([1536], F32, "sb_st", dma=True, parts=1)
        o = PROW_OFF["sconvb"][0]
        P.dma("sp", lambda e, L=L, o=o: e.dma_start(out=sb_st.ap, in_=prow_in[L][:, o:o + 1536]), sb_st.b, writes=[sb_st.b])
        P.op("dve", lambda e: e.tensor_copy(out=scb_r.ap, in_=sb_st.ap), reads=[sb_st.b], writes=[scb_r.b])
        xc = [alloc([16, 132], BF16, "xc%d" % i, dma=True) for i in range(3)]
        xs_ev = [alloc([D_INNER], BF16, "xs_ev%d" % i, dma=True) for i in range(2)]
        bt_ev = [alloc([512], BF16, "bt_ev%d" % i, dma=True) for i in range(2)]
        BT_ev = [alloc([4, 128], BF16, "BT_ev%d" % i, dma=True) for i in range(2)]
        CT_ev = [alloc([4, 128], BF16, "CT_ev%d" % i, dma=True) for i in range(2)]
        BT3 = BT_d.rearrange("(g p) t -> p g t", p=128)
        CT3 = CT_d.rearrange("(g p) t -> p g t", p=128)
        osb = PFM_OFF["sconvb"][0]
        for ch in range(NCH):
            tok0 = ch * 128
            x_, xs_, bt_, BT_, CT_ = xc[ch % 3], xs_ev[ch % 2], bt_ev[ch % 2], BT_ev[ch % 2], CT_ev[ch % 2]
            for nch in ([0, 1, 2] if ch == 0 else [ch + 2]):
                if nch < NCH:
                    xn = xc[nch % 3]
                    coln = xbc_col(nch * 128) - 2
                    P.dma("sp", lambda e, xn=xn, coln=coln: e.dma_start(out=xn.ap, in_=xb3[:, :, coln:coln + 132]), xn.b,
                          writes=[xn.b])
            for which, dst in ((0, BT_), (1, CT_)):
                pb = nextps()
                for g in range(4):
                    cc = 8 + which * 4 + g
                    for tp in range(SK):
                        P.op("pe", lambda e, pb=pb, g=g, cc=cc, tp=tp, x_=x_: e.matmul(
                            pb.ap[:, g * 128:(g + 1) * 128], lhsT=d5.ap[:, cc, tp, :], rhs=x_.ap[:, cc, tp:tp + 128],
                            start=(tp == 0), stop=(tp == SK - 1)), reads=[d5.b, x_.b], writes=[pb.b])
                for g in range(4):
                    cc = 8 + which * 4 + g
                    P.op("act", lambda e, pb=pb, g=g, cc=cc, dst=dst: e.activation(
                        out=dst.ap[:, g, :], in_=pb.ap[:, g * 128:(g + 1) * 128], func=AF.Silu,
                        bias=pf.ap[:, osb + cc:osb + cc + 1]), reads=[pb.b, pf.b], writes=[dst.b])
            for grp in range(3):
                pb = nextps()
                for q in range(4):
                    cc = grp * 4 + q
                    P.op("pe", lambda e, pb=pb, q=q, cc=cc: e.matmul(
                        pb.ap[:, q * 128:(q + 1) * 128], lhsT=onesb.ap[0:1, :], rhs=scb_r.ap[0:1, cc * 128:(cc + 1) * 128],
                        start=True, stop=False, skip_group_check=True), reads=[onesb.b, scb_r.b], writes=[pb.b])
                    for tp in range(SK):
                        P.op("pe", lambda e, pb=pb, q=q, cc=cc, tp=tp, x_=x_: e.matmul(
                            pb.ap[:, q * 128:(q + 1) * 128], lhsT=x_.ap[:, cc, tp:tp + 128], rhs=d5.ap[:, cc, tp, :],
                            start=False, stop=(tp == SK - 1), skip_group_check=True), reads=[d5.b, x_.b], writes=[pb.b])
                dst = xs_.ap[:, grp * 512:(grp + 1) * 512] if grp < 2 else bt_.ap
                dstb = xs_.b if grp < 2 else bt_.b
                P.op("act", lambda e, pb=pb, dst=dst: e.activation(out=dst, in_=pb.ap, func=AF.Silu),
                     reads=[pb.b], writes=[dstb])
            P.dma("act", lambda e, xs_=xs_, tok0=tok0: e.dma_start(out=xs_d[tok0:tok0 + 128, :], in_=xs_.ap), xs_.b,
                  reads=[xs_.b])
            P.dma("act", lambda e, bt_=bt_, tok0=tok0: e.dma_start(out=bt_d[tok0:tok0 + 128, :], in_=bt_.ap), bt_.b,
                  reads=[bt_.b])
            P.dma("act", lambda e, BT_=BT_, tok0=tok0: e.dma_start(out=BT3[:, :, tok0:tok0 + 128], in_=BT_.ap), BT_.b,
                  reads=[BT_.b])
            P.dma("act", lambda e, CT_=CT_, tok0=tok0: e.dma_start(out=CT3[:, :, tok0:tok0 + 128], in_=CT_.ap), CT_.b,
                  reads=[CT_.b])

        P.barrier()
        P.release([t.b for t in xc + xs_ev + bt_ev + BT_ev + CT_ev] + [sb_st.b])
        aoff[0] = layer_mark
        w_out = alloc([12, D], BF16, "w_out")
        w_out.b = P.buf("w_out_sw", sw=True)
        wo3 = W["w_out"][L].rearrange("(k p) n -> p k n", p=128)
        for k in range(12):
            P.dma("pool", lambda e, k=k: e.dma_start(out=w_out.ap[:, k, :], in_=wo3[:, k, :]), w_out.b, writes=[w_out.b])
        dsk = alloc([2 * HEADS, 128], BF16, "dsk")
        P.op("dve", lambda e: e.tensor_tensor(
            out=dsk.ap, in0=ident.ap.unsqueeze(1).broadcast_to([128, 2 * HEADS, 128]),
            in1=hrow.ap[:, 2].rearrange("p a b -> p (a b)").unsqueeze(2).broadcast_to([128, 2 * HEADS, 128]),
            op=ALU.mult), reads=[ident.b, hrow.b], writes=[dsk.b])
        NS = 3
        xs_l = [alloc([D_INNER], BF16, "xs_l%d" % i, dma=True) for i in range(NS)]
        bt_l = [alloc([512], BF16, "bt_l%d" % i, dma=True) for i in range(NS)]
        BT_l = [alloc([4, 128], BF16, "BT_l%d" % i, dma=True) for i in range(NS)]
        CT_l = [alloc([4, 128], BF16, "CT_l%d" % i, dma=True) for i in range(NS)]
        hst = alloc([D_INNER], F32, "hst")
        hbf_t = [alloc([D_INNER], BF16, "hbf%d" % i) for i in range(2)]
        htmp = alloc([D_INNER], F32, "htmp")
        dA_t = [alloc([HEADS], F32, "dA%d" % i) for i in range(NS)]
        cs_t = [alloc([HEADS], F32, "cs%d" % i) for i in range(NS)]
        cshi_t = [alloc([HEADS], BF16, "cshi%d" % i) for i in range(NS)]
        cslo_t = [alloc([HEADS], BF16, "cslo%d" % i) for i in range(NS)]
        wv_t = [alloc([HEADS], F32, "wv%d" % i) for i in range(NS)]
        cdec_t = [alloc([HEADS], F32, "cdec%d" % i) for i in range(NS)]
        E1_t = [alloc([4, 128], BF16, "E1_%d" % i) for i in range(3)]
        E2_t = [alloc([4, 128], BF16, "E2_%d" % i) for i in range(3)]
        Mo_t = [alloc([HEADS, 128], BF16, "Mo%d" % i) for i in range(NS)]
        Md_t = [alloc([HEADS, 128], BF16, "Md%d" % i) for i in range(NS)]
        cbm_t = [alloc([4, 128], BF16, "cbm%d" % i) for i in range(NS)]
        xdt_t = [alloc([D_INNER], BF16, "xdt%d" % i) for i in range(NS)]
        xw_t = [alloc([D_INNER], BF16, "xw%d" % i) for i in range(NS)]
        yfs = [alloc([D_INNER], F32, "yfs%d" % i, dma=True) for i in range(2)]
        szl = [alloc([D_INNER], F32, "szl%d" % i, dma=True) for i in range(2)]
        cvl = [alloc([4, 128], BF16, "cvl%d" % i, dma=True) for i in range(2)]
        xres = [alloc([D], F32, "xres%d" % i, dma=True) for i in range(2)]
        gbuf_t = [alloc([D_INNER], F32, "gbuf%d" % i) for i in range(2)]
        gsq = alloc([D_INNER], F32, "gsq")
        gnb = alloc([D_INNER], BF16, "gnb")
        gT = alloc([8, 128], BF16, "gT")
        ssq = alloc([4], F32, "ssq")
        rr = alloc([D], F32, "rr")
        xo = [alloc([D], F32, "xo%d" % i, dma=True) for i in range(2)]
        st12 = alloc([12], F32, "st12")
        agb = alloc([D], F32, "agb", dma=True)
        abb = alloc([D], F32, "abb", dma=True)
        g1b = alloc([D], F32, "g1b", dma=True)
        onw = PFM_OFF["normw"][0]

        def load_bc(seg, stage):
            if stage == 0:
                if L == 0:
                    P.op("pool", lambda e: e.memset(agb.ap, ALPHA), writes=[agb.b])
                    P.op("pool", lambda e: e.memset(abb.ap, 0.0), writes=[abb.b])
                    srcs = None
                else:
                    srcs = (prow_in[L - 1], PROW_OFF["ln2g"][0], PROW_OFF["ln2b"][0])
            else:
                srcs = (prow_in[L], PROW_OFF["ln1g"][0], PROW_OFF["ln1b"][0])
            if srcs is not None:
                pr, og_, ob_ = srcs
                P.dma("sp", lambda e: e.dma_start(out=agb.ap, in_=pr[:, og_:og_ + D].partition_broadcast(128)), agb.b,
                      writes=[agb.b])
                P.dma("sp", lambda e: e.dma_start(out=abb.ap, in_=pr[:, ob_:ob_ + D].partition_broadcast(128)), abb.b,
                      writes=[abb.b])
                P.op("pool", lambda e: e.tensor_scalar(out=agb.ap, in0=agb.ap, scalar1=ALPHA, scalar2=None, op0=ALU.mult),
                     reads=[agb.b], writes=[agb.b])
                P.op("pool", lambda e: e.tensor_scalar(out=abb.ap, in0=abb.ap, scalar1=ALPHA, scalar2=None, op0=ALU.mult),
                     reads=[abb.b], writes=[abb.b])
            P.dma("sp", lambda e: e.dma_start(out=g1b.ap, in_=g_d[seg, stage:stage + 1, :].partition_broadcast(128)), g1b.b,
                  writes=[g1b.b])

        def residual_pre(xr):
            P.op("pool", lambda e, xr=xr: e.tensor_tensor(out=xr.ap, in0=xr.ap, in1=agb.ap, op=ALU.mult),
                 reads=[xr.b, agb.b], writes=[xr.b])
            P.op("pool", lambda e, xr=xr: e.tensor_tensor(out=xr.ap, in0=xr.ap, in1=abb.ap, op=ALU.add),
                 reads=[xr.b, abb.b], writes=[xr.b])

        def residual_ln(pbs, xr, out_t, final=None, pre=True):
            if pre:
                residual_pre(xr)
            for half in range(2):
                sl = slice(half * 512, (half + 1) * 512)
                P.op("dve", lambda e, half=half, sl=sl: e.tensor_tensor(out=rr.ap[:, sl], in0=pbs[half].ap, in1=g1b.ap[:, sl],
                                                                        op=ALU.mult), reads=[pbs[half].b, g1b.b], writes=[rr.b])
            P.op("dve", lambda e, xr=xr: e.tensor_tensor(out=rr.ap, in0=rr.ap, in1=xr.ap, op=ALU.add),
                 reads=[rr.b, xr.b], writes=[rr.b])
            for half in range(2):
                P.op("dve", lambda e, half=half: e.bn_stats(out=st12.ap[:, half * 6:(half + 1) * 6],
                                                            in_=rr.ap[:, half * 512:(half + 1) * 512]),
                     reads=[rr.b], writes=[st12.b])
            P.op("dve", lambda e: e.bn_aggr(out=mv.ap, in_=st12.ap), reads=[st12.b], writes=[mv.b])
            rstd_from(mv.ap[:, 1:2], rs, epsln, [mv.b])
            P.op("dve", lambda e, out_t=out_t: e.tensor_scalar(
                out=out_t.ap, in0=rr.ap, scalar1=mv.ap[:, 0:1], scalar2=rs.ap[:, 0:1], op0=ALU.subtract, op1=ALU.mult),
                reads=[rr.b, mv.b, rs.b], writes=[out_t.b])
            if final is not None:
                fg, fb = final
                P.op("pool", lambda e, out_t=out_t: e.tensor_tensor(out=out_t.ap, in0=out_t.ap, in1=fg.ap, op=ALU.mult),
                     reads=[out_t.b, fg.b], writes=[out_t.b])
                P.op("pool", lambda e, out_t=out_t: e.tensor_tensor(out=out_t.ap, in0=out_t.ap, in1=fb.ap, op=ALU.add),
                     reads=[out_t.b, fb.b], writes=[out_t.b])

        st6 = alloc([6], F32, "st6b")
        mv = alloc([2], F32, "mvb")
        rs = alloc([1], F32, "rsb")
        pcs_b, pcb_b, pR, py, pst = psum[0], psum[1], [psum[2], psum[3], psum[4]], [psum[5], psum[6]], [psum[7], psum[1]]
        yf_bufs = [P.buf("yfd%d" % i) for i in range(NCH)]
        pt_b, pti = psum[7], 7
        for d in range(NDIR):
            tri, msk = (triF, maskF) if d == 0 else (triB, maskB)
            if d == 0:
                order = list(range(NCH))
            else:
                order = list(range(NCC - 1, -1, -1)) + list(range(NCH - 1, NCC - 1, -1))
            P.op("dve", lambda e: e.memset(hst.ap, 0.0), writes=[hst.b])
            for hb_ in hbf_t:
                P.op("pool", lambda e, hb_=hb_: e.memset(hb_.ap, 0.0), writes=[hb_.b])
            state = {"seg": -1, "rq": 0}

            def wanty(ch):
                return not (last and seg_of_chunk(ch) == 0)

            def stage_A_load(it, ch):
                tok0 = ch * 128
                b2 = it % NS
                xs_, bt_, BT_, CT_ = xs_l[b2], bt_l[b2], BT_l[b2], CT_l[b2]
                P.dma("sp", lambda e: e.dma_start(out=xs_.ap, in_=xs_d[tok0:tok0 + 128, :]), xs_.b, writes=[xs_.b])
                P.dma("sp", lambda e: e.dma_start(out=bt_.ap, in_=bt_d[tok0:tok0 + 128, :]), bt_.b, writes=[bt_.b])
                if wanty(ch):
                    P.dma("sp", lambda e: e.dma_start(out=BT_.ap, in_=BT3[:, :, tok0:tok0 + 128]), BT_.b, writes=[BT_.b])
                    P.dma("sp", lambda e: e.dma_start(out=CT_.ap, in_=CT3[:, :, tok0:tok0 + 128]), CT_.b, writes=[CT_.b])

            def stage_A(it, ch, d=d, tri=tri, msk=msk):
                tok0 = ch * 128
                b2 = it % NS
                want_y = wanty(ch)
                xs_, bt_, BT_, CT_ = xs_l[b2], bt_l[b2], BT_l[b2], CT_l[b2]
                dA, cs, wv, cdec, xdt, xw, cbm, Mo, Md = (dA_t[b2], cs_t[b2], wv_t[b2], cdec_t[b2], xdt_t[b2], xw_t[b2],
                                                         cbm_t[b2], Mo_t[b2], Md_t[b2])
                cshi, cslo = cshi_t[b2], cslo_t[b2]
                dtv = dt_sb.ap[:, ch, d, :]
                P.op("dve", lambda e: e.tensor_tensor(out=dA.ap, in0=dtv, in1=negA.ap[:, d, :], op=ALU.mult),
                     reads=[dt_sb.b, negA.b], writes=[dA.b])
                P.op("pe", lambda e: e.matmul(pcs_b.ap[:, 0:16], lhsT=tri.ap, rhs=dA.ap, start=True, stop=True),
                     reads=[tri.b, dA.b], writes=[pcs_b.b])
                P.op("pe", lambda e: e.matmul(pcs_b.ap[:, 16:32], lhsT=triF.ap[:, 127:128].broadcast_to([128, 128]),
                                              rhs=dA.ap, start=True, stop=True), reads=[triF.b, dA.b], writes=[pcs_b.b])
                P.op("dve", lambda e: e.tensor_scalar(out=cs.ap, in0=pcs_b.ap[:, 0:16], scalar1=-1.0, scalar2=None, op0=ALU.mult),
                     reads=[pcs_b.b], writes=[cs.b])
                P.op("dve", lambda e: e.tensor_tensor(out=wv.ap, in0=pcs_b.ap[:, 16:32], in1=cs.ap, op=ALU.add),
                     reads=[pcs_b.b, cs.b], writes=[wv.b])
                P.op("dve", lambda e: e.tensor_copy(out=cshi.ap, in_=cs.ap), reads=[cs.b], writes=[cshi.b])
                P.op("dve", lambda e: e.tensor_tensor(out=cslo.ap, in0=cs.ap, in1=cshi.ap, op=ALU.subtract),
                     reads=[cs.b, cshi.b], writes=[cslo.b])
                P.op("act", lambda e: e.activation(out=cdec.ap, in_=pcs_b.ap[:, 16:32], func=AF.Exp),
                     reads=[pcs_b.b], writes=[cdec.b])
                P.op("act", lambda e: e.activation(out=wv.ap, in_=wv.ap, func=AF.Exp), reads=[wv.b], writes=[wv.b])
                P.op("dve", lambda e: e.tensor_tensor(out=wv.ap, in0=wv.ap, in1=dtv, op=ALU.mult),
                     reads=[wv.b, dt_sb.b], writes=[wv.b])
                yield
                xs3 = xs_.ap.rearrange("p (h q) -> p h q", q=HP)
                P.op("pool", lambda e: e.tensor_tensor(
                    out=xw.ap.rearrange("p (h q) -> p h q", q=HP), in0=xs3,
                    in1=wv.ap.unsqueeze(2).broadcast_to([128, HEADS, HP]), op=ALU.mult),
                    reads=[xs_.b, wv.b], writes=[xw.b])
                if not want_y:
                    return
                P.op("dve", lambda e: e.tensor_tensor(
                    out=xdt.ap.rearrange("p (h q) -> p h q", q=HP), in0=xs3,
                    in1=dtv.unsqueeze(2).broadcast_to([128, HEADS, HP]), op=ALU.mult),
                    reads=[xs_.b, dt_sb.b], writes=[xdt.b])
                for g in range(4):
                    P.op("pe", lambda e, g=g: e.matmul(
                        pcb_b.ap[:, g * 128:(g + 1) * 128], lhsT=BT_.ap[:, g, :], rhs=CT_.ap[:, g, :], start=True, stop=True),
                        reads=[BT_.b, CT_.b], writes=[pcb_b.b])
                P.op("dve", lambda e: e.tensor_copy(out=cbm.ap.rearrange("p g l -> p (g l)"), in_=pcb_b.ap),
                     reads=[pcb_b.b], writes=[cbm.b])
                qinfo = []

                def part2(q, pb, E2):
                    P.op("pe", lambda e: e.matmul(pb.ap, lhsT=identb.ap, rhs=msk.ap.rearrange("p g l -> p (g l)"),
                                                  start=False, stop=False, skip_group_check=True),
                         reads=[identb.b, msk.b], writes=[pb.b])
                    P.op("pe", lambda e: e.matmul(pb.ap, lhsT=identb.ap,
                                                  rhs=cshi.ap[:, q * 4:(q + 1) * 4].unsqueeze(2).broadcast_to([128, 4, 128]),
                                                  start=False, stop=False, skip_group_check=True),
                         reads=[identb.b, cshi.b], writes=[pb.b])
                    P.op("pe", lambda e: e.matmul(pb.ap, lhsT=identb.ap,
                                                  rhs=cslo.ap[:, q * 4:(q + 1) * 4].unsqueeze(2).broadcast_to([128, 4, 128]),
                                                  start=False, stop=True, skip_group_check=True),
                         reads=[identb.b, cslo.b], writes=[pb.b])
                    P.op("act", lambda e: e.activation(out=E2.ap.rearrange("p r l -> p (r l)"), in_=pb.ap, func=AF.Exp),
                         reads=[pb.b], writes=[E2.b])
                    P.op("dve", lambda e: e.tensor_tensor(
                        out=Md.ap[:, q * 4:(q + 1) * 4, :], in0=E2.ap,
                        in1=cbm.ap[:, q:q + 1, :].broadcast_to([128, 4, 128]), op=ALU.mult),
                        reads=[E2.b, cbm.b], writes=[Md.b])

                for q in range(4):
                    state["rq"] += 1
                    pb = pR[state["rq"] % 3]
                    E1, E2 = E1_t[state["rq"] % 3], E2_t[state["rq"] % 3]
                    for r in range(4):
                        h = q * 4 + r
                        P.op("pe", lambda e, pb=pb, h=h, r=r: e.matmul(
                            pb.ap[:, r * 128:(r + 1) * 128], lhsT=dA.ap[:, h:h + 1].broadcast_to([128, 128]), rhs=tri.ap,
                            start=(r == 0), stop=False, skip_group_check=True), reads=[dA.b, tri.b], writes=[pb.b])
                    P.op("act", lambda e, pb=pb, E1=E1: e.activation(out=E1.ap.rearrange("p r l -> p (r l)"), in_=pb.ap,
                                                                    func=AF.Exp), reads=[pb.b], writes=[E1.b])
                    P.op("pool", lambda e, E1=E1, q=q: e.tensor_tensor(
                        out=Mo.ap[:, q * 4:(q + 1) * 4, :], in0=E1.ap,
                        in1=CT_.ap[:, q:q + 1, :].broadcast_to([128, 4, 128]), op=ALU.mult),
                        reads=[E1.b, CT_.b], writes=[Mo.b])
                    if qinfo:
                        part2(*qinfo.pop())
                    qinfo.append((q, pb, E2))
                    yield
                part2(*qinfo.pop())

            def stage_B(it, ch, d=d):
                tok0 = ch * 128
                b2 = it % NS
                want_y = wanty(ch)
                xs_, bt_ = xs_l[b2], bt_l[b2]
                cdec, xdt, xw, Mo, Md = cdec_t[b2], xdt_t[b2], xw_t[b2], Mo_t[b2], Md_t[b2]
                htm = htmp
                hb_prev, hb_new = hbf_t[(it + 1) % 2], hbf_t[it % 2]
                for g in range(4):
                    pb = pst[g // 2]
                    P.op("pe", lambda e, pb=pb, g=g: e.matmul(
                        pb.ap[:, (g % 2) * 256:(g % 2 + 1) * 256], lhsT=bt_.ap[:, g * 128:(g + 1) * 128],
                        rhs=xw.ap[:, g * 256:(g + 1) * 256], start=True, stop=True), reads=[bt_.b, xw.b], writes=[pb.b])
                P.op("pool", lambda e: e.tensor_tensor(
                    out=htm.ap.rearrange("p (h q) -> p h q", q=HP), in0=hst.ap.rearrange("p (h q) -> p h q", q=HP),
                    in1=cdec.ap.unsqueeze(2).broadcast_to([128, HEADS, HP]), op=ALU.mult),
                    reads=[hst.b, cdec.b], writes=[htm.b])
                if want_y:
                    for h in range(HEADS):
                        pyb = py[h // 8]
                        osl = slice((h % 8) * HP, (h % 8 + 1) * HP)
                        P.op("pe", lambda e, pyb=pyb, h=h, osl=osl: e.matmul(
                            pyb.ap[:, osl], lhsT=Md.ap[:, h, :], rhs=xdt.ap[:, h * HP:(h + 1) * HP],
                            start=(h % 8 == 0), stop=False, skip_group_check=True), reads=[Md.b, xdt.b], writes=[pyb.b])
                        P.op("pe", lambda e, pyb=pyb, h=h, osl=osl: e.matmul(
                            pyb.ap[:, osl], lhsT=dsk.ap[:, d * HEADS + h, :], rhs=xs_.ap[:, h * HP:(h + 1) * HP],
                            start=False, stop=False, skip_group_check=True), reads=[dsk.b, xs_.b], writes=[pyb.b])
                    for h in range(HEADS):
                        pyb = py[h // 8]
                        osl = slice((h % 8) * HP, (h % 8 + 1) * HP)
                        P.op("pe", lambda e, pyb=pyb, h=h, osl=osl: e.matmul(
                            pyb.ap[:, osl], lhsT=Mo.ap[:, h, :], rhs=hb_prev.ap[:, h * HP:(h + 1) * HP],
                            start=False, stop=True, skip_group_check=True), reads=[Mo.b, hb_prev.b], writes=[pyb.b])
                yield
                for half in range(2):
                    sl = slice(half * 512, (half + 1) * 512)
                    P.op("dve", lambda e, half=half, sl=sl: e.tensor_tensor(
                        out=hst.ap[:, sl], in0=htm.ap[:, sl], in1=pst[half].ap, op=ALU.add),
                        reads=[htm.b, pst[half].b], writes=[hst.b])
                yield
                P.op("act", lambda e: e.activation(out=hb_new.ap, in_=hst.ap, func=AF.Identity), reads=[hst.b],
                     writes=[hb_new.b])
                yield
                if not want_y:
                    return
                c2 = it % 2
                yf = yfs[c2]
                if d == 0:
                    P.op("act", lambda e: e.activation(out=yf.ap[:, 0:512], in_=py[0].ap, func=AF.Identity),
                         reads=[py[0].b], writes=[yf.b])
                    P.op("dve", lambda e: e.tensor_copy(out=yf.ap[:, 512:1024], in_=py[1].ap), reads=[py[1].b], writes=[yf.b])
                    P.dma("act", lambda e: e.dma_start(out=yf_d[tok0:tok0 + 128, :], in_=yf.ap), yf.b, reads=[yf.b],
                          writes=[yf_bufs[ch]])
                    return
                seg = seg_of_chunk(ch)
                sz, cv, xr, gbuf = szl[c2], cvl[c2], xres[c2], gbuf_t[c2]
                P.dma("sp", lambda e: e.dma_start(out=yf.ap, in_=yf_d[tok0:tok0 + 128, :]), yf.b, writes=[yf.b],
                      reads=[yf_bufs[ch]])
                P.dma("sp", lambda e: e.dma_start(out=sz.ap, in_=siluz_d[tok0:tok0 + 128, :]), sz.b, writes=[sz.b])
                P.dma("sp", lambda e: e.dma_start(out=cv.ap, in_=cvT3[:, :, tok0:tok0 + 128]), cv.b, writes=[cv.b])
                P.dma("sp", lambda e: e.dma_start(out=xr.ap, in_=stream_rows(tok0, 128)), xr.b, writes=[xr.b])
                for half in range(2):
                    sl = slice(half * 512, (half + 1) * 512)
                    P.op("dve", lambda e, sl=sl, half=half: e.tensor_tensor(
                        out=gbuf.ap[:, sl], in0=py[half].ap, in1=yf.ap[:, sl], op=ALU.add),
                        reads=[py[half].b, yf.b], writes=[gbuf.b])
                if dbg:
                    P.dma("sp", lambda e: e.dma_start(out=dbg_y[tok0:tok0 + 128, :], in_=gbuf.ap), sz.b, reads=[gbuf.b])
                P.op("pool", lambda e: e.tensor_tensor(out=gbuf.ap, in0=gbuf.ap, in1=sz.ap, op=ALU.mult),
                     reads=[gbuf.b, sz.b], writes=[gbuf.b])

            def stage_C2(it, ch):
                tok0 = ch * 128
                b2 = it % 2
                seg = seg_of_chunk(ch)
                if seg != state["seg"]:
                    load_bc(seg, 0)
                    state["seg"] = seg
                cv, xr, gbuf, xo_ = cvl[b2], xres[b2], gbuf_t[b2], xo[b2]
                residual_pre(xr)
                for g in range(4):
                    P.op("act", lambda e, g=g: e.activation(out=gsq.ap[:, g * 256:(g + 1) * 256], in_=gbuf.ap[:, g * 256:(g + 1) * 256],
                                                            func=AF.Square, accum_out=ssq.ap[:, g:g + 1]),
                         reads=[gbuf.b], writes=[gsq.b, ssq.b])
                rstd_from(ssq.ap, ssq, epsrms, [ssq.b], scale=1.0 / 256.0)
                yield
                P.op("dve", lambda e: e.tensor_tensor(
                    out=gnb.ap.rearrange("p (g q) -> p g q", q=256), in0=gbuf.ap.rearrange("p (g q) -> p g q", q=256),
                    in1=ssq.ap.unsqueeze(2).broadcast_to([128, 4, 256]), op=ALU.mult), reads=[gbuf.b, ssq.b], writes=[gnb.b])
                yield
                po = [py[0], py[1]]
                for k in range(8):
                    P.op("pe", lambda e, k=k: e.transpose(out=psb(pti)[:, k * 128:(k + 1) * 128],
                                                          in_=gnb.ap[:, k * 128:(k + 1) * 128], identity=identb.ap),
                         reads=[gnb.b, identb.b], writes=[pt_b.b])
                yield
                for k in range(8):
                    P.op("act", lambda e, k=k: e.activation(out=gT.ap[:, k, :], in_=psb(pti)[:, k * 128:(k + 1) * 128],
                                                            func=AF.Identity, scale=pf.ap[:, onw + k:onw + k + 1]),
                         reads=[pt_b.b, pf.b], writes=[gT.b])
                yield
                for k in range(12):
                    for half in range(2):
                        lhs = cv.ap[:, k, :] if k < 4 else gT.ap[:, k - 4, :]
                        lb = cv.b if k < 4 else gT.b
                        P.op("pe", lambda e, half=half, k=k, lhs=lhs: e.matmul(
                            po[half].ap, lhsT=lhs, rhs=w_out.ap[:, k, half * 512:(half + 1) * 512],
                            start=(k == 0), stop=(k == 11), skip_group_check=True), reads=[lb, w_out.b], writes=[po[half].b])
                yield
                residual_ln(po, xr, xo_, pre=False)
                P.dma("act", lambda e: e.dma_start(out=sA[tok0:tok0 + 128, :], in_=xo_.ap), xo_.b, reads=[xo_.b])
                if dbg:
                    P.dma("sp", lambda e: e.dma_start(out=dbg_r[tok0:tok0 + 128, :], in_=rr.ap), xo_.b, reads=[rr.b])

            n = len(order)

            def run(g):
                for _ in g:
                    pass

            def step(g):
                if g is not None:
                    next(g, None)

            for i0 in range(min(3, n)):
                stage_A_load(i0, order[i0])
            run(stage_A(0, order[0]))
            if n > 1:
                run(stage_A(1, order[1]))
            for it in range(n):
                gens = [stage_B(it, order[it])]
                if it + 2 < n:
                    gens.append(stage_A(it + 2, order[it + 2]))
                if d == 1 and it >= 1 and wanty(order[it - 1]):
                    gens.append(stage_C2(it - 1, order[it - 1]))
                while gens:
                    for g in list(gens):
                        try:
                            next(g)
                        except StopIteration:
                            gens.remove(g)
                if it + 3 < n:
                    stage_A_load(it + 3, order[it + 3])
            if d == 1 and wanty(order[n - 1]):
                run(stage_C2(n - 1, order[n - 1]))

        P.barrier()
        P.release([t.b for t in xs_l + bt_l + BT_l + CT_l + yfs + szl + cvl + xres + xo] + [w_out.b, agb.b, abb.b, g1b.b])
        aoff[0] = layer_mark
        w1 = alloc([8, D_FF], BF16, "w1")
        w13 = W["w1"][L].rearrange("(k p) n -> p k n", p=128)
        w1_pc = [P.buf("w1_pc%d" % i, sw=True) for i in range(8)]
        for wpc in range(8):
            P.dma("pool", lambda e, wpc=wpc: e.dma_start(out=w1.ap[:, :, wpc * 512:(wpc + 1) * 512],
                                                         in_=w13[:, :, wpc * 512:(wpc + 1) * 512]), w1_pc[wpc], writes=[w1_pc[wpc]])
        w2 = alloc([32, D], BF16, "w2")
        w23 = W["w2"][L].rearrange("(k p) n -> p k n", p=128)
        w2_pc = [P.buf("w2_pc%d" % i, sw=True) for i in range(8)]
        for wpc in range(8):
            for hf in range(2):
                P.dma("pool", lambda e, wpc=wpc, hf=hf: e.dma_start(out=w2.ap[:, wpc * 4:(wpc + 1) * 4, hf * 512:(hf + 1) * 512],
                                                                   in_=w23[:, wpc * 4:(wpc + 1) * 4, hf * 512:(hf + 1) * 512]),
                      w2_pc[wpc], writes=[w2_pc[wpc]])
        agb = alloc([D], F32, "agb3", dma=True)
        abb = alloc([D], F32, "abb3", dma=True)
        g1b = alloc([D], F32, "g1b3", dma=True)
        xin3 = [alloc([2, D], F32, "xin3_%d" % i, dma=True) for i in range(2)]
        h2T = [alloc([8, TB], BF16, "h2T0")] * 2
        hid = alloc([32, TB], BF16, "hid")
        rtmp = [alloc([TB], F32, "rtmp%d" % i) for i in range(2)]
        rr = alloc([D], F32, "rr3")
        xo3 = [alloc([D], F32, "xo3_0", dma=True)] * 2
        st12 = alloc([12], F32, "st12_3")
        mv = alloc([2], F32, "mv3")
        rs = alloc([1], F32, "rs3")
        fin = None
        if last:
            fg = alloc([D], F32, "fg", dma=True)
            fb = alloc([D], F32, "fb", dma=True)
            o1_, o2_ = PROW_OFF["ln2g"][0], PROW_OFF["ln2b"][0]
            P.dma("sp", lambda e: e.dma_start(out=fg.ap, in_=prow_in[L][:, o1_:o1_ + D].partition_broadcast(128)), fg.b,
                  writes=[fg.b])
            P.dma("sp", lambda e: e.dma_start(out=fb.ap, in_=prow_in[L][:, o2_:o2_ + D].partition_broadcast(128)), fb.b,
                  writes=[fb.b])
            fin = (fg, fb)
        cur_seg = -1
        blk0 = (CTX // TB) if last else 0
        for bi in range(blk0, nblk):
            tok0 = bi * TB
            seg = 0 if tok0 < CTX else 1
            if seg != cur_seg:
                load_bc(seg, 1)
                cur_seg = seg
            xi, h = xin3[bi % 2], h2T[bi % 2]
            for nb in ([blk0, blk0 + 1] if bi == blk0 else [bi + 1]):
                if nb < nblk:
                    xn = xin3[nb % 2]
                    P.dma("sp", lambda e, xn=xn, nb=nb: e.dma_start(
                        out=xn.ap, in_=sA[nb * TB:(nb + 1) * TB, :].rearrange("(j p) d -> p j d", p=128)), xn.b, writes=[xn.b])
            for k in range(8):
                pb = nextps()
                for j in range(2):
                    P.op("pe", lambda e, pb=pb, xi=xi, j=j, k=k: e.transpose(
                        out=pb.ap[:, j * 128:(j + 1) * 128], in_=xi.ap[:, j, k * 128:(k + 1) * 128], identity=ident.ap),
                        reads=[xi.b, ident.b], writes=[pb.b])
                P.op("act", lambda e, pb=pb, h=h, k=k, seg=seg: e.activation(
                    out=h.ap[:, k, :], in_=pb.ap[:, 0:TB], func=AF.Identity,
                    scale=AB.ap[:, 1, k, seg:seg + 1], bias=AB.ap[:, 1, k, 2 + seg:3 + seg]),
                    reads=[pb.b, AB.b], writes=[h.b])
            for f in range(32):
                pb = nextps()
                for k in range(8):
                    P.op("pe", lambda e, pb=pb, k=k, f=f, h=h: e.matmul(
                        pb.ap[:, 0:TB], lhsT=w1.ap[:, k, f * 128:(f + 1) * 128], rhs=h.ap[:, k, :],
                        start=(k == 0), stop=(k == 7)), reads=[w1_pc[f // 4], h.b], writes=[pb.b])
                rt = rtmp[f % 2]
                if f % 2 == 0:
                    P.op("act", lambda e, pb=pb, rt=rt: e.activation(out=rt.ap, in_=pb.ap[:, 0:TB], func=AF.Relu),
                         reads=[pb.b], writes=[rt.b])
                else:
                    P.op("dve", lambda e, pb=pb, rt=rt: e.tensor_scalar(out=rt.ap, in0=pb.ap[:, 0:TB], scalar1=0.0, scalar2=None,
                                                                        op0=ALU.max), reads=[pb.b], writes=[rt.b])
                P.op("pool", lambda e, f=f, rt=rt: e.tensor_tensor(out=hid.ap[:, f, :], in0=rt.ap, in1=rt.ap, op=ALU.mult),
                     reads=[rt.b], writes=[hid.b])
            for j in range(2):
                po = [nextps(), nextps()]
                for half in range(2):
                    for f in range(32):
                        P.op("pe", lambda e, half=half, f=f, j=j, po=po: e.matmul(
                            po[half].ap, lhsT=hid.ap[:, f, j * 128:(j + 1) * 128], rhs=w2.ap[:, f, half * 512:(half + 1) * 512],
                            start=(f == 0), stop=(f == 31)), reads=[hid.b, w2_pc[f // 4]], writes=[po[half].b])
                xo_ = xo3[j]
                xr = Tl(xi.ap[:, j, :], xi.b)
                residual_ln(po, xr, xo_, final=fin)
                t0_ = tok0 + j * 128
                if last:
                    P.dma("sp", lambda e, xo_=xo_, t0_=t0_: e.dma_start(out=y_out[t0_ - CTX:t0_ - CTX + 128, :], in_=xo_.ap),
                          xo_.b, reads=[xo_.b])
                else:
                    P.dma("sp", lambda e, xo_=xo_, t0_=t0_: e.dma_start(out=sB[t0_:t0_ + 128, :], in_=xo_.ap), xo_.b,
                          reads=[xo_.b])
        P.barrier()
        rel = w1_pc + w2_pc + [agb.b, abb.b, g1b.b, pf.b, hrow.b] + [t.b for t in xin3 + xo3[:1]]
        if last:
            rel += [fin[0].b, fin[1].b]
        P.release(rel)


_CACHE = {}


def kernel(**inp):
    inp = {k: np.asarray(v) for k, v in inp.items()}
    B, SEQ, _ = inp["x"].shape
    CTX = inp["ctx"].shape[1]
    depth = inp["w_in"].shape[0]
    key = (SEQ, CTX, depth)
    if key not in _CACHE:
        _CACHE[key] = build(SEQ, CTX, depth)
    nc, _P = _CACHE[key]
    packs = [pack_params(inp, L) for L in range(depth)]
    pfm = np.stack([p[0] for p in packs]).astype(np.float32)
    prow = np.stack([p[1] for p in packs]).astype(np.float32)
    shared = {k: np.ascontiguousarray(inp[k], dtype=np.float32) for k in ("w_mod", "w_in", "w_out", "w1", "w2")}
    in_maps = []
    for b in range(B):
        cf = np.stack([_fm(inp["c_ctx"]), _fm(inp["c"][b])], axis=-1).reshape(128, 16).astype(np.float32)
        m = {"x": np.ascontiguousarray(inp["x"][b], dtype=np.float32),
             "ctx": np.ascontiguousarray(inp["ctx"][b], dtype=np.float32),
             "c_fm": np.ascontiguousarray(cf), "pfm": pfm, "prow": prow}
        m.update(shared)
        in_maps.append(m)
    res = run_bass_kernel_spmd(nc, in_maps, core_ids=list(range(B)))
    return np.stack([np.asarray(r["y"], dtype=np.float32) for r in res.results], axis=0)
```
